# Optimizing a Trainium2 kernel written in Bass

```python
import jax, jax.numpy as jnp
from jax import lax
import numpy as np

D_MODEL = 1024
BATCH = 16
SEQ = 2048
DEPTH = 4
DEC_BATCH = 32
DEC_SEQ = 2048
PAST_LEN = 128

HEAD_DIM = 64
A_HEADS = 8
A_KV_HEADS = 2
A_RADIUS = 128
A_BLOCK = 128
B_GROUPS = ((128, 1), (512, 4), (2048, 16))
B_HPG = 4
B_HEADS = B_HPG * len(B_GROUPS)
C_HEADS = 4
C_HEAD_DIM = 128
N_MEM = 256
ROPE_THETA = 500000.0
ROPE_DIMS = HEAD_DIM // 4
FFN_HIDDEN = ((8 * D_MODEL + 3 * 256 - 1) // (3 * 256)) * 256
N_BRANCH = 3
A_Q_W = A_HEADS * HEAD_DIM
A_KV_W = A_KV_HEADS * HEAD_DIM
B_W = B_HEADS * HEAD_DIM
C_Q_W = C_HEADS * C_HEAD_DIM
GATE_W = N_BRANCH * D_MODEL
IN_W = A_Q_W + 2 * A_KV_W + 3 * B_W + C_Q_W + GATE_W
EPS = 1e-6
NEG_INF = -1e30

kernel_name = "hybrid_gated_window_dilated_memory_encoder"


def _rmsnorm(x, g):
    xf = x.astype(jnp.float32)
    y = xf * lax.rsqrt(jnp.mean(xf * xf, axis=-1, keepdims=True) + EPS)
    return (y * g.astype(jnp.float32)).astype(x.dtype)


def _rope_tables(seq):
    inv_freq = ROPE_THETA ** (-jnp.arange(0, ROPE_DIMS, 2, dtype=jnp.float32) / ROPE_DIMS)
    ang = jnp.arange(seq, dtype=jnp.float32)[:, None] * inv_freq[None, :]
    return jnp.cos(ang), jnp.sin(ang)


def _rope(x, cos, sin):
    half = ROPE_DIMS // 2
    xf = x.astype(jnp.float32)
    x1, x2 = xf[..., :half], xf[..., half:ROPE_DIMS]
    c, s = cos[None, :, None, :], sin[None, :, None, :]
    out = jnp.concatenate([x1 * c - x2 * s, x2 * c + x1 * s, xf[..., ROPE_DIMS:]], axis=-1)
    return out.astype(x.dtype)


def _banded_attention(q, k, v, radius, block, sink=None):
    b, L, g, r, dh = q.shape
    nb = -(-L // block)
    lp = nb * block
    pad = lp - L
    q = jnp.pad(q, ((0, 0), (0, pad), (0, 0), (0, 0), (0, 0)))

    def kv_blocks(t):
        t = jnp.pad(t, ((0, 0), (block, block + pad), (0, 0), (0, 0)))
        t = t.reshape(b, nb + 2, block, g, dh)
        return jnp.concatenate([t[:, :-2], t[:, 1:-1], t[:, 2:]], axis=2)

    kb, vb = kv_blocks(k), kv_blocks(v)
    qb = q.reshape(b, nb, block, g, r, dh)
    s = jnp.einsum('bnqgrd,bnkgd->bngrqk', qb, kb).astype(jnp.float32) * (dh ** -0.5)
    n_idx = jnp.arange(nb)[:, None, None]
    q_pos = n_idx * block + jnp.arange(block)[None, :, None]
    k_pos = (n_idx - 1) * block + jnp.arange(3 * block)[None, None, :]
    mask = (jnp.abs(q_pos - k_pos) <= radius) & (k_pos >= 0) & (k_pos < L)
    s = jnp.where(mask[None, :, None, None], s, NEG_INF)
    m = jnp.max(s, axis=-1)
    if sink is not None:
        sk = sink.astype(jnp.float32)[None, None, :, :, None]
        m = jnp.maximum(m, sk)
    p = jnp.exp(s - m[..., None])
    denom = jnp.sum(p, axis=-1)
    if sink is not None:
        denom = denom + jnp.exp(sk - m)
    o = jnp.einsum('bngrqk,bnkgd->bnqgrd', p.astype(v.dtype), vb).astype(jnp.float32)
    o = o / jnp.moveaxis(denom, -1, 2)[..., None]
    o = o.reshape(b, lp, g, r, dh)[:, :L].astype(q.dtype)
    lse = jnp.moveaxis(m + jnp.log(denom), -1, 2).reshape(b, lp, g, r)[:, :L]
    return o, lse


def _windowed_gqa(q, k, v, sink):
    b, S = q.shape[0], q.shape[1]
    rep = A_HEADS // A_KV_HEADS
    qg = q.reshape(b, S, A_KV_HEADS, rep, HEAD_DIM)
    o, _ = _banded_attention(qg, k, v, A_RADIUS, A_BLOCK, sink.reshape(A_KV_HEADS, rep))
    return o.reshape(b, S, A_Q_W)


def _dilated_attention(q, k, v):
    b, S = q.shape[0], q.shape[1]
    outs, lses = [], []
    for gi, (window, dil) in enumerate(B_GROUPS):
        lo, hi = gi * B_HPG, (gi + 1) * B_HPG
        L = S // dil
        radius = window // (2 * dil)

        def gather(t):
            return (t[:, :, lo:hi].reshape(b, L, dil, B_HPG, HEAD_DIM)
                    .transpose(0, 2, 1, 3, 4).reshape(b * dil, L, B_HPG, HEAD_DIM))

        o, lse = _banded_attention(gather(q)[:, :, :, None], gather(k), gather(v), radius, radius)
        o = o.reshape(b, dil, L, B_HPG, HEAD_DIM).transpose(0, 2, 1, 3, 4).reshape(b, S, B_HPG, HEAD_DIM)
        lse = lse.reshape(b, dil, L, B_HPG).transpose(0, 2, 1, 3).reshape(b, S, B_HPG)
        outs.append(o)
        lses.append(lse)
    alpha = jax.nn.softmax(jnp.stack(lses, axis=0), axis=0)
    o = jnp.sum(alpha[..., None] * jnp.stack(outs, axis=0).astype(jnp.float32), axis=0)
    return o.reshape(b, S, B_HPG * HEAD_DIM).astype(q.dtype)


def _memory_attention(q, mem_n, w_mem_kv):
    b, S = q.shape[0], q.shape[1]
    kv = mem_n @ w_mem_kv
    mk, mv = jnp.split(kv, 2, axis=-1)
    mk = mk.reshape(b, N_MEM, C_HEADS, C_HEAD_DIM)
    mv = mv.reshape(b, N_MEM, C_HEADS, C_HEAD_DIM)
    s = jnp.einsum('bshd,bmhd->bhsm', q, mk).astype(jnp.float32) * (C_HEAD_DIM ** -0.5)
    p = jax.nn.softmax(s, axis=-1)
    o = jnp.einsum('bhsm,bmhd->bshd', p.astype(mv.dtype), mv)
    return o.reshape(b, S, C_Q_W)


def _layer(x, mem, cos, sin, g_mix_pre, g_mix_post, g_mem, w_in, sink_a, w_mem_kv,
           w_o_a, w_o_b, w_o_c, w_out, g_ffn_pre, g_ffn_post, w_ffn_in, w_ffn_out):
    b, S, _ = x.shape
    h = _rmsnorm(x, g_mix_pre)
    proj = h @ w_in
    cuts = [A_Q_W, A_KV_W, A_KV_W, B_W, B_W, B_W, C_Q_W]
    idx, off = [], 0
    for c in cuts:
        off += c
        idx.append(off)
    qa, ka, va, qb, kb, vb, qc, gates = jnp.split(proj, idx, axis=-1)
    qa = _rope(qa.reshape(b, S, A_HEADS, HEAD_DIM), cos, sin)
    ka = _rope(ka.reshape(b, S, A_KV_HEADS, HEAD_DIM), cos, sin)
    va = va.reshape(b, S, A_KV_HEADS, HEAD_DIM)
    qb = _rope(qb.reshape(b, S, B_HEADS, HEAD_DIM), cos, sin)
    kb = _rope(kb.reshape(b, S, B_HEADS, HEAD_DIM), cos, sin)
    vb = vb.reshape(b, S, B_HEADS, HEAD_DIM)
    qc = qc.reshape(b, S, C_HEADS, C_HEAD_DIM)

    o_a = _windowed_gqa(qa, ka, va, sink_a) @ w_o_a
    o_b = _dilated_attention(qb, kb, vb) @ w_o_b
    o_c = _memory_attention(qc, _rmsnorm(mem, g_mem), w_mem_kv) @ w_o_c

    gate = jax.nn.sigmoid(gates.reshape(b, S, N_BRANCH, D_MODEL))
    merged = gate[:, :, 0] * o_a + gate[:, :, 1] * o_b + gate[:, :, 2] * o_c
    x = x + _rmsnorm(merged @ w_out, g_mix_post)
    h = _rmsnorm(x, g_ffn_pre)
    gu = h @ w_ffn_in
    g_, u_ = jnp.split(gu, 2, axis=-1)
    y = (jax.nn.silu(g_) * u_) @ w_ffn_out
    return x + _rmsnorm(y, g_ffn_post)


def _trunk(x, mem, weights):
    cos, sin = _rope_tables(x.shape[1])
    for l in range(DEPTH):
        x = _layer(x, mem, cos, sin, *[w[l] for w in weights])
    return x


def setup_inputs(seed: int = 0) -> dict:
    key = jax.random.key(seed)
    ks = jax.random.split(key, 20)
    f32 = jnp.float32

    def dense(k, fan_in, fan_out):
        return jax.random.normal(k, (DEPTH, fan_in, fan_out), f32) * (fan_in ** -0.5)

    def gain(k):
        return 1.0 + 0.05 * jax.random.normal(k, (DEPTH, D_MODEL), f32)

    return {
        "x_prompt": jax.random.normal(ks[0], (BATCH, SEQ, D_MODEL), f32),
        "x_sample": jax.random.normal(ks[1], (DEC_BATCH, DEC_SEQ, D_MODEL), f32),
        "mem_prompt": jax.random.normal(ks[2], (BATCH, N_MEM, D_MODEL), f32),
        "mem_sample": jax.random.normal(ks[3], (DEC_BATCH, N_MEM, D_MODEL), f32),
        "norm_mix_pre": gain(ks[4]),
        "norm_mix_post": gain(ks[5]),
        "norm_mem": gain(ks[6]),
        "w_in": dense(ks[7], D_MODEL, IN_W),
        "sink_a": 0.5 * jax.random.normal(ks[8], (DEPTH, A_HEADS), f32),
        "w_mem_kv": dense(ks[9], D_MODEL, 2 * C_Q_W),
        "w_o_a": dense(ks[10], A_Q_W, D_MODEL),
        "w_o_b": dense(ks[11], B_HPG * HEAD_DIM, D_MODEL),
        "w_o_c": dense(ks[12], C_Q_W, D_MODEL),
        "w_out": dense(ks[13], D_MODEL, D_MODEL),
        "norm_ffn_pre": gain(ks[14]),
        "norm_ffn_post": gain(ks[15]),
        "w_ffn_in": dense(ks[16], D_MODEL, 2 * FFN_HIDDEN),
        "w_ffn_out": dense(ks[17], FFN_HIDDEN, D_MODEL),
    }


def reference(x_prompt, x_sample, mem_prompt, mem_sample, norm_mix_pre, norm_mix_post, norm_mem,
              w_in, sink_a, w_mem_kv, w_o_a, w_o_b, w_o_c, w_out, norm_ffn_pre, norm_ffn_post,
              w_ffn_in, w_ffn_out):
    weights = (norm_mix_pre, norm_mix_post, norm_mem, w_in, sink_a, w_mem_kv, w_o_a, w_o_b,
               w_o_c, w_out, norm_ffn_pre, norm_ffn_post, w_ffn_in, w_ffn_out)
    y_prompt = _trunk(x_prompt, mem_prompt, weights)
    y_sample = _trunk(x_sample, mem_sample, weights)
    return (y_prompt, y_sample)
```

```python
import numpy as np
import ml_dtypes
import concourse.bass as bass
import concourse.mybir as mybir
from concourse.bass_utils import run_bass_kernel_spmd

F32 = mybir.dt.float32
BF16 = mybir.dt.bfloat16
AF = mybir.ActivationFunctionType
ALU = mybir.AluOpType

D = 1024
S = 2048
NMEM = 256
DEPTH = 4
NCORES = 8
INW = 6656
FH = 2816
EPS = 1e-6
NEG = -30000.0
ENGS = ("pe", "act", "dve", "pool", "sp")


class Prog:
    def __init__(self):
        self.q = {e: [] for e in ENGS}
        self.count = {}
        self.seen = {e: {} for e in ENGS}
        self.res = {}
        self.epoch = 0

    def esem(self, eng):
        return "%s_%d" % (eng, self.epoch)

    def _deps(self, eng, reads, writes):
        deps = {}

        def add(s, v):
            if deps.get(s, 0) < v:
                deps[s] = v

        for r in reads:
            st = self.res.get(r)
            if st and st[0]:
                add(*st[0])
        for w in writes:
            st = self.res.get(w)
            if st:
                if st[0]:
                    add(*st[0])
                for s, v in st[1].items():
                    add(s, v)
        waits = []
        for s, v in deps.items():
            if eng == "pe" and s.startswith("pe_"):
                continue
            if self.seen[eng].get(s, 0) >= v:
                continue
            self.seen[eng][s] = v
            waits.append((s, v))
        return waits

    def _commit(self, ev, reads, writes):
        for r in reads:
            st = self.res.setdefault(r, [None, {}])
            if st[1].get(ev[0], 0) < ev[1]:
                st[1][ev[0]] = ev[1]
        for w in writes:
            self.res[w] = [ev, {}]

    def op(self, eng, fn, reads=(), writes=(), signal=True):
        waits = self._deps(eng, reads, writes)
        s = self.esem(eng)
        ev = (s, self.count.get(s, 0) + 1)
        if signal:
            self.count[s] = ev[1]
        self._commit(ev, reads, writes)
        self.q[eng].append((waits, fn, (s, 1) if signal else None))

    def dma(self, sem, fns, reads=(), writes=(), eng="sp"):
        waits = self._deps(eng, reads, writes)
        tgt = self.count.get(sem, 0) + 16 * len(fns)
        self.count[sem] = tgt
        self._commit((sem, tgt), reads, writes)
        for i, fn in enumerate(fns):
            self.q[eng].append((waits if i == 0 else [], fn, (sem, 16)))

    def barrier(self):
        for e in ENGS:
            waits = []
            for s, v in self.count.items():
                if v > 0 and self.seen[e].get(s, 0) < v:
                    if e == "pe" and s.startswith("pe_"):
                        continue
                    self.seen[e][s] = v
                    waits.append((s, v))
            if waits:
                self.q[e].append((waits, None, None))


def _consts():
    bf = ml_dtypes.bfloat16
    ident = np.eye(128, dtype=np.float32)
    perm = np.zeros((128, 128), np.float32)
    for k in range(128):
        d = k % 64
        if d < 8:
            perm[k, k + 8] = 1.0
        elif d < 16:
            perm[k, k - 8] = 1.0
    ones = np.ones((128, 128), np.float32)
    ki = np.arange(128)[:, None]
    qa = np.arange(384)[None, :]
    maskA = np.where((qa >= ki) & (qa <= ki + 256), 0.0, NEG).astype(np.float32)
    qb = np.arange(256)[None, :]
    maskB = np.where((qb >= ki) & (qb <= ki + 128), 0.0, NEG).astype(np.float32)
    cb = np.concatenate([ident, perm, ones, maskA, maskB], axis=1).astype(bf)
    inv_freq = (np.float32(500000.0) ** (-np.arange(0, 16, 2, dtype=np.float32) / np.float32(16))).astype(np.float32)
    ang = (np.arange(S, dtype=np.float32)[:, None] * inv_freq[None, :]).astype(np.float32)
    cos = np.cos(ang).astype(np.float32).T
    sin = np.sin(ang).astype(np.float32).T
    C = np.ones((128, S), np.float32)
    S2 = np.zeros((128, S), np.float32)
    for k in range(128):
        d = k % 64
        if d < 8:
            C[k] = cos[d]
            S2[k] = sin[d]
        elif d < 16:
            C[k] = cos[d - 8]
            S2[k] = -sin[d - 8]
    rope = np.stack([C, S2], axis=1)
    return cb, np.ascontiguousarray(rope)


def _win_pieces():
    P = {}
    P["KV0"] = [(512, 128), (1536, 384)]
    P["KV1"] = [(1920, 384)]
    P["V0"] = [(640, 128), (2304, 256)]
    P["V1"] = [(2560, 512)]
    segs = []
    for j in range(4):
        segs.append((64 * j, 64))
        segs.append((64 * (4 + j), 64))
    P["QA"] = segs
    P["QB0"] = [(768, 512)]
    P["QB1"] = [(1280, 256)]
    P["QC"] = [(3072, 512)]
    for i, b in enumerate("ABC"):
        P["G%s0" % b] = [(3584 + 1024 * i, 512)]
        P["G%s1" % b] = [(3584 + 1024 * i + 512, 512)]
    return P


WIN_P = _win_pieces()
WIN_ORDER = list(WIN_P.keys())


def build(NS, depth):
    nc = bass.Bass("TRN2", target_bir_lowering=False, dynamic_dma_scratch_size=256)
    p = Prog()

    def din(name, shape, dt=F32):
        return nc.dram_tensor(name, list(shape), dt, kind="ExternalInput").ap()

    x_d = din("x", [NS, S, D])
    mem_d = din("mem", [NS, NMEM, D])
    g_d = {n: din(n, [DEPTH, D]) for n in ("norm_mix_pre", "norm_mix_post", "norm_mem", "norm_ffn_pre", "norm_ffn_post")}
    sink_d = din("sink_a", [DEPTH, 8])
    w_in_d = din("w_in", [DEPTH, D, INW])
    w_mkv_d = din("w_mem_kv", [DEPTH, D, 1024])
    w_oa_d = din("w_o_a", [DEPTH, 512, D])
    w_ob_d = din("w_o_b", [DEPTH, 256, D])
    w_oc_d = din("w_o_c", [DEPTH, 512, D])
    w_out_d = din("w_out", [DEPTH, D, D])
    w_fi_d = din("w_ffn_in", [DEPTH, D, 2 * FH])
    w_fo_d = din("w_ffn_out", [DEPTH, FH, D])
    cb_d = din("c_bf", [128, 1024], BF16)
    rope_d = din("c_rope", [128, 2, S])
    y_d = nc.dram_tensor("y", [NS, S, D], F32, kind="ExternalOutput").ap()

    NPIECE = len(WIN_ORDER) + 2 + 2 + 1 + 1 + 2 + 11 + 6
    scr = nc.dram_tensor("wscr", [depth, NPIECE, 128, 4096], BF16, kind="Internal").ap()
    pidx = {}
    for n in WIN_ORDER + ["MK", "MV", "OA0", "OA1", "OB", "OC", "WO0", "WO1"] + ["FI%d" % i for i in range(11)] + ["FO%d" % i for i in range(6)]:
        pidx[n] = len(pidx)
    assert len(pidx) == NPIECE

    off = [512]

    def sb(name, shape, dt):
        nbytes = int(np.prod(shape[1:])) * (2 if dt == BF16 else 4)
        t = nc.alloc_sbuf_tensor_at(name, list(shape), dt, offset=off[0])
        off[0] += (nbytes + 31) // 32 * 32
        return t

    x_sb = sb("x_sb", [128, 16, D], F32)
    cb = sb("cb", [128, 1024], BF16)
    gA = sb("gA", [128, D], F32)
    gB = sb("gB", [128, D], F32)
    stats = sb("stats", [128, 64], F32)
    hT = sb("hT", [128, 8, 512], BF16)
    ring = [sb("ring%d" % i, [128, 4096], BF16) for i in range(3)]
    htok = sb("htok", [128, D], BF16)
    ytmp = sb("ytmp", [128, D], F32)
    junk = sb("junk", [128, D], BF16)
    phase0 = off[0]
    kT = sb("kT", [128, 7, S], BF16)
    v0 = sb("v0", [128, 16, 384], BF16)
    vb1 = sb("vb1", [128, 16, 256], BF16)
    vb2 = sb("vb2", [128, 16, 256], BF16)
    mkT = sb("mkT", [128, 4, 256], BF16)
    mv = sb("mv", [128, 2, 512], BF16)
    rope = sb("rope", [128, 2, 512], F32)
    qreg = off[0]
    qT = sb("qT", [128, 8, 512], BF16)
    OT = sb("OT", [128, 8, 512], BF16)
    merged = sb("merged", [128, 8, 512], F32)
    sg = [sb("sg%d" % i, [128, 512], F32) for i in range(2)]
    pT = [sb("pT%d" % i, [128, 512], BF16) for i in range(2)]
    rqc = sb("rqc", [128, 512], F32)
    rqs = sb("rqs", [128, 512], BF16)
    rec = sb("rec", [128, 512], F32)
    mix_end = off[0]
    hTf = nc.alloc_sbuf_tensor_at("hTf", [128, 8, S], BF16, offset=qreg)
    mbf = nc.alloc_sbuf_tensor_at("mbf", [128, 8, 512], BF16, offset=qreg)
    memT = nc.alloc_sbuf_tensor_at("memT", [128, 8, 256], BF16, offset=qreg + 32768)
    off[0] = phase0
    wfo = sb("wfo", [128, 22, D], BF16)
    actT = sb("actT", [128, 22, 512], BF16)
    sgf = [sb("sgf%d" % i, [128, 512], F32) for i in range(2)]
    off[0] = phase0
    stg = [sb("stg%d" % i, [128, 4096], F32) for i in range(2)]
    stb = [sb("stb%d" % i, [128, 4096], BF16) for i in range(2)]
    assert mix_end < nc.sbuf_top, (mix_end, nc.sbuf_top)

    psm = nc.alloc_psum_tensor("psm", [128, 3584], F32)
    pst = nc.alloc_psum_tensor("pst", [128, 1024], BF16)

    def bank(b, n=512):
        return psm[:, 512 * b:512 * b + n]

    ident = cb[:, 0:128]
    perm = cb[:, 128:256]
    ones = cb[:, 256:384]
    maskA = cb[:, 384:768]
    maskB = cb[:, 768:1024]
    eps_c = stats[:, 63:64]

    E = {"pe": "tensor", "act": "scalar", "dve": "vector", "pool": "gpsimd", "sp": "sync"}

    def mm(out, lhsT, rhs, start, stop, reads, writes, signal):
        p.op("pe", lambda e: e.matmul(out, lhsT, rhs, start=start, stop=stop), reads, writes, signal)

    def act(out, in_, func, reads, writes, **kw):
        p.op("act", lambda e: e.activation(out, in_, func, **kw), reads, writes)

    def tt(out, in0, in1, op, reads, writes, eng="dve"):
        p.op(eng, lambda e: e.tensor_tensor(out, in0, in1, op), reads, writes)

    rr = [0]

    def load_piece(l, name, parts=128):
        s = rr[0] % 3
        rr[0] += 1
        src = scr[l, pidx[name]]
        dst = ring[s]
        p.dma("ring%d" % s, [lambda e: e.dma_start(out=dst[0:parts, :], in_=src[0:parts, :])],
              reads=["scr"], writes=["ring%d" % s])
        return dst, "ring%d" % s

    mmrot = [0]

    def mmbank(banks):
        b = banks[mmrot[0] % len(banks)]
        mmrot[0] += 1
        return b

    pl = [0]
    cast_engs = ("act", "dve", "pool")

    def convert(l, name, loads, parts=128, width=4096):
        i = pl[0] % 2
        eng = cast_engs[pl[0] % 3]
        pl[0] += 1
        st_, sbt = stg[i], stb[i]
        fns = []
        for dv, src in loads:
            d_ = dv(st_)
            fns.append(lambda e, d_=d_, src=src: e.dma_start(out=d_, in_=src))
        p.dma("plin%d" % i, fns, reads=[], writes=["stg%d" % i])
        if eng == "act":
            p.op("act", lambda e: e.activation(sbt[0:parts, 0:width], st_[0:parts, 0:width], AF.Copy),
                 ["stg%d" % i], ["stb%d" % i])
        else:
            p.op(eng, lambda e: e.tensor_copy(sbt[0:parts, 0:width], st_[0:parts, 0:width]),
                 ["stg%d" % i], ["stb%d" % i])
        dst = scr[l, pidx[name]]
        p.dma("plout%d" % i, [lambda e: e.dma_start(out=dst[0:parts, 0:width], in_=sbt[0:parts, 0:width])],
              reads=["stb%d" % i], writes=["scr"])

    def prologue():
        for l in range(depth):
            for name in WIN_ORDER:
                segs = WIN_P[name]
                tot = sum(n for _, n in segs)
                loads = []
                o = 0
                for (c0, n) in segs:
                    src = w_in_d[l, :, c0:c0 + n].rearrange("(kc p) c -> p kc c", p=128)
                    loads.append((lambda t, o=o, n=n, tot=tot: t[:, 0:8 * tot].rearrange("p (kc c) -> p kc c", kc=8)[:, :, o:o + n], src))
                    o += n
                convert(l, name, loads, width=8 * tot)
            for name, c0 in (("MK", 0), ("MV", 512)):
                src = w_mkv_d[l, :, c0:c0 + 512].rearrange("(kc p) c -> p kc c", p=128)
                convert(l, name, [(lambda t: t[:, :].rearrange("p (kc c) -> p kc c", kc=8), src)])
            for hf in range(2):
                src = w_oa_d[l, :, 512 * hf:512 * hf + 512].rearrange("(h d) c -> d h c", d=64)
                convert(l, "OA%d" % hf, [(lambda t: t[0:64, :].rearrange("p (h c) -> p h c", h=8), src)], parts=64)
            src = w_ob_d[l].rearrange("(h d) c -> d h c", d=64)
            convert(l, "OB", [(lambda t: t[0:64, :].rearrange("p (h c) -> p h c", h=4), src)], parts=64)
            src = w_oc_d[l].rearrange("(h d) c -> d h c", d=128)
            convert(l, "OC", [(lambda t: t[:, :].rearrange("p (h c) -> p h c", h=4), src)])
            for hf in range(2):
                src = w_out_d[l, :, 512 * hf:512 * hf + 512].rearrange("(kc p) c -> p kc c", p=128)
                convert(l, "WO%d" % hf, [(lambda t: t[:, :].rearrange("p (kc c) -> p kc c", kc=8), src)])
            for i in range(11):
                loads = []
                for j in range(2):
                    c = 2 * i + j
                    for k, base in enumerate((0, FH)):
                        src = w_fi_d[l, :, base + 128 * c:base + 128 * c + 128].rearrange("(kc p) c -> p kc c", p=128)
                        o = 256 * j + 128 * k
                        loads.append((lambda t, o=o: t[:, :].rearrange("p (kc c) -> p kc c", kc=8)[:, :, o:o + 128], src))
                convert(l, "FI%d" % i, loads)
            for i in range(6):
                nch = 4 if i < 5 else 2
                src = w_fo_d[l, 512 * i:512 * i + 128 * nch, :].rearrange("(c p) n -> p c n", p=128)
                convert(l, "FO%d" % i, [(lambda t, nch=nch: t[:, 0:1024 * nch].rearrange("p (c n) -> p c n", c=nch), src)],
                        width=1024 * nch)

    def load_g(buf, bname, which, l):
        src = g_d[which][l].partition_broadcast(128)
        p.dma("gld", [lambda e: e.dma_start(out=buf[:, :], in_=src)], reads=[], writes=[bname])

    def rstd_from(ss_ap, n, reads_w):
        act(ss_ap, ss_ap, AF.Sqrt, [reads_w], [reads_w], bias=eps_c, scale=1.0 / D)
        p.op("dve", lambda e: e.reciprocal(ss_ap, ss_ap), [reads_w], [reads_w])

    def norm_tok(src_fn, nst, gbuf, gname, dst_fn, dst_res, src_res_fn):
        for st in range(nst):
            xin = src_fn(st)
            act(junk[:, :], xin, AF.Square, [src_res_fn(st)], ["junk", "ss%d" % st], accum_out=stats[:, st:st + 1])
        for st in range(nst):
            rstd_from(stats[:, st:st + 1], 1, "ss%d" % st)
        for st in range(nst):
            xin = src_fn(st)
            p.op("dve", lambda e, xin=xin, st=st: e.scalar_tensor_tensor(htok[:, :], xin, stats[:, st:st + 1], gbuf[:, :], ALU.mult, ALU.mult),
                 [src_res_fn(st), "ss%d" % st, gname], ["htok"])
            for c in range(8):
                p.op("pe", lambda e, c=c: e.transpose(pst[:, 128 * c:128 * c + 128], htok[:, 128 * c:128 * c + 128], ident),
                     ["htok", "cb"], ["pst"], signal=(c == 7))
            d_ = dst_fn(st)
            act(d_, pst[:, :].rearrange("p (c t) -> p c t", c=8), AF.Copy, ["pst"], [dst_res(st)])

    def proj_fm(piece, pres, ncols_tot, c0, rhs_fn, rhs_res, n, banks):
        b = mmbank(banks)
        pv = piece[:, 0:8 * ncols_tot].rearrange("p (kc c) -> p kc c", kc=8)
        for kc in range(8):
            mm(bank(b, n), pv[:, kc, c0:c0 + 128], rhs_fn(kc), kc == 0, kc == 7, [pres] + rhs_res, ["ps%d" % b], kc == 7)
        return b

    def rotary_evac(b, dst, dst_res, n=512):
        tt(rqc[:, 0:n], bank(b, n), rope[:, 0, 0:n], ALU.mult, ["ps%d" % b, "rope"], ["rqc"])
        tt(rqs[:, 0:n], bank(b, n), rope[:, 1, 0:n], ALU.mult, ["ps%d" % b, "rope"], ["rqs"])
        b2 = mmbank((0, 1, 2))
        mm(bank(b2, n), perm, rqs[:, 0:n], True, True, ["rqs", "cb"], ["ps%d" % b2], True)
        tt(dst, bank(b2, n), rqc[:, 0:n], ALU.add, ["ps%d" % b2, "rqc"], [dst_res])

    strot = [0]

    def band_tile(kT_ap, qT_ap, n, mask_ap, scale, v_ap, ob, db, o_cols, first, last, reads):
        sbk = (3, 4)[strot[0] % 2]
        pt_ = pT[strot[0] % 2]
        ptn = "pT%d" % (strot[0] % 2)
        strot[0] += 1
        mm(bank(sbk, n), kT_ap, qT_ap, True, mask_ap is None, reads, ["ps%d" % sbk], mask_ap is None)
        if mask_ap is not None:
            mm(bank(sbk, n), ident, mask_ap, False, True, ["cb"], ["ps%d" % sbk], True)
        act(pt_[:, 0:n], bank(sbk, n), AF.Exp, ["ps%d" % sbk], [ptn], scale=scale)
        M = v_ap.shape[-1]
        mm(psm[0:M, 512 * ob:512 * ob + 512][:, o_cols], v_ap, pt_[:, 0:n], first, last, [ptn, "kv"], ["ps%d" % ob], False)
        mm(psm[0:M, 512 * db:512 * db + 512][:, o_cols], ones[:, 0:M], pt_[:, 0:n], first, last, [ptn, "cb"], ["ps%d" % db], last)

    def attn_banded(tt_i, R, dil, L, mask, kT_fn, qT_fn, v_fn, ob, db, first, last_group):
        t0 = 512 * tt_i
        tiles = []
        for c in range(dil):
            mlo, mhi = t0 // dil, (t0 + 512) // dil
            for kt in range(L // 128):
                lo = max(128 * kt - R, 0, mlo)
                hi = min(128 * kt + 128 + R, L, mhi)
                if hi <= lo:
                    continue
                tiles.append((c, kt, lo, hi))
        for i, (c, kt, lo, hi) in enumerate(tiles):
            n = hi - lo
            qi0 = lo - (128 * kt - R)
            loc = lo * dil + c - t0
            if dil == 1:
                cols = slice(loc, loc + n)
            else:
                cols = slice(loc, loc + (n - 1) * dil + 1, dil)
            band_tile(kT_fn(c, kt), qT_fn(cols), n, mask[:, qi0:qi0 + n], 0.125, v_fn(c, kt), ob, db, cols,
                      first and i == 0, last_group and i == len(tiles) - 1, ["kv", "qT"])

    def layer(sq, l):
        p.barrier()
        load_g(gA, "gA", "norm_mix_pre", l)
        load_g(gB, "gB", "norm_mem", l)
        src = sink_d[l].partition_broadcast(128)
        p.dma("gld", [lambda e: e.dma_start(out=stats[:, 48:56], in_=src)], reads=[], writes=["sink"])
        act(stats[:, 48:56], stats[:, 48:56], AF.Exp, ["sink"], ["sink"])
        for t4 in range(4):
            norm_tok(lambda st: x_sb[:, 4 * t4 + st, :], 4, gA, "gA",
                     lambda st: hTf[:, :, 512 * t4 + 128 * st:512 * t4 + 128 * st + 128],
                     lambda st: "hTf", lambda st: "x%d" % (4 * t4 + st))
        for pn, nchunk, cbase in (("KV0", 4, 0), ("KV1", 3, 4)):
            piece, pres = load_piece(l, pn)
            for t4 in range(4):
                p.dma("ropeld", [lambda e, t4=t4: e.dma_start(out=rope[:, :, :], in_=rope_d[:, :, 512 * t4:512 * t4 + 512])],
                      reads=[], writes=["rope"])
                for ch in range(nchunk):
                    b = proj_fm(piece, pres, 128 * nchunk, 128 * ch, lambda kc: hTf[:, kc, 512 * t4:512 * t4 + 512], ["hTf"], 512, (0, 1, 2))
                    rotary_evac(b, kT[:, cbase + ch, 512 * t4:512 * t4 + 512], "kv")
        piece, pres = load_piece(l, "V0")
        pv = piece[:, 0:8 * 384].rearrange("p (kc c) -> p kc c", kc=8)
        for g in range(16):
            b = mmbank((0, 1, 2))
            for kc in range(8):
                mm(bank(b, 384), hTf[:, kc, 128 * g:128 * g + 128], pv[:, kc, :], kc == 0, kc == 7, [pres, "hTf"], ["ps%d" % b], kc == 7)
            act(v0[:, g, :], bank(b, 384), AF.Copy, ["ps%d" % b], ["kv"])
        piece, pres = load_piece(l, "V1")
        pv = piece[:, :].rearrange("p (kc c) -> p kc c", kc=8)
        for c in range(4):
            for kt in range(4):
                b = mmbank((0, 1, 2))
                for kc in range(8):
                    mm(bank(b, 256), hTf[:, kc, 512 * kt + c:512 * kt + 512:4], pv[:, kc, 0:256], kc == 0, kc == 7, [pres, "hTf"], ["ps%d" % b], kc == 7)
                p.op("dve", lambda e, b=b, c=c, kt=kt: e.tensor_copy(vb1[:, 4 * c + kt, :], bank(b, 256)), ["ps%d" % b], ["kv"])
        for c in range(16):
            b = mmbank((0, 1, 2))
            for kc in range(8):
                mm(bank(b, 256), hTf[:, kc, c:S:16], pv[:, kc, 256:512], kc == 0, kc == 7, [pres, "hTf"], ["ps%d" % b], kc == 7)
            p.op("dve", lambda e, b=b, c=c: e.tensor_copy(vb2[:, c, :], bank(b, 256)), ["ps%d" % b], ["kv"])
        for mt in range(2):
            p.dma("memld", [lambda e, mt=mt: e.dma_start(out=ytmp[:, :], in_=mem_d[sq, 128 * mt:128 * mt + 128, :])], reads=[], writes=["ytmp"])
            norm_tok(lambda st: ytmp[:, :], 1, gB, "gB", lambda st, mt=mt: memT[:, :, 128 * mt:128 * mt + 128], lambda st: "memT", lambda st: "ytmp")
        piece, pres = load_piece(l, "MK")
        for h in range(4):
            b = proj_fm(piece, pres, 512, 128 * h, lambda kc: memT[:, kc, :], ["memT"], 256, (0, 1, 2))
            act(mkT[:, h, :], bank(b, 256), AF.Copy, ["ps%d" % b], ["kv"])
        piece, pres = load_piece(l, "MV")
        pv = piece[:, :].rearrange("p (kc c) -> p kc c", kc=8)
        for mt in range(2):
            b = mmbank((0, 1, 2))
            for kc in range(8):
                mm(bank(b), memT[:, kc, 128 * mt:128 * mt + 128], pv[:, kc, :], kc == 0, kc == 7, [pres, "memT"], ["ps%d" % b], kc == 7)
            act(mv[:, mt, :], bank(b), AF.Copy, ["ps%d" % b], ["kv"])
        p.barrier()
        load_g(gB, "gB", "norm_mix_post", l)

        for t4 in range(4):
            norm_tok(lambda st: x_sb[:, 4 * t4 + st, :], 4, gA, "gA",
                     lambda st: hT[:, :, 128 * st:128 * st + 128], lambda st: "hT", lambda st: "x%d" % (4 * t4 + st))
            p.dma("ropeld", [lambda e, t4=t4: e.dma_start(out=rope[:, :, :], in_=rope_d[:, :, 512 * t4:512 * t4 + 512])],
                  reads=[], writes=["rope"])
            hrhs = lambda kc: hT[:, kc, :]

            def outproj(branch, nh, kdim, wfn, first_b, last_b):
                for hf in range(2):
                    gp, gres = load_piece(l, "G%s%d" % (branch, hf))
                    wpc, wres, wv = wfn(hf)
                    for o4 in range(4):
                        oc = 4 * hf + o4
                        bg = proj_fm(gp, gres, 512, 128 * o4, hrhs, ["hT"], 512, (0, 1, 2))
                        sgt = sg[oc % 2]
                        sgn = "sg%d" % (oc % 2)
                        act(sgt[:, :], bank(bg), AF.Sigmoid, ["ps%d" % bg], [sgn])
                        bo = mmbank((0, 1, 2))
                        for h in range(nh):
                            mm(bank(bo), wv(h, o4), OT[0:kdim, h, :], h == 0, h == nh - 1, [wres, "OT"], ["ps%d" % bo], h == nh - 1)
                        if first_b:
                            tt(merged[:, oc, :], bank(bo), sgt[:, :], ALU.mult, ["ps%d" % bo, sgn], ["merged"])
                        else:
                            tt(rec[:, :], bank(bo), sgt[:, :], ALU.mult, ["ps%d" % bo, sgn], ["rec"])
                            if last_b:
                                tt(mbf[:, oc, :], merged[:, oc, :], rec[:, :], ALU.add, ["merged", "rec"], ["qT"])
                            else:
                                tt(merged[:, oc, :], merged[:, oc, :], rec[:, :], ALU.add, ["merged", "rec"], ["merged"])

            piece, pres = load_piece(l, "QA")
            for ch in range(4):
                b = proj_fm(piece, pres, 512, 128 * ch, hrhs, ["hT"], 512, (0, 1, 2))
                rotary_evac(b, qT[:, ch, :], "qT")
            for h in range(8):
                j, pb = h % 4, 64 * (h // 4)
                attn_banded(t4, 128, 1, S, maskA,
                            lambda c, kt, pb=pb: kT[pb:pb + 64, 0, 128 * kt:128 * kt + 128],
                            lambda cols, j=j, pb=pb: qT[pb:pb + 64, j, cols],
                            lambda c, kt, h=h: v0[:, kt, 64 * (h // 4):64 * (h // 4) + 64],
                            5, 6, True, True)
                p.op("dve", lambda e, h=h: e.tensor_scalar(rec[0:64, :], bank(6)[0:64, :], stats[0:64, 48 + h:49 + h], None, ALU.add),
                     ["ps6", "sink"], ["rec"])
                p.op("dve", lambda e: e.reciprocal(rec[0:64, :], rec[0:64, :]), ["rec"], ["rec"])
                tt(OT[0:64, h, :], bank(5)[0:64, :], rec[0:64, :], ALU.mult, ["ps5", "rec"], ["OT"])

            def wfa(hf):
                wpc, wres = load_piece(l, "OA%d" % hf, parts=64)
                v = wpc[0:64, :].rearrange("p (h c) -> p h c", h=8)
                return wpc, wres, (lambda h, o4: v[:, h, 128 * o4:128 * o4 + 128])
            outproj("A", 8, 64, wfa, True, False)

            for pn, nchunk, cbase in (("QB0", 4, 0), ("QB1", 2, 4)):
                piece, pres = load_piece(l, pn)
                for ch in range(nchunk):
                    b = proj_fm(piece, pres, 128 * nchunk, 128 * ch, hrhs, ["hT"], 512, (0, 1, 2))
                    rotary_evac(b, qT[:, cbase + ch, :], "qT")
            for j in range(4):
                for gi, (dil, L) in enumerate(((1, 2048), (4, 512), (16, 128))):
                    hb = 4 * gi + j
                    chq, pb = hb // 2, 64 * (hb % 2)
                    if gi == 0:
                        vfn = lambda c, kt, j=j: v0[:, kt, 128 + 64 * j:128 + 64 * j + 64]
                    elif gi == 1:
                        vfn = lambda c, kt, j=j: vb1[:, 4 * c + kt, 64 * j:64 * j + 64]
                    else:
                        vfn = lambda c, kt, j=j: vb2[:, c, 64 * j:64 * j + 64]

                    def kfn(c, kt, chq=chq, pb=pb, dil=dil):
                        s0 = 128 * kt * dil + c
                        if dil == 1:
                            return kT[pb:pb + 64, 1 + chq, s0:s0 + 128]
                        return kT[pb:pb + 64, 1 + chq, s0:s0 + 127 * dil + 1:dil]
                    attn_banded(t4, 64, dil, L, maskB, kfn,
                                lambda cols, chq=chq, pb=pb: qT[pb:pb + 64, chq, cols],
                                vfn, 5, 6, gi == 0, gi == 2)
                p.op("dve", lambda e: e.reciprocal(rec[0:64, :], bank(6)[0:64, :]), ["ps6"], ["rec"])
                tt(OT[0:64, j, :], bank(5)[0:64, :], rec[0:64, :], ALU.mult, ["ps5", "rec"], ["OT"])

            def wfb(hf):
                if hf == 0:
                    wfb.pc = load_piece(l, "OB", parts=64)
                wpc, wres = wfb.pc
                v = wpc[0:64, :].rearrange("p (h c) -> p h c", h=4)
                return wpc, wres, (lambda h, o4: v[:, h, 512 * hf + 128 * o4:512 * hf + 128 * o4 + 128])
            outproj("B", 4, 64, wfb, False, False)

            piece, pres = load_piece(l, "QC")
            for ch in range(4):
                b = proj_fm(piece, pres, 512, 128 * ch, hrhs, ["hT"], 512, (0, 1, 2))
                act(qT[:, ch, :], bank(b), AF.Copy, ["ps%d" % b], ["qT"])
            for h in range(4):
                for mt in range(2):
                    band_tile(mkT[:, h, 128 * mt:128 * mt + 128], qT[:, h, :], 512, None, float(128 ** -0.5),
                              mv[:, mt, 128 * h:128 * h + 128], 5, 6, slice(0, 512), mt == 0, mt == 1, ["kv", "qT"])
                p.op("dve", lambda e: e.reciprocal(rec[:, :], bank(6)), ["ps6"], ["rec"])
                tt(OT[:, h, :], bank(5), rec[:, :], ALU.mult, ["ps5", "rec"], ["OT"])

            def wfc(hf):
                if hf == 0:
                    wfc.pc = load_piece(l, "OC")
                wpc, wres = wfc.pc
                v = wpc[:, :].rearrange("p (h c) -> p h c", h=4)
                return wpc, wres, (lambda h, o4: v[:, h, 512 * hf + 128 * o4:512 * hf + 128 * o4 + 128])
            outproj("C", 4, 128, wfc, False, True)

            wo = [load_piece(l, "WO%d" % hf) for hf in range(2)]
            for st in range(4):
                g = 4 * t4 + st
                for hf in range(2):
                    wv = wo[hf][0][:, :].rearrange("p (kc c) -> p kc c", kc=8)
                    for kc in range(8):
                        mm(bank(hf), mbf[:, kc, 128 * st:128 * st + 128], wv[:, kc, :], kc == 0, kc == 7, [wo[hf][1], "qT"], ["ps%d" % hf], kc == 7)
                post_norm(psm[:, 0:1024], ["ps0", "ps1"], g)

    def post_norm(yps, yres, g):
        act(junk[:, :], yps, AF.Square, yres, ["junk", "ssp"], accum_out=stats[:, 32:33])
        rstd_from(stats[:, 32:33], 1, "ssp")
        p.op("dve", lambda e: e.scalar_tensor_tensor(ytmp[:, :], yps, stats[:, 32:33], gB[:, :], ALU.mult, ALU.mult),
             yres + ["ssp", "gB"], ["ytmp"])
        tt(x_sb[:, g, :], x_sb[:, g, :], ytmp[:, :], ALU.add, ["x%d" % g, "ytmp"], ["x%d" % g], eng="pool")

    def ffn(sq, l):
        p.barrier()
        load_g(gA, "gA", "norm_ffn_pre", l)
        load_g(gB, "gB", "norm_ffn_post", l)
        for i in range(6):
            nch = 4 if i < 5 else 2
            src = scr[l, pidx["FO%d" % i]]
            p.dma("wfold", [lambda e, i=i, nch=nch, src=src: e.dma_start(
                out=wfo[:, 4 * i:4 * i + nch, :], in_=src[:, 0:1024 * nch].rearrange("p (c n) -> p c n", c=nch))],
                reads=["scr"], writes=["wfo"])
        for t4 in range(4):
            norm_tok(lambda st: x_sb[:, 4 * t4 + st, :], 4, gA, "gA",
                     lambda st: hT[:, :, 128 * st:128 * st + 128], lambda st: "hT", lambda st: "x%d" % (4 * t4 + st))
            hrhs = lambda kc: hT[:, kc, :]
            pr = 0
            for i in range(11):
                piece, pres = load_piece(l, "FI%d" % i)
                for j in range(2):
                    c = 2 * i + j
                    banks = ((0, 1), (2, 3), (4, 5))[pr % 3]
                    pr += 1
                    bg = proj_fm(piece, pres, 512, 256 * j, hrhs, ["hT"], 512, (banks[0],))
                    bu = proj_fm(piece, pres, 512, 256 * j + 128, hrhs, ["hT"], 512, (banks[1],))
                    sf = sgf[c % 2]
                    sfn = "sgf%d" % (c % 2)
                    act(sf[:, :], bank(bg), AF.Silu, ["ps%d" % bg], [sfn])
                    tt(actT[:, c, :], bank(bu), sf[:, :], ALU.mult, ["ps%d" % bu, sfn], ["actT"])
            for st in range(4):
                g = 4 * t4 + st
                pb_ = (0, 2, 4)[st % 3]
                for hf in range(2):
                    for c in range(22):
                        mm(bank(pb_ + hf), actT[:, c, 128 * st:128 * st + 128], wfo[:, c, 512 * hf:512 * hf + 512],
                           c == 0, c == 21, ["actT", "wfo"], ["ps%d" % (pb_ + hf)], c == 21)
                post_norm(psm[:, 512 * pb_:512 * pb_ + 1024], ["ps%d" % pb_, "ps%d" % (pb_ + 1)], g)

    p.dma("cst", [lambda e: e.dma_start(out=cb[:, :], in_=cb_d)], reads=[], writes=["cb"])
    p.op("dve", lambda e: e.memset(stats[:, :], 0.0), [], ["stats0"])
    p.op("dve", lambda e: e.memset(eps_c, EPS), ["stats0"], ["eps"])
    prologue()
    p.barrier()
    for sq in range(NS):
        p.epoch = sq
        p.dma("xld", [lambda e, sq=sq, q4=q4: e.dma_start(out=x_sb[:, 4 * q4:4 * q4 + 4, :],
                                                      in_=x_d[sq, 512 * q4:512 * q4 + 512, :].rearrange("(g p) d -> p g d", p=128)) for q4 in range(4)],
              reads=[], writes=["x%d" % g for g in range(16)])
        for l in range(depth):
            layer(sq, l)
            ffn(sq, l)
        p.dma("xst", [lambda e, sq=sq, q4=q4: e.dma_start(out=y_d[sq, 512 * q4:512 * q4 + 512, :].rearrange("(g p) d -> p g d", p=128),
                                                      in_=x_sb[:, 4 * q4:4 * q4 + 4, :]) for q4 in range(4)],
              reads=["x%d" % g for g in range(16)], writes=[])
    p.barrier()

    sems = {}
    for s in p.count:
        sems[s] = nc.alloc_semaphore(s)

    def replay(eng, e):
        for waits, fn, inc in p.q[eng]:
            for s, v in waits:
                e.wait_ge(sems[s], v)
            if fn is None:
                continue
            ins = fn(e)
            if inc is not None:
                ins.then_inc(sems[inc[0]], inc[1])

    with nc.Block() as block:
        @block.tensor
        def _(e):
            replay("pe", e)

        @block.scalar
        def _(e):
            replay("act", e)

        @block.vector
        def _(e):
            replay("dve", e)

        @block.gpsimd
        def _(e):
            replay("pool", e)

        @block.sync
        def _(e):
            replay("sp", e)
    return nc, p


_CACHE = {}


def _run(x_all, mem_all, weights, depth=DEPTH):
    NS = x_all.shape[0] // NCORES
    key = (NS, depth)
    if key not in _CACHE:
        _CACHE[key] = build(NS, depth)[0]
    nc = _CACHE[key]
    cbc, ropec = _consts()
    in_maps = []
    for c in range(NCORES):
        m = {"x": np.ascontiguousarray(x_all[NS * c:NS * (c + 1)]),
             "mem": np.ascontiguousarray(mem_all[NS * c:NS * (c + 1)]),
             "c_bf": cbc, "c_rope": ropec}
        m.update(weights)
        in_maps.append(m)
    res = run_bass_kernel_spmd(nc, in_maps, core_ids=list(range(NCORES)))
    return np.concatenate([np.asarray(r["y"]) for r in res.results], axis=0)


def kernel(x_prompt, x_sample, mem_prompt, mem_sample, norm_mix_pre, norm_mix_post, norm_mem,
           w_in, sink_a, w_mem_kv, w_o_a, w_o_b, w_o_c, w_out, norm_ffn_pre, norm_ffn_post,
           w_ffn_in, w_ffn_out):
    f = lambda a: np.ascontiguousarray(np.asarray(a, dtype=np.float32))
    x_all = np.concatenate([f(x_prompt), f(x_sample)], axis=0)
    mem_all = np.concatenate([f(mem_prompt), f(mem_sample)], axis=0)
    weights = dict(norm_mix_pre=f(norm_mix_pre), norm_mix_post=f(norm_mix_post), norm_mem=f(norm_mem),
                   w_in=f(w_in), sink_a=f(sink_a), w_mem_kv=f(w_mem_kv), w_o_a=f(w_o_a), w_o_b=f(w_o_b),
                   w_o_c=f(w_o_c), w_out=f(w_out), norm_ffn_pre=f(norm_ffn_pre), norm_ffn_post=f(norm_ffn_post),
                   w_ffn_in=f(w_ffn_in), w_ffn_out=f(w_ffn_out))
    y = _run(x_all, mem_all, weights)
    nb = x_prompt.shape[0]
    return (np.ascontiguousarray(y[:nb]), np.ascontiguousarray(y[nb:]))
```

```python
import numpy as np
import ml_dtypes
import concourse.bass as bass
import concourse.mybir as mybir
from concourse.bass_utils import run_bass_kernel_spmd

F32 = mybir.dt.float32
BF16 = mybir.dt.bfloat16
AF = mybir.ActivationFunctionType
ALU = mybir.AluOpType

D = 1024
S = 2048
NMEM = 256
DEPTH = 4
NCORES = 8
INW = 6656
FH = 2816
EPS = 1e-6
NEG = -30000.0
ENGS = ("pe", "act", "dve", "pool", "sp")


class Prog:
    def __init__(self):
        self.q = {e: [] for e in ENGS}
        self.count = {}
        self.seen = {e: {} for e in ENGS}
        self.res = {}
        self.epoch = 0
        self.phase = ''
        self.pe_phase = []

    def esem(self, eng):
        return "%s_%d" % (eng, self.epoch)

    def _deps(self, eng, reads, writes):
        deps = {}

        def add(s, v):
            if deps.get(s, 0) < v:
                deps[s] = v

        for r in reads:
            st = self.res.get(r)
            if st and st[0]:
                add(*st[0])
        for w in writes:
            st = self.res.get(w)
            if st:
                if st[0]:
                    add(*st[0])
                for s, v in st[1].items():
                    add(s, v)
        waits = []
        for s, v in deps.items():
            if eng == "pe" and s.startswith("pe_"):
                continue
            if self.seen[eng].get(s, 0) >= v:
                continue
            self.seen[eng][s] = v
            waits.append((s, v))
        return waits

    def _commit(self, ev, reads, writes):
        for r in reads:
            st = self.res.setdefault(r, [None, {}])
            if st[1].get(ev[0], 0) < ev[1]:
                st[1][ev[0]] = ev[1]
        for w in writes:
            self.res[w] = [ev, {}]

    def op(self, eng, fn, reads=(), writes=(), signal=True):
        waits = self._deps(eng, reads, writes)
        s = self.esem(eng)
        ev = (s, self.count.get(s, 0) + 1)
        if signal:
            self.count[s] = ev[1]
        self._commit(ev, reads, writes)
        if eng == 'pe':
            self.pe_phase.append(self.phase)
        self.q[eng].append((waits, fn, (s, 1) if signal else None))

    def dma(self, sem, fns, reads=(), writes=(), eng="sp"):
        waits = self._deps(eng, reads, writes)
        tgt = self.count.get(sem, 0) + 16 * len(fns)
        self.count[sem] = tgt
        self._commit((sem, tgt), reads, writes)
        for i, fn in enumerate(fns):
            self.q[eng].append((waits if i == 0 else [], fn, (sem, 16)))

    def barrier(self):
        for e in ENGS:
            waits = []
            for s, v in self.count.items():
                if v > 0 and self.seen[e].get(s, 0) < v:
                    if e == "pe" and s.startswith("pe_"):
                        continue
                    self.seen[e][s] = v
                    waits.append((s, v))
            if waits:
                self.q[e].append((waits, None, None))


def _consts():
    bf = ml_dtypes.bfloat16
    ident = np.eye(128, dtype=np.float32)
    perm = np.zeros((128, 128), np.float32)
    for k in range(128):
        d = k % 64
        if d < 8:
            perm[k, k + 8] = 1.0
        elif d < 16:
            perm[k, k - 8] = 1.0
    ones = np.ones((128, 128), np.float32)
    ki = np.arange(128)[:, None]
    qa = np.arange(384)[None, :]
    maskA = np.where((qa >= ki) & (qa <= ki + 256), 0.0, NEG).astype(np.float32)
    qb = np.arange(256)[None, :]
    maskB = np.where((qb >= ki) & (qb <= ki + 128), 0.0, NEG).astype(np.float32)
    cb = np.concatenate([ident, perm, ones, maskA, maskB], axis=1).astype(bf)
    inv_freq = (np.float32(500000.0) ** (-np.arange(0, 16, 2, dtype=np.float32) / np.float32(16))).astype(np.float32)
    ang = (np.arange(S, dtype=np.float32)[:, None] * inv_freq[None, :]).astype(np.float32)
    cos = np.cos(ang).astype(np.float32).T
    sin = np.sin(ang).astype(np.float32).T
    C = np.ones((128, S), np.float32)
    S2 = np.zeros((128, S), np.float32)
    for k in range(128):
        d = k % 64
        if d < 8:
            C[k] = cos[d]
            S2[k] = sin[d]
        elif d < 16:
            C[k] = cos[d - 8]
            S2[k] = -sin[d - 8]
    rope = np.stack([C, S2], axis=1)
    return cb, np.ascontiguousarray(rope)


def _win_pieces():
    P = {}
    P["KV0"] = [(512, 128), (1536, 384)]
    P["KV1"] = [(1920, 384)]
    P["V0"] = [(640, 128), (2304, 256)]
    P["V1"] = [(2560, 512)]
    segs = []
    for j in range(4):
        segs.append((64 * j, 64))
        segs.append((64 * (4 + j), 64))
    P["QA"] = segs
    P["QB0"] = [(768, 512)]
    P["QB1"] = [(1280, 256)]
    P["QC"] = [(3072, 512)]
    for i, b in enumerate("ABC"):
        P["G%s0" % b] = [(3584 + 1024 * i, 512)]
        P["G%s1" % b] = [(3584 + 1024 * i + 512, 512)]
    return P


WIN_P = _win_pieces()
WIN_ORDER = list(WIN_P.keys())


def build(NS, depth):
    nc = bass.Bass("TRN2", target_bir_lowering=False, dynamic_dma_scratch_size=256)
    p = Prog()

    def din(name, shape, dt=F32):
        return nc.dram_tensor(name, list(shape), dt, kind="ExternalInput").ap()

    x_d = din("x", [NS, S, D])
    mem_d = din("mem", [NS, NMEM, D])
    g_d = {n: din(n, [DEPTH, D]) for n in ("norm_mix_pre", "norm_mix_post", "norm_mem", "norm_ffn_pre", "norm_ffn_post")}
    sink_d = din("sink_a", [DEPTH, 8])
    w_in_d = din("w_in", [DEPTH, D, INW])
    w_mkv_d = din("w_mem_kv", [DEPTH, D, 1024])
    w_oa_d = din("w_o_a", [DEPTH, 512, D])
    w_ob_d = din("w_o_b", [DEPTH, 256, D])
    w_oc_d = din("w_o_c", [DEPTH, 512, D])
    w_out_d = din("w_out", [DEPTH, D, D])
    w_fi_d = din("w_ffn_in", [DEPTH, D, 2 * FH])
    w_fo_d = din("w_ffn_out", [DEPTH, FH, D])
    cb_d = din("c_bf", [128, 1024], BF16)
    rope_d = din("c_rope", [128, 2, S])
    y_d = nc.dram_tensor("y", [NS, S, D], F32, kind="ExternalOutput").ap()

    NPIECE = len(WIN_ORDER) + 2 + 2 + 1 + 1 + 2 + 11 + 6
    scr = nc.dram_tensor("wscr", [depth, NPIECE, 128, 4096], BF16, kind="Internal").ap()
    pidx = {}
    for n in WIN_ORDER + ["MK", "MV", "OA0", "OA1", "OB", "OC", "WO0", "WO1"] + ["FI%d" % i for i in range(11)] + ["FO%d" % i for i in range(6)]:
        pidx[n] = len(pidx)
    assert len(pidx) == NPIECE

    off = [512]

    def sb(name, shape, dt):
        nbytes = int(np.prod(shape[1:])) * (2 if dt == BF16 else 4)
        t = nc.alloc_sbuf_tensor_at(name, list(shape), dt, offset=off[0])
        off[0] += (nbytes + 31) // 32 * 32
        return t

    x_sb = sb("x_sb", [128, 16, D], F32)
    cb = sb("cb", [128, 1024], BF16)
    gA = sb("gA", [128, D], F32)
    gB = sb("gB", [128, D], F32)
    stats = sb("stats", [128, 64], F32)
    hT = sb("hT", [128, 8, 512], BF16)
    ring = [sb("ring%d" % i, [128, 4096], BF16) for i in range(3)]
    htok = sb("htok", [128, D], BF16)
    ytmp = sb("ytmp", [128, D], F32)
    junk = sb("junk", [128, D], BF16)
    phase0 = off[0]
    kT = sb("kT", [128, 7, S], BF16)
    v0 = sb("v0", [128, 16, 384], BF16)
    vb1 = sb("vb1", [128, 16, 256], BF16)
    vb2 = sb("vb2", [128, 16, 256], BF16)
    mkT = sb("mkT", [128, 4, 256], BF16)
    mv = sb("mv", [128, 2, 512], BF16)
    rope = sb("rope", [128, 2, 512], F32)
    qreg = off[0]
    qT = sb("qT", [128, 8, 512], BF16)
    OT = sb("OT", [128, 8, 512], BF16)
    merged = sb("merged", [128, 8, 512], F32)
    sg = [sb("sg%d" % i, [128, 512], F32) for i in range(2)]
    pT = [sb("pT%d" % i, [128, 512], BF16) for i in range(3)]
    rqc = sb("rqc", [128, 512], F32)
    rqs = sb("rqs", [128, 512], BF16)
    rec = sb("rec", [128, 512], F32)
    mix_end = off[0]
    hTf = nc.alloc_sbuf_tensor_at("hTf", [128, 8, S], BF16, offset=qreg)
    mbf = nc.alloc_sbuf_tensor_at("mbf", [128, 8, 512], BF16, offset=qreg)
    memT = nc.alloc_sbuf_tensor_at("memT", [128, 8, 256], BF16, offset=qreg + 32768)
    off[0] = phase0
    wfo = sb("wfo", [128, 22, D], BF16)
    actT = sb("actT", [128, 22, 512], BF16)
    sgf = [sb("sgf%d" % i, [128, 512], F32) for i in range(2)]
    off[0] = phase0
    stg = [sb("stg%d" % i, [128, 4096], F32) for i in range(2)]
    stb = [sb("stb%d" % i, [128, 4096], BF16) for i in range(2)]
    assert mix_end < nc.sbuf_top, (mix_end, nc.sbuf_top)

    psm = nc.alloc_psum_tensor("psm", [128, 4096], F32)
    pst = psm[:, 3584:4096].bitcast(BF16)
    MMB = (0, 1, 2, 3)

    def bank(b, n=512):
        return psm[:, 512 * b:512 * b + n]

    ident = cb[:, 0:128]
    perm = cb[:, 128:256]
    ones = cb[:, 256:384]
    maskA = cb[:, 384:768]
    maskB = cb[:, 768:1024]
    eps_c = stats[:, 63:64]

    E = {"pe": "tensor", "act": "scalar", "dve": "vector", "pool": "gpsimd", "sp": "sync"}

    def mm(out, lhsT, rhs, start, stop, reads, writes, signal):
        p.op("pe", lambda e: e.matmul(out, lhsT, rhs, start=start, stop=stop), reads, writes, signal)

    def act(out, in_, func, reads, writes, **kw):
        p.op("act", lambda e: e.activation(out, in_, func, **kw), reads, writes)

    def tt(out, in0, in1, op, reads, writes, eng="dve"):
        p.op(eng, lambda e: e.tensor_tensor(out, in0, in1, op), reads, writes)

    rr = [0]

    def load_piece(l, name, parts=128):
        s = rr[0] % 3
        rr[0] += 1
        src = scr[l, pidx[name]]
        dst = ring[s]
        p.dma("ring%d" % s, [lambda e: e.dma_start(out=dst[0:parts, :], in_=src[0:parts, :])],
              reads=["scr"], writes=["ring%d" % s])
        return dst, "ring%d" % s

    mmrot = [0]

    def mmbank(banks):
        b = banks[mmrot[0] % len(banks)]
        mmrot[0] += 1
        return b

    pl = [0]
    cast_engs = ("act", "dve", "pool")

    def convert(l, name, loads, parts=128, width=4096):
        i = pl[0] % 2
        eng = cast_engs[pl[0] % 3]
        pl[0] += 1
        st_, sbt = stg[i], stb[i]
        fns = []
        for dv, src in loads:
            d_ = dv(st_)
            fns.append(lambda e, d_=d_, src=src: e.dma_start(out=d_, in_=src))
        p.dma("plin%d" % i, fns, reads=[], writes=["stg%d" % i])
        if eng == "act":
            p.op("act", lambda e: e.activation(sbt[0:parts, 0:width], st_[0:parts, 0:width], AF.Copy),
                 ["stg%d" % i], ["stb%d" % i])
        else:
            p.op(eng, lambda e: e.tensor_copy(sbt[0:parts, 0:width], st_[0:parts, 0:width]),
                 ["stg%d" % i], ["stb%d" % i])
        dst = scr[l, pidx[name]]
        p.dma("plout%d" % i, [lambda e: e.dma_start(out=dst[0:parts, 0:width], in_=sbt[0:parts, 0:width])],
              reads=["stb%d" % i], writes=["scr"])

    def prologue():
        for l in range(depth):
            for name in WIN_ORDER:
                segs = WIN_P[name]
                tot = sum(n for _, n in segs)
                loads = []
                o = 0
                for (c0, n) in segs:
                    src = w_in_d[l, :, c0:c0 + n].rearrange("(kc p) c -> p kc c", p=128)
                    loads.append((lambda t, o=o, n=n, tot=tot: t[:, 0:8 * tot].rearrange("p (kc c) -> p kc c", kc=8)[:, :, o:o + n], src))
                    o += n
                convert(l, name, loads, width=8 * tot)
            for name, c0 in (("MK", 0), ("MV", 512)):
                src = w_mkv_d[l, :, c0:c0 + 512].rearrange("(kc p) c -> p kc c", p=128)
                convert(l, name, [(lambda t: t[:, :].rearrange("p (kc c) -> p kc c", kc=8), src)])
            for hf in range(2):
                src = w_oa_d[l, :, 512 * hf:512 * hf + 512].rearrange("(h d) c -> d h c", d=64)
                convert(l, "OA%d" % hf, [(lambda t: t[0:64, :].rearrange("p (h c) -> p h c", h=8), src)], parts=64)
            src = w_ob_d[l].rearrange("(h d) c -> d h c", d=64)
            convert(l, "OB", [(lambda t: t[0:64, :].rearrange("p (h c) -> p h c", h=4), src)], parts=64)
            src = w_oc_d[l].rearrange("(h d) c -> d h c", d=128)
            convert(l, "OC", [(lambda t: t[:, :].rearrange("p (h c) -> p h c", h=4), src)])
            for hf in range(2):
                src = w_out_d[l, :, 512 * hf:512 * hf + 512].rearrange("(kc p) c -> p kc c", p=128)
                convert(l, "WO%d" % hf, [(lambda t: t[:, :].rearrange("p (kc c) -> p kc c", kc=8), src)])
            for i in range(11):
                loads = []
                for j in range(2):
                    c = 2 * i + j
                    for k, base in enumerate((0, FH)):
                        src = w_fi_d[l, :, base + 128 * c:base + 128 * c + 128].rearrange("(kc p) c -> p kc c", p=128)
                        o = 256 * j + 128 * k
                        loads.append((lambda t, o=o: t[:, :].rearrange("p (kc c) -> p kc c", kc=8)[:, :, o:o + 128], src))
                convert(l, "FI%d" % i, loads)
            for i in range(6):
                nch = 4 if i < 5 else 2
                src = w_fo_d[l, 512 * i:512 * i + 128 * nch, :].rearrange("(c p) n -> p c n", p=128)
                convert(l, "FO%d" % i, [(lambda t, nch=nch: t[:, 0:1024 * nch].rearrange("p (c n) -> p c n", c=nch), src)],
                        width=1024 * nch)

    def load_g(buf, bname, which, l):
        src = g_d[which][l].partition_broadcast(128)
        p.dma("gld_" + bname, [lambda e: e.dma_start(out=buf[:, :], in_=src)], reads=[], writes=[bname])

    def rstd_from(ss_ap, n, reads_w):
        act(ss_ap, ss_ap, AF.Sqrt, [reads_w], [reads_w], bias=eps_c, scale=1.0 / D)
        p.op("dve", lambda e: e.reciprocal(ss_ap, ss_ap), [reads_w], [reads_w])

    def norm_tok(src_fn, nst, gbuf, gname, dst_fn, dst_res, src_res_fn):
        for st in range(nst):
            xin = src_fn(st)
            act(junk[:, :], xin, AF.Square, [src_res_fn(st)], ["junk", "ss%d" % st], accum_out=stats[:, st:st + 1])
        for st in range(nst):
            rstd_from(stats[:, st:st + 1], 1, "ss%d" % st)
        for st in range(nst):
            xin = src_fn(st)
            p.op("dve", lambda e, xin=xin, st=st: e.scalar_tensor_tensor(htok[:, :], xin, stats[:, st:st + 1], gbuf[:, :], ALU.mult, ALU.mult),
                 [src_res_fn(st), "ss%d" % st, gname], ["htok"])
            for c in range(8):
                p.op("pe", lambda e, c=c: e.transpose(pst[:, 128 * c:128 * c + 128], htok[:, 128 * c:128 * c + 128], ident),
                     ["htok", "cb"], ["ps7"], signal=(c == 7))
            d_ = dst_fn(st)
            act(d_, pst[:, :].rearrange("p (c t) -> p c t", c=8), AF.Copy, ["ps7"], [dst_res(st)])

    def proj_fm(piece, pres, ncols_tot, c0, rhs_fn, rhs_res, n, banks):
        b = mmbank(banks)
        pv = piece[:, 0:8 * ncols_tot].rearrange("p (kc c) -> p kc c", kc=8)
        for kc in range(8):
            mm(bank(b, n), pv[:, kc, c0:c0 + 128], rhs_fn(kc), kc == 0, kc == 7, [pres] + rhs_res, ["ps%d" % b], kc == 7)
        return b

    def rotary_evac(b, dst, dst_res, n=512):
        tt(rqc[:, 0:n], bank(b, n), rope[:, 0, 0:n], ALU.mult, ["ps%d" % b, "rope"], ["rqc"])
        tt(rqs[:, 0:n], bank(b, n), rope[:, 1, 0:n], ALU.mult, ["ps%d" % b, "rope"], ["rqs"])

        def stage2():
            b2 = mmbank(MMB)
            mm(bank(b2, n), perm, rqs[:, 0:n], True, True, ["rqs", "cb"], ["ps%d" % b2], True)
            tt(dst, bank(b2, n), rqc[:, 0:n], ALU.add, ["ps%d" % b2, "rqc"], [dst_res])
        return stage2

    def rot_chunks(items):
        pend = None
        for pj, dst, dres in items:
            b = pj()
            if pend:
                pend()
            pend = rotary_evac(b, dst, dres)
        if pend:
            pend()

    ptrot = [0]
    odrot = [0]

    def od_pair():
        pr = ((4, 5), (6, 7))[odrot[0] % 2]
        odrot[0] += 1
        return pr

    def emit_tiles(tiles, ob, db, LA=2):
        nt = len(tiles)
        slots = []
        for i in range(nt + LA):
            if i < nt:
                kT_ap, qT_ap, n, mask_ap, scale, v_ap, o_cols = tiles[i]
                sbk = mmbank(MMB)
                pi = ptrot[0] % 3
                ptrot[0] += 1
                slots.append(pi)
                mm(bank(sbk, n), kT_ap, qT_ap, True, mask_ap is None, ["kv", "qT"], ["ps%d" % sbk], mask_ap is None)
                if mask_ap is not None:
                    mm(bank(sbk, n), ident, mask_ap, False, True, ["cb"], ["ps%d" % sbk], True)
                act(pT[pi][:, 0:n], bank(sbk, n), AF.Exp, ["ps%d" % sbk], ["pT%d" % pi], scale=scale)
            if i >= LA:
                k = i - LA
                kT_ap, qT_ap, n, mask_ap, scale, v_ap, o_cols = tiles[k]
                pi = slots[k]
                M = v_ap.shape[-1]
                mm(psm[0:M, 512 * ob:512 * ob + 512][:, o_cols], v_ap, pT[pi][:, 0:n], k == 0, k == nt - 1,
                   ["pT%d" % pi, "kv"], ["ps%d" % ob], False)
                mm(psm[0:M, 512 * db:512 * db + 512][:, o_cols], ones[:, 0:M], pT[pi][:, 0:n], k == 0, k == nt - 1,
                   ["pT%d" % pi, "cb"], ["ps%d" % db], k == nt - 1)

    def band_tiles(tt_i, R, dil, L, mask, kT_fn, qT_fn, v_fn):
        t0 = 512 * tt_i
        out = []
        for c in range(dil):
            mlo, mhi = t0 // dil, (t0 + 512) // dil
            for kt in range(L // 128):
                lo = max(128 * kt - R, 0, mlo)
                hi = min(128 * kt + 128 + R, L, mhi)
                if hi <= lo:
                    continue
                n = hi - lo
                qi0 = lo - (128 * kt - R)
                loc = lo * dil + c - t0
                if dil == 1:
                    cols = slice(loc, loc + n)
                else:
                    cols = slice(loc, loc + (n - 1) * dil + 1, dil)
                out.append((kT_fn(c, kt), qT_fn(cols), n, mask[:, qi0:qi0 + n], 0.125, v_fn(c, kt), cols))
        return out

    def layer(sq, l):
        p.phase = 'kv_norm'
        p.barrier()
        load_g(gA, "gA", "norm_mix_pre", l)
        load_g(gB, "gB", "norm_mem", l)
        src = sink_d[l].partition_broadcast(128)
        p.dma("gld_sink", [lambda e: e.dma_start(out=stats[:, 48:56], in_=src)], reads=[], writes=["sink"])
        act(stats[:, 48:56], stats[:, 48:56], AF.Exp, ["sink"], ["sink"])
        for t4 in range(4):
            norm_tok(lambda st: x_sb[:, 4 * t4 + st, :], 4, gA, "gA",
                     lambda st: hTf[:, :, 512 * t4 + 128 * st:512 * t4 + 128 * st + 128],
                     lambda st: "hTf", lambda st: "x%d" % (4 * t4 + st))
        p.phase = 'kv_K'
        for pn, nchunk, cbase in (("KV0", 4, 0), ("KV1", 3, 4)):
            piece, pres = load_piece(l, pn)
            for t4 in range(4):
                p.dma("ropeld", [lambda e, t4=t4: e.dma_start(out=rope[:, :, :], in_=rope_d[:, :, 512 * t4:512 * t4 + 512])],
                      reads=[], writes=["rope"])
                rot_chunks([((lambda ch=ch, t4=t4: proj_fm(piece, pres, 128 * nchunk, 128 * ch,
                                                          lambda kc: hTf[:, kc, 512 * t4:512 * t4 + 512], ["hTf"], 512, MMB)),
                             kT[:, cbase + ch, 512 * t4:512 * t4 + 512], "kv") for ch in range(nchunk)])
        p.phase = 'kv_V'
        piece, pres = load_piece(l, "V0")
        pv = piece[:, 0:8 * 384].rearrange("p (kc c) -> p kc c", kc=8)
        for g in range(16):
            b = mmbank(MMB)
            for kc in range(8):
                mm(bank(b, 384), hTf[:, kc, 128 * g:128 * g + 128], pv[:, kc, :], kc == 0, kc == 7, [pres, "hTf"], ["ps%d" % b], kc == 7)
            act(v0[:, g, :], bank(b, 384), AF.Copy, ["ps%d" % b], ["kv"])
        piece, pres = load_piece(l, "V1")
        pv = piece[:, :].rearrange("p (kc c) -> p kc c", kc=8)
        for c in range(4):
            for kt in range(4):
                b = mmbank(MMB)
                for kc in range(8):
                    mm(bank(b, 256), hTf[:, kc, 512 * kt + c:512 * kt + 512:4], pv[:, kc, 0:256], kc == 0, kc == 7, [pres, "hTf"], ["ps%d" % b], kc == 7)
                p.op("dve", lambda e, b=b, c=c, kt=kt: e.tensor_copy(vb1[:, 4 * c + kt, :], bank(b, 256)), ["ps%d" % b], ["kv"])
        for c in range(16):
            b = mmbank(MMB)
            for kc in range(8):
                mm(bank(b, 256), hTf[:, kc, c:S:16], pv[:, kc, 256:512], kc == 0, kc == 7, [pres, "hTf"], ["ps%d" % b], kc == 7)
            p.op("dve", lambda e, b=b, c=c: e.tensor_copy(vb2[:, c, :], bank(b, 256)), ["ps%d" % b], ["kv"])
        p.phase = 'kv_mem'
        for mt in range(2):
            p.dma("memld", [lambda e, mt=mt: e.dma_start(out=ytmp[:, :], in_=mem_d[sq, 128 * mt:128 * mt + 128, :])], reads=[], writes=["ytmp"])
            norm_tok(lambda st: ytmp[:, :], 1, gB, "gB", lambda st, mt=mt: memT[:, :, 128 * mt:128 * mt + 128], lambda st: "memT", lambda st: "ytmp")
        piece, pres = load_piece(l, "MK")
        for h in range(4):
            b = proj_fm(piece, pres, 512, 128 * h, lambda kc: memT[:, kc, :], ["memT"], 256, MMB)
            act(mkT[:, h, :], bank(b, 256), AF.Copy, ["ps%d" % b], ["kv"])
        piece, pres = load_piece(l, "MV")
        pv = piece[:, :].rearrange("p (kc c) -> p kc c", kc=8)
        for mt in range(2):
            b = mmbank(MMB)
            for kc in range(8):
                mm(bank(b), memT[:, kc, 128 * mt:128 * mt + 128], pv[:, kc, :], kc == 0, kc == 7, [pres, "memT"], ["ps%d" % b], kc == 7)
            act(mv[:, mt, :], bank(b), AF.Copy, ["ps%d" % b], ["kv"])
        p.barrier()
        load_g(gB, "gB", "norm_mix_post", l)

        for t4 in range(4):
            p.phase = 'q_norm'
            norm_tok(lambda st: x_sb[:, 4 * t4 + st, :], 4, gA, "gA",
                     lambda st: hT[:, :, 128 * st:128 * st + 128], lambda st: "hT", lambda st: "x%d" % (4 * t4 + st))
            p.dma("ropeld", [lambda e, t4=t4: e.dma_start(out=rope[:, :, :], in_=rope_d[:, :, 512 * t4:512 * t4 + 512])],
                  reads=[], writes=["rope"])
            hrhs = lambda kc: hT[:, kc, :]

            def outproj(branch, nh, kdim, wfn, first_b, last_b):
                for hf in range(2):
                    gp, gres = load_piece(l, "G%s%d" % (branch, hf))
                    wpc, wres, wv = wfn(hf)
                    for o4 in range(4):
                        oc = 4 * hf + o4
                        bg = proj_fm(gp, gres, 512, 128 * o4, hrhs, ["hT"], 512, MMB)
                        sgt = sg[oc % 2]
                        sgn = "sg%d" % (oc % 2)
                        act(sgt[:, :], bank(bg), AF.Sigmoid, ["ps%d" % bg], [sgn])
                        bo = mmbank(MMB)
                        for h in range(nh):
                            mm(bank(bo), wv(h, o4), OT[0:kdim, h, :], h == 0, h == nh - 1, [wres, "OT"], ["ps%d" % bo], h == nh - 1)
                        if first_b:
                            tt(merged[:, oc, :], bank(bo), sgt[:, :], ALU.mult, ["ps%d" % bo, sgn], ["merged"])
                        else:
                            tt(rec[:, :], bank(bo), sgt[:, :], ALU.mult, ["ps%d" % bo, sgn], ["rec"])
                            if last_b:
                                tt(mbf[:, oc, :], merged[:, oc, :], rec[:, :], ALU.add, ["merged", "rec"], ["qT"])
                            else:
                                tt(merged[:, oc, :], merged[:, oc, :], rec[:, :], ALU.add, ["merged", "rec"], ["merged"])

            p.phase = 'qA_proj'
            piece, pres = load_piece(l, "QA")
            rot_chunks([((lambda ch=ch: proj_fm(piece, pres, 512, 128 * ch, hrhs, ["hT"], 512, MMB)), qT[:, ch, :], "qT")
                        for ch in range(4)])
            p.phase = 'qA_attn'
            for h in range(8):
                j, pb = h % 4, 64 * (h // 4)
                ob, db = od_pair()
                tiles = band_tiles(t4, 128, 1, S, maskA,
                                   lambda c, kt, pb=pb: kT[pb:pb + 64, 0, 128 * kt:128 * kt + 128],
                                   lambda cols, j=j, pb=pb: qT[pb:pb + 64, j, cols],
                                   lambda c, kt, h=h: v0[:, kt, 64 * (h // 4):64 * (h // 4) + 64])
                emit_tiles(tiles, ob, db)
                p.op("dve", lambda e, h=h, db=db: e.tensor_scalar(rec[0:64, :], bank(db)[0:64, :], stats[0:64, 48 + h:49 + h], None, ALU.add),
                     ["ps%d" % db, "sink"], ["rec"])
                p.op("dve", lambda e: e.reciprocal(rec[0:64, :], rec[0:64, :]), ["rec"], ["rec"])
                tt(OT[0:64, h, :], bank(ob)[0:64, :], rec[0:64, :], ALU.mult, ["ps%d" % ob, "rec"], ["OT"])

            def wfa(hf):
                wpc, wres = load_piece(l, "OA%d" % hf, parts=64)
                v = wpc[0:64, :].rearrange("p (h c) -> p h c", h=8)
                return wpc, wres, (lambda h, o4: v[:, h, 128 * o4:128 * o4 + 128])
            p.phase = 'qA_out'
            outproj("A", 8, 64, wfa, True, False)

            p.phase = 'qB_proj'
            for pn, nchunk, cbase in (("QB0", 4, 0), ("QB1", 2, 4)):
                piece, pres = load_piece(l, pn)
                rot_chunks([((lambda ch=ch, piece=piece, pres=pres, nchunk=nchunk: proj_fm(piece, pres, 128 * nchunk, 128 * ch, hrhs, ["hT"], 512, MMB)),
                             qT[:, cbase + ch, :], "qT") for ch in range(nchunk)])
            p.phase = 'qB_attn'
            for j in range(4):
                tiles = []
                for gi, (dil, L) in enumerate(((1, 2048), (4, 512), (16, 128))):
                    hb = 4 * gi + j
                    chq, pb = hb // 2, 64 * (hb % 2)
                    if gi == 0:
                        vfn = lambda c, kt, j=j: v0[:, kt, 128 + 64 * j:128 + 64 * j + 64]
                    elif gi == 1:
                        vfn = lambda c, kt, j=j: vb1[:, 4 * c + kt, 64 * j:64 * j + 64]
                    else:
                        vfn = lambda c, kt, j=j: vb2[:, c, 64 * j:64 * j + 64]

                    def kfn(c, kt, chq=chq, pb=pb, dil=dil):
                        s0 = 128 * kt * dil + c
                        if dil == 1:
                            return kT[pb:pb + 64, 1 + chq, s0:s0 + 128]
                        return kT[pb:pb + 64, 1 + chq, s0:s0 + 127 * dil + 1:dil]
                    tiles += band_tiles(t4, 64, dil, L, maskB, kfn,
                                        lambda cols, chq=chq, pb=pb: qT[pb:pb + 64, chq, cols], vfn)
                ob, db = od_pair()
                emit_tiles(tiles, ob, db)
                p.op("dve", lambda e, db=db: e.reciprocal(rec[0:64, :], bank(db)[0:64, :]), ["ps%d" % db], ["rec"])
                tt(OT[0:64, j, :], bank(ob)[0:64, :], rec[0:64, :], ALU.mult, ["ps%d" % ob, "rec"], ["OT"])

            def wfb(hf):
                if hf == 0:
                    wfb.pc = load_piece(l, "OB", parts=64)
                wpc, wres = wfb.pc
                v = wpc[0:64, :].rearrange("p (h c) -> p h c", h=4)
                return wpc, wres, (lambda h, o4: v[:, h, 512 * hf + 128 * o4:512 * hf + 128 * o4 + 128])
            p.phase = 'qB_out'
            outproj("B", 4, 64, wfb, False, False)

            p.phase = 'qC_proj'
            piece, pres = load_piece(l, "QC")
            for ch in range(4):
                b = proj_fm(piece, pres, 512, 128 * ch, hrhs, ["hT"], 512, MMB)
                act(qT[:, ch, :], bank(b), AF.Copy, ["ps%d" % b], ["qT"])
            p.phase = 'qC_attn'
            for h in range(4):
                ob, db = od_pair()
                tiles = [(mkT[:, h, 128 * mt:128 * mt + 128], qT[:, h, :], 512, None, float(128 ** -0.5),
                          mv[:, mt, 128 * h:128 * h + 128], slice(0, 512)) for mt in range(2)]
                emit_tiles(tiles, ob, db)
                p.op("dve", lambda e, db=db: e.reciprocal(rec[:, :], bank(db)), ["ps%d" % db], ["rec"])
                tt(OT[:, h, :], bank(ob), rec[:, :], ALU.mult, ["ps%d" % ob, "rec"], ["OT"])

            def wfc(hf):
                if hf == 0:
                    wfc.pc = load_piece(l, "OC")
                wpc, wres = wfc.pc
                v = wpc[:, :].rearrange("p (h c) -> p h c", h=4)
                return wpc, wres, (lambda h, o4: v[:, h, 512 * hf + 128 * o4:512 * hf + 128 * o4 + 128])
            p.phase = 'qC_out'
            outproj("C", 4, 128, wfc, False, True)

            p.phase = 'q_wout'
            wo = [load_piece(l, "WO%d" % hf) for hf in range(2)]
            for st in range(4):
                g = 4 * t4 + st
                yb = (0, 2)[st % 2]
                for hf in range(2):
                    wv = wo[hf][0][:, :].rearrange("p (kc c) -> p kc c", kc=8)
                    for kc in range(8):
                        mm(bank(yb + hf), mbf[:, kc, 128 * st:128 * st + 128], wv[:, kc, :], kc == 0, kc == 7, [wo[hf][1], "qT"], ["ps%d" % (yb + hf)], kc == 7)
                post_norm(psm[:, 512 * yb:512 * yb + 1024], ["ps%d" % yb, "ps%d" % (yb + 1)], g)

    def post_norm(yps, yres, g):
        act(junk[:, :], yps, AF.Square, yres, ["junk", "ssp"], accum_out=stats[:, 32:33])
        rstd_from(stats[:, 32:33], 1, "ssp")
        p.op("dve", lambda e: e.scalar_tensor_tensor(ytmp[:, :], yps, stats[:, 32:33], gB[:, :], ALU.mult, ALU.mult),
             yres + ["ssp", "gB"], ["ytmp"])
        tt(x_sb[:, g, :], x_sb[:, g, :], ytmp[:, :], ALU.add, ["x%d" % g, "ytmp"], ["x%d" % g], eng="pool")

    def ffn(sq, l):
        p.barrier()
        load_g(gA, "gA", "norm_ffn_pre", l)
        load_g(gB, "gB", "norm_ffn_post", l)
        for i in range(6):
            nch = 4 if i < 5 else 2
            src = scr[l, pidx["FO%d" % i]]
            p.dma("wfold", [lambda e, i=i, nch=nch, src=src: e.dma_start(
                out=wfo[:, 4 * i:4 * i + nch, :], in_=src[:, 0:1024 * nch].rearrange("p (c n) -> p c n", c=nch))],
                reads=["scr"], writes=["wfo"])
        for t4 in range(4):
            p.phase = 'f_norm'
            norm_tok(lambda st: x_sb[:, 4 * t4 + st, :], 4, gA, "gA",
                     lambda st: hT[:, :, 128 * st:128 * st + 128], lambda st: "hT", lambda st: "x%d" % (4 * t4 + st))
            hrhs = lambda kc: hT[:, kc, :]
            pr = 0
            p.phase = 'f_gu'
            for i in range(11):
                piece, pres = load_piece(l, "FI%d" % i)
                for j in range(2):
                    c = 2 * i + j
                    banks = ((0, 1), (2, 3), (4, 5))[pr % 3]
                    pr += 1
                    bg = proj_fm(piece, pres, 512, 256 * j, hrhs, ["hT"], 512, (banks[0],))
                    bu = proj_fm(piece, pres, 512, 256 * j + 128, hrhs, ["hT"], 512, (banks[1],))
                    sf = sgf[c % 2]
                    sfn = "sgf%d" % (c % 2)
                    act(sf[:, :], bank(bg), AF.Silu, ["ps%d" % bg], [sfn])
                    tt(actT[:, c, :], bank(bu), sf[:, :], ALU.mult, ["ps%d" % bu, sfn], ["actT"])
            p.phase = 'f_y2'
            for st in range(4):
                g = 4 * t4 + st
                pb_ = (0, 2, 4)[st % 3]
                for hf in range(2):
                    for c in range(22):
                        mm(bank(pb_ + hf), actT[:, c, 128 * st:128 * st + 128], wfo[:, c, 512 * hf:512 * hf + 512],
                           c == 0, c == 21, ["actT", "wfo"], ["ps%d" % (pb_ + hf)], c == 21)
                post_norm(psm[:, 512 * pb_:512 * pb_ + 1024], ["ps%d" % pb_, "ps%d" % (pb_ + 1)], g)

    p.dma("cst", [lambda e: e.dma_start(out=cb[:, :], in_=cb_d)], reads=[], writes=["cb"])
    p.op("dve", lambda e: e.memset(stats[:, :], 0.0), [], ["stats0"])
    p.op("dve", lambda e: e.memset(eps_c, EPS), ["stats0"], ["eps"])
    prologue()
    p.barrier()
    for sq in range(NS):
        p.epoch = sq
        p.dma("xld", [lambda e, sq=sq, q4=q4: e.dma_start(out=x_sb[:, 4 * q4:4 * q4 + 4, :],
                                                      in_=x_d[sq, 512 * q4:512 * q4 + 512, :].rearrange("(g p) d -> p g d", p=128)) for q4 in range(4)],
              reads=[], writes=["x%d" % g for g in range(16)])
        for l in range(depth):
            layer(sq, l)
            ffn(sq, l)
        p.dma("xst", [lambda e, sq=sq, q4=q4: e.dma_start(out=y_d[sq, 512 * q4:512 * q4 + 512, :].rearrange("(g p) d -> p g d", p=128),
                                                      in_=x_sb[:, 4 * q4:4 * q4 + 4, :]) for q4 in range(4)],
              reads=["x%d" % g for g in range(16)], writes=[])
    p.barrier()

    sems = {}
    for s in p.count:
        sems[s] = nc.alloc_semaphore(s)

    def replay(eng, e):
        for waits, fn, inc in p.q[eng]:
            for s, v in waits:
                e.wait_ge(sems[s], v)
            if fn is None:
                continue
            ins = fn(e)
            if inc is not None:
                ins.then_inc(sems[inc[0]], inc[1])

    with nc.Block() as block:
        @block.tensor
        def _(e):
            replay("pe", e)

        @block.scalar
        def _(e):
            replay("act", e)

        @block.vector
        def _(e):
            replay("dve", e)

        @block.gpsimd
        def _(e):
            replay("pool", e)

        @block.sync
        def _(e):
            replay("sp", e)
    return nc, p


_CACHE = {}


def _run(x_all, mem_all, weights, depth=DEPTH):
    NS = x_all.shape[0] // NCORES
    key = (NS, depth)
    if key not in _CACHE:
        _CACHE[key] = build(NS, depth)[0]
    nc = _CACHE[key]
    cbc, ropec = _consts()
    in_maps = []
    for c in range(NCORES):
        m = {"x": np.ascontiguousarray(x_all[NS * c:NS * (c + 1)]),
             "mem": np.ascontiguousarray(mem_all[NS * c:NS * (c + 1)]),
             "c_bf": cbc, "c_rope": ropec}
        m.update(weights)
        in_maps.append(m)
    res = run_bass_kernel_spmd(nc, in_maps, core_ids=list(range(NCORES)))
    return np.concatenate([np.asarray(r["y"]) for r in res.results], axis=0)


def kernel(x_prompt, x_sample, mem_prompt, mem_sample, norm_mix_pre, norm_mix_post, norm_mem,
           w_in, sink_a, w_mem_kv, w_o_a, w_o_b, w_o_c, w_out, norm_ffn_pre, norm_ffn_post,
           w_ffn_in, w_ffn_out):
    f = lambda a: np.ascontiguousarray(np.asarray(a, dtype=np.float32))
    x_all = np.concatenate([f(x_prompt), f(x_sample)], axis=0)
    mem_all = np.concatenate([f(mem_prompt), f(mem_sample)], axis=0)
    weights = dict(norm_mix_pre=f(norm_mix_pre), norm_mix_post=f(norm_mix_post), norm_mem=f(norm_mem),
                   w_in=f(w_in), sink_a=f(sink_a), w_mem_kv=f(w_mem_kv), w_o_a=f(w_o_a), w_o_b=f(w_o_b),
                   w_o_c=f(w_o_c), w_out=f(w_out), norm_ffn_pre=f(norm_ffn_pre), norm_ffn_post=f(norm_ffn_post),
                   w_ffn_in=f(w_ffn_in), w_ffn_out=f(w_ffn_out))
    y = _run(x_all, mem_all, weights)
    nb = x_prompt.shape[0]
    return (np.ascontiguousarray(y[:nb]), np.ascontiguousarray(y[nb:]))
```

```python
import numpy as np
import ml_dtypes
import concourse.bass as bass
import concourse.mybir as mybir
from concourse.bass_utils import run_bass_kernel_spmd

F32 = mybir.dt.float32
BF16 = mybir.dt.bfloat16
AF = mybir.ActivationFunctionType
ALU = mybir.AluOpType

D = 1024
S = 2048
NMEM = 256
DEPTH = 4
NCORES = 8
INW = 6656
FH = 2816
EPS = 1e-6
NEG = -30000.0
ENGS = ("pe", "act", "dve", "pool", "sp")


class Prog:
    def __init__(self):
        self.q = {e: [] for e in ENGS}
        self.count = {}
        self.seen = {e: {} for e in ENGS}
        self.res = {}
        self.epoch = 0
        self.phase = ''
        self.pe_phase = []

    def esem(self, eng):
        return "%s_%d" % (eng, self.epoch)

    def _deps(self, eng, reads, writes):
        deps = {}

        def add(s, v):
            if deps.get(s, 0) < v:
                deps[s] = v

        for r in reads:
            st = self.res.get(r)
            if st and st[0]:
                add(*st[0])
        for w in writes:
            st = self.res.get(w)
            if st:
                if st[0]:
                    add(*st[0])
                for s, v in st[1].items():
                    add(s, v)
        waits = []
        for s, v in deps.items():
            if eng == "pe" and s.startswith("pe_"):
                continue
            if self.seen[eng].get(s, 0) >= v:
                continue
            self.seen[eng][s] = v
            waits.append((s, v))
        return waits

    def _commit(self, ev, reads, writes):
        for r in reads:
            st = self.res.setdefault(r, [None, {}])
            if st[1].get(ev[0], 0) < ev[1]:
                st[1][ev[0]] = ev[1]
        for w in writes:
            self.res[w] = [ev, {}]

    def op(self, eng, fn, reads=(), writes=(), signal=True):
        waits = self._deps(eng, reads, writes)
        s = self.esem(eng)
        ev = (s, self.count.get(s, 0) + 1)
        if signal:
            self.count[s] = ev[1]
        self._commit(ev, reads, writes)
        if eng == 'pe':
            self.pe_phase.append(self.phase)
        self.q[eng].append((waits, fn, (s, 1) if signal else None))

    def dma(self, sem, fns, reads=(), writes=(), eng="sp"):
        waits = self._deps(eng, reads, writes)
        tgt = self.count.get(sem, 0) + 16 * len(fns)
        self.count[sem] = tgt
        self._commit((sem, tgt), reads, writes)
        for i, fn in enumerate(fns):
            self.q[eng].append((waits if i == 0 else [], fn, (sem, 16)))

    def barrier(self):
        for e in ENGS:
            waits = []
            for s, v in self.count.items():
                if v > 0 and self.seen[e].get(s, 0) < v:
                    if e == "pe" and s.startswith("pe_"):
                        continue
                    self.seen[e][s] = v
                    waits.append((s, v))
            if waits:
                self.q[e].append((waits, None, None))


def _consts():
    bf = ml_dtypes.bfloat16
    ident = np.eye(128, dtype=np.float32)
    perm = np.zeros((128, 128), np.float32)
    for k in range(128):
        d = k % 64
        if d < 8:
            perm[k, k + 8] = 1.0
        elif d < 16:
            perm[k, k - 8] = 1.0
    ones = np.ones((128, 128), np.float32)
    ki = np.arange(128)[:, None]
    qa = np.arange(384)[None, :]
    maskA = np.where((qa >= ki) & (qa <= ki + 256), 0.0, NEG).astype(np.float32)
    qb = np.arange(256)[None, :]
    maskB = np.where((qb >= ki) & (qb <= ki + 128), 0.0, NEG).astype(np.float32)
    cb = np.concatenate([ident, perm, ones, maskA, maskB], axis=1).astype(bf)
    inv_freq = (np.float32(500000.0) ** (-np.arange(0, 16, 2, dtype=np.float32) / np.float32(16))).astype(np.float32)
    ang = (np.arange(S, dtype=np.float32)[:, None] * inv_freq[None, :]).astype(np.float32)
    cos = np.cos(ang).astype(np.float32).T
    sin = np.sin(ang).astype(np.float32).T
    C = np.ones((128, S), np.float32)
    S2 = np.zeros((128, S), np.float32)
    for k in range(128):
        d = k % 64
        if d < 8:
            C[k] = cos[d]
            S2[k] = sin[d]
        elif d < 16:
            C[k] = cos[d - 8]
            S2[k] = -sin[d - 8]
    rope = np.stack([C, S2], axis=1)
    return cb, np.ascontiguousarray(rope)


def _win_pieces():
    P = {}
    P["KV0"] = [(512, 128), (1536, 384)]
    P["KV1"] = [(1920, 384)]
    P["V0"] = [(640, 128), (2304, 256)]
    P["V1"] = [(2560, 512)]
    segs = []
    for j in range(4):
        segs.append((64 * j, 64))
        segs.append((64 * (4 + j), 64))
    P["QA"] = segs
    P["QB0"] = [(768, 512)]
    P["QB1"] = [(1280, 256)]
    P["QC"] = [(3072, 512)]
    for i, b in enumerate("ABC"):
        P["G%s0" % b] = [(3584 + 1024 * i, 512)]
        P["G%s1" % b] = [(3584 + 1024 * i + 512, 512)]
    return P


WIN_P = _win_pieces()
WIN_ORDER = list(WIN_P.keys())


def build(NS, depth):
    nc = bass.Bass("TRN2", target_bir_lowering=False, dynamic_dma_scratch_size=256)
    p = Prog()

    def din(name, shape, dt=F32):
        return nc.dram_tensor(name, list(shape), dt, kind="ExternalInput").ap()

    x_d = din("x", [NS, S, D])
    mem_d = din("mem", [NS, NMEM, D])
    g_d = {n: din(n, [DEPTH, D]) for n in ("norm_mix_pre", "norm_mix_post", "norm_mem", "norm_ffn_pre", "norm_ffn_post")}
    sink_d = din("sink_a", [DEPTH, 8])
    w_in_d = din("w_in", [DEPTH, D, INW])
    w_mkv_d = din("w_mem_kv", [DEPTH, D, 1024])
    w_oa_d = din("w_o_a", [DEPTH, 512, D])
    w_ob_d = din("w_o_b", [DEPTH, 256, D])
    w_oc_d = din("w_o_c", [DEPTH, 512, D])
    w_out_d = din("w_out", [DEPTH, D, D])
    w_fi_d = din("w_ffn_in", [DEPTH, D, 2 * FH])
    w_fo_d = din("w_ffn_out", [DEPTH, FH, D])
    cb_d = din("c_bf", [128, 1024], BF16)
    rope_d = din("c_rope", [128, 2, S])
    y_d = nc.dram_tensor("y", [NS, S, D], F32, kind="ExternalOutput").ap()

    NPIECE = len(WIN_ORDER) + 2 + 2 + 1 + 1 + 2 + 11 + 6
    scr = nc.dram_tensor("wscr", [depth, NPIECE, 128, 4096], BF16, kind="Internal").ap()
    pidx = {}
    for n in WIN_ORDER + ["MK", "MV", "OA0", "OA1", "OB", "OC", "WO0", "WO1"] + ["FI%d" % i for i in range(11)] + ["FO%d" % i for i in range(6)]:
        pidx[n] = len(pidx)
    assert len(pidx) == NPIECE

    off = [512]

    def sb(name, shape, dt):
        nbytes = int(np.prod(shape[1:])) * (2 if dt == BF16 else 4)
        t = nc.alloc_sbuf_tensor_at(name, list(shape), dt, offset=off[0])
        off[0] += (nbytes + 31) // 32 * 32
        return t

    x_sb = sb("x_sb", [128, 16, D], F32)
    cb = sb("cb", [128, 1024], BF16)
    gA = sb("gA", [128, D], F32)
    gB = sb("gB", [128, D], F32)
    stats = sb("stats", [128, 64], F32)
    hT = sb("hT", [128, 8, 512], BF16)
    ring = [sb("ring%d" % i, [128, 4096], BF16) for i in range(3)]
    htok = sb("htok", [128, D], BF16)
    ytmp = sb("ytmp", [128, D], F32)
    junk = sb("junk", [128, D], BF16)
    phase0 = off[0]
    kT = sb("kT", [128, 7, S], BF16)
    v0 = sb("v0", [128, 16, 384], BF16)
    vb1 = sb("vb1", [128, 16, 256], BF16)
    vb2 = sb("vb2", [128, 16, 256], BF16)
    mkT = sb("mkT", [128, 4, 256], BF16)
    mv = sb("mv", [128, 2, 512], BF16)
    rope = sb("rope", [128, 2, 512], F32)
    qreg = off[0]
    qT = sb("qT", [128, 8, 512], BF16)
    OT = sb("OT", [128, 8, 512], BF16)
    merged = sb("merged", [128, 8, 512], F32)
    sg = [sb("sg%d" % i, [128, 512], F32) for i in range(2)]
    pT = [sb("pT%d" % i, [128, 512], BF16) for i in range(3)]
    rqc = sb("rqc", [128, 512], F32)
    rqs = sb("rqs", [128, 512], BF16)
    rec = sb("rec", [128, 512], F32)
    mix_end = off[0]
    hTf = nc.alloc_sbuf_tensor_at("hTf", [128, 8, S], BF16, offset=qreg)
    mbf = nc.alloc_sbuf_tensor_at("mbf", [128, 8, 512], BF16, offset=qreg)
    memT = nc.alloc_sbuf_tensor_at("memT", [128, 8, 256], BF16, offset=qreg + 32768)
    off[0] = phase0
    wfo = sb("wfo", [128, 22, D], BF16)
    actT = sb("actT", [128, 22, 512], BF16)
    sgf = [sb("sgf%d" % i, [128, 512], F32) for i in range(2)]
    off[0] = phase0
    stg = [sb("stg%d" % i, [128, 4096], F32) for i in range(2)]
    stb = [sb("stb%d" % i, [128, 4096], BF16) for i in range(2)]
    assert mix_end < nc.sbuf_top, (mix_end, nc.sbuf_top)

    psm = nc.alloc_psum_tensor("psm", [128, 4096], F32)
    pst = psm[:, 3584:4096].bitcast(BF16)
    MMB = (0, 1, 2, 3)

    def bank(b, n=512):
        return psm[:, 512 * b:512 * b + n]

    ident = cb[:, 0:128]
    perm = cb[:, 128:256]
    ones = cb[:, 256:384]
    maskA = cb[:, 384:768]
    maskB = cb[:, 768:1024]
    eps_c = stats[:, 63:64]

    E = {"pe": "tensor", "act": "scalar", "dve": "vector", "pool": "gpsimd", "sp": "sync"}

    def mm(out, lhsT, rhs, start, stop, reads, writes, signal):
        p.op("pe", lambda e: e.matmul(out, lhsT, rhs, start=start, stop=stop), reads, writes, signal)

    def act(out, in_, func, reads, writes, **kw):
        p.op("act", lambda e: e.activation(out, in_, func, **kw), reads, writes)

    def tt(out, in0, in1, op, reads, writes, eng="dve"):
        p.op(eng, lambda e: e.tensor_tensor(out, in0, in1, op), reads, writes)

    rr = [0]

    def load_piece(l, name, parts=128):
        s = rr[0] % 3
        rr[0] += 1
        src = scr[l, pidx[name]]
        dst = ring[s]
        p.dma("ring%d" % s, [lambda e: e.dma_start(out=dst[0:parts, :], in_=src[0:parts, :])],
              reads=["scr"], writes=["ring%d" % s])
        return dst, "ring%d" % s

    mmrot = [0]

    def mmbank(banks):
        b = banks[mmrot[0] % len(banks)]
        mmrot[0] += 1
        return b

    pl = [0]
    cast_engs = ("act", "dve", "pool")

    def convert(l, name, loads, parts=128, width=4096):
        i = pl[0] % 2
        eng = cast_engs[pl[0] % 3]
        pl[0] += 1
        st_, sbt = stg[i], stb[i]
        fns = []
        for dv, src in loads:
            d_ = dv(st_)
            fns.append(lambda e, d_=d_, src=src: e.dma_start(out=d_, in_=src))
        p.dma("plin%d" % i, fns, reads=[], writes=["stg%d" % i])
        if eng == "act":
            p.op("act", lambda e: e.activation(sbt[0:parts, 0:width], st_[0:parts, 0:width], AF.Copy),
                 ["stg%d" % i], ["stb%d" % i])
        else:
            p.op(eng, lambda e: e.tensor_copy(sbt[0:parts, 0:width], st_[0:parts, 0:width]),
                 ["stg%d" % i], ["stb%d" % i])
        dst = scr[l, pidx[name]]
        p.dma("plout%d" % i, [lambda e: e.dma_start(out=dst[0:parts, 0:width], in_=sbt[0:parts, 0:width])],
              reads=["stb%d" % i], writes=["scr"])

    def prologue():
        for l in range(depth):
            for name in WIN_ORDER:
                segs = WIN_P[name]
                tot = sum(n for _, n in segs)
                loads = []
                o = 0
                for (c0, n) in segs:
                    src = w_in_d[l, :, c0:c0 + n].rearrange("(kc p) c -> p kc c", p=128)
                    loads.append((lambda t, o=o, n=n, tot=tot: t[:, 0:8 * tot].rearrange("p (kc c) -> p kc c", kc=8)[:, :, o:o + n], src))
                    o += n
                convert(l, name, loads, width=8 * tot)
            for name, c0 in (("MK", 0), ("MV", 512)):
                src = w_mkv_d[l, :, c0:c0 + 512].rearrange("(kc p) c -> p kc c", p=128)
                convert(l, name, [(lambda t: t[:, :].rearrange("p (kc c) -> p kc c", kc=8), src)])
            for hf in range(2):
                src = w_oa_d[l, :, 512 * hf:512 * hf + 512].rearrange("(h d) c -> d h c", d=64)
                convert(l, "OA%d" % hf, [(lambda t: t[0:64, :].rearrange("p (h c) -> p h c", h=8), src)], parts=64)
            src = w_ob_d[l].rearrange("(h d) c -> d h c", d=64)
            convert(l, "OB", [(lambda t: t[0:64, :].rearrange("p (h c) -> p h c", h=4), src)], parts=64)
            src = w_oc_d[l].rearrange("(h d) c -> d h c", d=128)
            convert(l, "OC", [(lambda t: t[:, :].rearrange("p (h c) -> p h c", h=4), src)])
            for hf in range(2):
                src = w_out_d[l, :, 512 * hf:512 * hf + 512].rearrange("(kc p) c -> p kc c", p=128)
                convert(l, "WO%d" % hf, [(lambda t: t[:, :].rearrange("p (kc c) -> p kc c", kc=8), src)])
            for i in range(11):
                loads = []
                for j in range(2):
                    c = 2 * i + j
                    for k, base in enumerate((0, FH)):
                        src = w_fi_d[l, :, base + 128 * c:base + 128 * c + 128].rearrange("(kc p) c -> p kc c", p=128)
                        o = 256 * j + 128 * k
                        loads.append((lambda t, o=o: t[:, :].rearrange("p (kc c) -> p kc c", kc=8)[:, :, o:o + 128], src))
                convert(l, "FI%d" % i, loads)
            for i in range(6):
                nch = 4 if i < 5 else 2
                src = w_fo_d[l, 512 * i:512 * i + 128 * nch, :].rearrange("(c p) n -> p c n", p=128)
                convert(l, "FO%d" % i, [(lambda t, nch=nch: t[:, 0:1024 * nch].rearrange("p (c n) -> p c n", c=nch), src)],
                        width=1024 * nch)

    def load_g(buf, bname, which, l):
        src = g_d[which][l].partition_broadcast(128)
        p.dma("gld_" + bname, [lambda e: e.dma_start(out=buf[:, :], in_=src)], reads=[], writes=[bname])

    def rstd_from(ss_ap, n, reads_w):
        act(ss_ap, ss_ap, AF.Sqrt, [reads_w], [reads_w], bias=eps_c, scale=1.0 / D)
        p.op("dve", lambda e: e.reciprocal(ss_ap, ss_ap), [reads_w], [reads_w])

    def norm_tok(src_fn, nst, gbuf, gname, dst_fn, dst_res, src_res_fn):
        for st in range(nst):
            xin = src_fn(st)
            act(junk[:, :], xin, AF.Square, [src_res_fn(st)], ["junk", "ss%d" % st], accum_out=stats[:, st:st + 1])
        for st in range(nst):
            rstd_from(stats[:, st:st + 1], 1, "ss%d" % st)
        for st in range(nst):
            xin = src_fn(st)
            p.op("dve", lambda e, xin=xin, st=st: e.scalar_tensor_tensor(htok[:, :], xin, stats[:, st:st + 1], gbuf[:, :], ALU.mult, ALU.mult),
                 [src_res_fn(st), "ss%d" % st, gname], ["htok"])
            for c in range(8):
                p.op("pe", lambda e, c=c: e.transpose(pst[:, 128 * c:128 * c + 128], htok[:, 128 * c:128 * c + 128], ident),
                     ["htok", "cb"], ["ps7"], signal=(c == 7))
            d_ = dst_fn(st)
            act(d_, pst[:, :].rearrange("p (c t) -> p c t", c=8), AF.Copy, ["ps7"], [dst_res(st)])

    def proj_fm(piece, pres, ncols_tot, c0, rhs_fn, rhs_res, n, banks):
        b = mmbank(banks)
        pv = piece[:, 0:8 * ncols_tot].rearrange("p (kc c) -> p kc c", kc=8)
        for kc in range(8):
            mm(bank(b, n), pv[:, kc, c0:c0 + 128], rhs_fn(kc), kc == 0, kc == 7, [pres] + rhs_res, ["ps%d" % b], kc == 7)
        return b

    def rotary_evac(b, dst, dst_res, n=512):
        tt(rqc[:, 0:n], bank(b, n), rope[:, 0, 0:n], ALU.mult, ["ps%d" % b, "rope"], ["rqc"])
        tt(rqs[:, 0:n], bank(b, n), rope[:, 1, 0:n], ALU.mult, ["ps%d" % b, "rope"], ["rqs"])

        def stage2():
            b2 = mmbank(MMB)
            mm(bank(b2, n), perm, rqs[:, 0:n], True, True, ["rqs", "cb"], ["ps%d" % b2], True)
            tt(dst, bank(b2, n), rqc[:, 0:n], ALU.add, ["ps%d" % b2, "rqc"], [dst_res])
        return stage2

    def rot_chunks(items):
        pend = None
        for pj, dst, dres in items:
            b = pj()
            if pend:
                pend()
            pend = rotary_evac(b, dst, dres)
        if pend:
            pend()

    ptrot = [0]
    odrot = [0]

    def od_pair():
        pr = ((4, 5), (6, 7))[odrot[0] % 2]
        odrot[0] += 1
        return pr

    def emit_stream(tiles, LA=2):
        nt = len(tiles)
        slots = []
        for i in range(nt + LA):
            if i < nt:
                t = tiles[i]
                n = t["n"]
                sbk = mmbank(MMB)
                pi = ptrot[0] % 3
                ptrot[0] += 1
                slots.append(pi)
                so = bank(sbk, n)
                if t.get("h4"):
                    so = so.rearrange("p (h q) -> p h q", h=4)
                mm(so, t["k"], t["q"], True, t["mask"] is None, ["kv", "qT"], ["ps%d" % sbk], t["mask"] is None)
                if t["mask"] is not None:
                    mm(so, ident, t["mask"], False, True, ["cb"], ["ps%d" % sbk], True)
                act(pT[pi][:, 0:n], bank(sbk, n), AF.Exp, ["ps%d" % sbk], ["pT%d" % pi], scale=t["scale"])
            if i >= LA:
                t = tiles[i - LA]
                n = t["n"]
                pi = slots[i - LA]
                ob, db = t["ob"], t["db"]
                M = t["v"].shape[-1]
                mm(psm[0:M, 512 * ob:512 * ob + 512][:, t["cols"]], t["v"], pT[pi][:, 0:n], t["first"], t["last"],
                   ["pT%d" % pi, "kv"], ["ps%d" % ob], False)
                mm(psm[0:M, 512 * db:512 * db + 512][:, t["cols"]], ones[:, 0:M], pT[pi][:, 0:n], t["first"], t["last"],
                   ["pT%d" % pi, "cb"], ["ps%d" % db], t["last"])
                if t["last"] and t.get("post"):
                    t["post"]()

    def band_tiles(tt_i, R, dil, L, mask, kT_fn, qT_fn, v_fn):
        t0 = 512 * tt_i
        out = []
        for c in range(dil):
            mlo, mhi = t0 // dil, (t0 + 512) // dil
            for kt in range(L // 128):
                lo = max(128 * kt - R, 0, mlo)
                hi = min(128 * kt + 128 + R, L, mhi)
                if hi <= lo:
                    continue
                n = hi - lo
                qi0 = lo - (128 * kt - R)
                loc = lo * dil + c - t0
                if dil == 1:
                    cols = slice(loc, loc + n)
                else:
                    cols = slice(loc, loc + (n - 1) * dil + 1, dil)
                out.append(dict(k=kT_fn(c, kt), q=qT_fn(cols), n=n, mask=mask[:, qi0:qi0 + n], scale=0.125, v=v_fn(c, kt), cols=cols))
        return out

    def layer(sq, l):
        p.phase = 'kv_norm'
        p.barrier()
        load_g(gA, "gA", "norm_mix_pre", l)
        load_g(gB, "gB", "norm_mem", l)
        src = sink_d[l].partition_broadcast(128)
        p.dma("gld_sink", [lambda e: e.dma_start(out=stats[:, 48:56], in_=src)], reads=[], writes=["sink"])
        act(stats[:, 48:56], stats[:, 48:56], AF.Exp, ["sink"], ["sink"])
        for t4 in range(4):
            norm_tok(lambda st: x_sb[:, 4 * t4 + st, :], 4, gA, "gA",
                     lambda st: hTf[:, :, 512 * t4 + 128 * st:512 * t4 + 128 * st + 128],
                     lambda st: "hTf", lambda st: "x%d" % (4 * t4 + st))
        p.phase = 'kv_K'
        for pn, nchunk, cbase in (("KV0", 4, 0), ("KV1", 3, 4)):
            piece, pres = load_piece(l, pn)
            for t4 in range(4):
                p.dma("ropeld", [lambda e, t4=t4: e.dma_start(out=rope[:, :, :], in_=rope_d[:, :, 512 * t4:512 * t4 + 512])],
                      reads=[], writes=["rope"])
                rot_chunks([((lambda ch=ch, t4=t4: proj_fm(piece, pres, 128 * nchunk, 128 * ch,
                                                          lambda kc: hTf[:, kc, 512 * t4:512 * t4 + 512], ["hTf"], 512, MMB)),
                             kT[:, cbase + ch, 512 * t4:512 * t4 + 512], "kv") for ch in range(nchunk)])
        p.phase = 'kv_V'
        piece, pres = load_piece(l, "V0")
        pv = piece[:, 0:8 * 384].rearrange("p (kc c) -> p kc c", kc=8)
        for g in range(16):
            b = mmbank(MMB)
            for kc in range(8):
                mm(bank(b, 384), hTf[:, kc, 128 * g:128 * g + 128], pv[:, kc, :], kc == 0, kc == 7, [pres, "hTf"], ["ps%d" % b], kc == 7)
            act(v0[:, g, :], bank(b, 384), AF.Copy, ["ps%d" % b], ["kv"])
        piece, pres = load_piece(l, "V1")
        pv = piece[:, :].rearrange("p (kc c) -> p kc c", kc=8)
        for c in range(4):
            for kt in range(4):
                b = mmbank(MMB)
                for kc in range(8):
                    mm(bank(b, 256), hTf[:, kc, 512 * kt + c:512 * kt + 512:4], pv[:, kc, 0:256], kc == 0, kc == 7, [pres, "hTf"], ["ps%d" % b], kc == 7)
                p.op("dve", lambda e, b=b, c=c, kt=kt: e.tensor_copy(vb1[:, 4 * c + kt, :], bank(b, 256)), ["ps%d" % b], ["kv"])
        for c in range(16):
            b = mmbank(MMB)
            for kc in range(8):
                mm(bank(b, 256), hTf[:, kc, c:S:16], pv[:, kc, 256:512], kc == 0, kc == 7, [pres, "hTf"], ["ps%d" % b], kc == 7)
            p.op("dve", lambda e, b=b, c=c: e.tensor_copy(vb2[:, c, :], bank(b, 256)), ["ps%d" % b], ["kv"])
        p.phase = 'kv_mem'
        for mt in range(2):
            p.dma("memld", [lambda e, mt=mt: e.dma_start(out=ytmp[:, :], in_=mem_d[sq, 128 * mt:128 * mt + 128, :])], reads=[], writes=["ytmp"])
            norm_tok(lambda st: ytmp[:, :], 1, gB, "gB", lambda st, mt=mt: memT[:, :, 128 * mt:128 * mt + 128], lambda st: "memT", lambda st: "ytmp")
        piece, pres = load_piece(l, "MK")
        for h in range(4):
            b = proj_fm(piece, pres, 512, 128 * h, lambda kc: memT[:, kc, :], ["memT"], 256, MMB)
            act(mkT[:, h, :], bank(b, 256), AF.Copy, ["ps%d" % b], ["kv"])
        piece, pres = load_piece(l, "MV")
        pv = piece[:, :].rearrange("p (kc c) -> p kc c", kc=8)
        for mt in range(2):
            b = mmbank(MMB)
            for kc in range(8):
                mm(bank(b), memT[:, kc, 128 * mt:128 * mt + 128], pv[:, kc, :], kc == 0, kc == 7, [pres, "memT"], ["ps%d" % b], kc == 7)
            act(mv[:, mt, :], bank(b), AF.Copy, ["ps%d" % b], ["kv"])
        p.barrier()
        load_g(gB, "gB", "norm_mix_post", l)

        for t4 in range(4):
            p.phase = 'q_norm'
            norm_tok(lambda st: x_sb[:, 4 * t4 + st, :], 4, gA, "gA",
                     lambda st: hT[:, :, 128 * st:128 * st + 128], lambda st: "hT", lambda st: "x%d" % (4 * t4 + st))
            p.dma("ropeld", [lambda e, t4=t4: e.dma_start(out=rope[:, :, :], in_=rope_d[:, :, 512 * t4:512 * t4 + 512])],
                  reads=[], writes=["rope"])
            hrhs = lambda kc: hT[:, kc, :]

            def outproj(branch, nh, kdim, wfn, first_b, last_b):
                for hf in range(2):
                    gp, gres = load_piece(l, "G%s%d" % (branch, hf))
                    wpc, wres, wv = wfn(hf)
                    for o4 in range(4):
                        oc = 4 * hf + o4
                        bg = proj_fm(gp, gres, 512, 128 * o4, hrhs, ["hT"], 512, MMB)
                        sgt = sg[oc % 2]
                        sgn = "sg%d" % (oc % 2)
                        act(sgt[:, :], bank(bg), AF.Sigmoid, ["ps%d" % bg], [sgn])
                        bo = mmbank(MMB)
                        for h in range(nh):
                            mm(bank(bo), wv(h, o4), OT[0:kdim, h, :], h == 0, h == nh - 1, [wres, "OT"], ["ps%d" % bo], h == nh - 1)
                        if first_b:
                            tt(merged[:, oc, :], bank(bo), sgt[:, :], ALU.mult, ["ps%d" % bo, sgn], ["merged"])
                        else:
                            tt(rec[:, :], bank(bo), sgt[:, :], ALU.mult, ["ps%d" % bo, sgn], ["rec"])
                            if last_b:
                                tt(mbf[:, oc, :], merged[:, oc, :], rec[:, :], ALU.add, ["merged", "rec"], ["qT"])
                            else:
                                tt(merged[:, oc, :], merged[:, oc, :], rec[:, :], ALU.add, ["merged", "rec"], ["merged"])

            p.phase = 'qA_proj'
            piece, pres = load_piece(l, "QA")
            rot_chunks([((lambda ch=ch: proj_fm(piece, pres, 512, 128 * ch, hrhs, ["hT"], 512, MMB)), qT[:, ch, :], "qT")
                        for ch in range(4)])
            p.phase = 'qA_attn'
            tiles = []
            for g2 in range(2):
                pb = 64 * g2
                for qbl in range(4):
                    qb = 4 * t4 + qbl
                    ob, db = od_pair()
                    kts = [kt for kt in (qb - 1, qb, qb + 1) if 0 <= kt < 16]

                    def post(g2=g2, qbl=qbl, ob=ob, db=db):
                        for jj in range(4):
                            h = 4 * g2 + jj
                            act(rec[0:64, 128 * jj:128 * jj + 128], bank(db)[0:64, 128 * jj:128 * jj + 128], AF.Ln,
                                ["ps%d" % db, "sink"], ["rec"], bias=stats[0:64, 48 + h:49 + h])
                        act(rec[0:64, :], rec[0:64, :], AF.Exp, ["rec"], ["rec"], scale=-1.0)
                        tt(OT[0:64, 4 * g2:4 * g2 + 4, 128 * qbl:128 * qbl + 128],
                           bank(ob)[0:64, :].rearrange("p (h q) -> p h q", h=4),
                           rec[0:64, :].rearrange("p (h q) -> p h q", h=4), ALU.mult, ["ps%d" % ob, "rec"], ["OT"])
                    for i, kt in enumerate(kts):
                        if kt == qb:
                            mk = None
                        else:
                            a = 256 if kt < qb else 0
                            mk = maskA[:, a:a + 128].unsqueeze(1).broadcast_to([128, 4, 128])
                        tiles.append(dict(k=kT[pb:pb + 64, 0, 128 * kt:128 * kt + 128],
                                          q=qT[pb:pb + 64, 0:4, 128 * qbl:128 * qbl + 128], n=512, mask=mk, scale=0.125,
                                          v=v0[:, kt, 64 * g2:64 * g2 + 64], cols=slice(0, 512), ob=ob, db=db,
                                          first=(i == 0), last=(i == len(kts) - 1), post=post, h4=True))
            emit_stream(tiles)

            def wfa(hf):
                wpc, wres = load_piece(l, "OA%d" % hf, parts=64)
                v = wpc[0:64, :].rearrange("p (h c) -> p h c", h=8)
                return wpc, wres, (lambda h, o4: v[:, h, 128 * o4:128 * o4 + 128])
            p.phase = 'qA_out'
            outproj("A", 8, 64, wfa, True, False)

            p.phase = 'qB_proj'
            for pn, nchunk, cbase in (("QB0", 4, 0), ("QB1", 2, 4)):
                piece, pres = load_piece(l, pn)
                rot_chunks([((lambda ch=ch, piece=piece, pres=pres, nchunk=nchunk: proj_fm(piece, pres, 128 * nchunk, 128 * ch, hrhs, ["hT"], 512, MMB)),
                             qT[:, cbase + ch, :], "qT") for ch in range(nchunk)])
            p.phase = 'qB_attn'
            stream = []
            for j in range(4):
                tiles = []
                for gi, (dil, L) in enumerate(((1, 2048), (4, 512), (16, 128))):
                    hb = 4 * gi + j
                    chq, pb = hb // 2, 64 * (hb % 2)
                    if gi == 0:
                        vfn = lambda c, kt, j=j: v0[:, kt, 128 + 64 * j:128 + 64 * j + 64]
                    elif gi == 1:
                        vfn = lambda c, kt, j=j: vb1[:, 4 * c + kt, 64 * j:64 * j + 64]
                    else:
                        vfn = lambda c, kt, j=j: vb2[:, c, 64 * j:64 * j + 64]

                    def kfn(c, kt, chq=chq, pb=pb, dil=dil):
                        s0 = 128 * kt * dil + c
                        if dil == 1:
                            return kT[pb:pb + 64, 1 + chq, s0:s0 + 128]
                        return kT[pb:pb + 64, 1 + chq, s0:s0 + 127 * dil + 1:dil]
                    tiles += band_tiles(t4, 64, dil, L, maskB, kfn,
                                        lambda cols, chq=chq, pb=pb: qT[pb:pb + 64, chq, cols], vfn)
                ob, db = od_pair()

                def post(j=j, ob=ob, db=db):
                    act(rec[0:64, :], bank(db)[0:64, :], AF.Ln, ["ps%d" % db], ["rec"])
                    act(rec[0:64, :], rec[0:64, :], AF.Exp, ["rec"], ["rec"], scale=-1.0)
                    tt(OT[0:64, j, :], bank(ob)[0:64, :], rec[0:64, :], ALU.mult, ["ps%d" % ob, "rec"], ["OT"])
                for i, t in enumerate(tiles):
                    t.update(ob=ob, db=db, first=(i == 0), last=(i == len(tiles) - 1), post=post)
                stream += tiles
            emit_stream(stream)

            def wfb(hf):
                if hf == 0:
                    wfb.pc = load_piece(l, "OB", parts=64)
                wpc, wres = wfb.pc
                v = wpc[0:64, :].rearrange("p (h c) -> p h c", h=4)
                return wpc, wres, (lambda h, o4: v[:, h, 512 * hf + 128 * o4:512 * hf + 128 * o4 + 128])
            p.phase = 'qB_out'
            outproj("B", 4, 64, wfb, False, False)

            p.phase = 'qC_proj'
            piece, pres = load_piece(l, "QC")
            for ch in range(4):
                b = proj_fm(piece, pres, 512, 128 * ch, hrhs, ["hT"], 512, MMB)
                act(qT[:, ch, :], bank(b), AF.Copy, ["ps%d" % b], ["qT"])
            p.phase = 'qC_attn'
            stream = []
            for h in range(4):
                ob, db = od_pair()

                def post(h=h, ob=ob, db=db):
                    act(rec[:, :], bank(db), AF.Ln, ["ps%d" % db], ["rec"])
                    act(rec[:, :], rec[:, :], AF.Exp, ["rec"], ["rec"], scale=-1.0)
                    tt(OT[:, h, :], bank(ob), rec[:, :], ALU.mult, ["ps%d" % ob, "rec"], ["OT"])
                for mt in range(2):
                    stream.append(dict(k=mkT[:, h, 128 * mt:128 * mt + 128], q=qT[:, h, :], n=512, mask=None,
                                       scale=float(128 ** -0.5), v=mv[:, mt, 128 * h:128 * h + 128], cols=slice(0, 512),
                                       ob=ob, db=db, first=(mt == 0), last=(mt == 1), post=post))
            emit_stream(stream)

            def wfc(hf):
                if hf == 0:
                    wfc.pc = load_piece(l, "OC")
                wpc, wres = wfc.pc
                v = wpc[:, :].rearrange("p (h c) -> p h c", h=4)
                return wpc, wres, (lambda h, o4: v[:, h, 512 * hf + 128 * o4:512 * hf + 128 * o4 + 128])
            p.phase = 'qC_out'
            outproj("C", 4, 128, wfc, False, True)

            p.phase = 'q_wout'
            wo = [load_piece(l, "WO%d" % hf) for hf in range(2)]
            for st in range(4):
                g = 4 * t4 + st
                yb = (0, 2)[st % 2]
                for hf in range(2):
                    wv = wo[hf][0][:, :].rearrange("p (kc c) -> p kc c", kc=8)
                    for kc in range(8):
                        mm(bank(yb + hf), mbf[:, kc, 128 * st:128 * st + 128], wv[:, kc, :], kc == 0, kc == 7, [wo[hf][1], "qT"], ["ps%d" % (yb + hf)], kc == 7)
                post_norm(psm[:, 512 * yb:512 * yb + 1024], ["ps%d" % yb, "ps%d" % (yb + 1)], g)

    def post_norm(yps, yres, g):
        act(junk[:, :], yps, AF.Square, yres, ["junk", "ssp"], accum_out=stats[:, 32:33])
        rstd_from(stats[:, 32:33], 1, "ssp")
        p.op("dve", lambda e: e.scalar_tensor_tensor(ytmp[:, :], yps, stats[:, 32:33], gB[:, :], ALU.mult, ALU.mult),
             yres + ["ssp", "gB"], ["ytmp"])
        tt(x_sb[:, g, :], x_sb[:, g, :], ytmp[:, :], ALU.add, ["x%d" % g, "ytmp"], ["x%d" % g], eng="pool")

    def ffn(sq, l):
        p.barrier()
        load_g(gA, "gA", "norm_ffn_pre", l)
        load_g(gB, "gB", "norm_ffn_post", l)
        for i in range(6):
            nch = 4 if i < 5 else 2
            src = scr[l, pidx["FO%d" % i]]
            p.dma("wfold", [lambda e, i=i, nch=nch, src=src: e.dma_start(
                out=wfo[:, 4 * i:4 * i + nch, :], in_=src[:, 0:1024 * nch].rearrange("p (c n) -> p c n", c=nch))],
                reads=["scr"], writes=["wfo"])
        for t4 in range(4):
            p.phase = 'f_norm'
            norm_tok(lambda st: x_sb[:, 4 * t4 + st, :], 4, gA, "gA",
                     lambda st: hT[:, :, 128 * st:128 * st + 128], lambda st: "hT", lambda st: "x%d" % (4 * t4 + st))
            hrhs = lambda kc: hT[:, kc, :]
            pr = 0
            p.phase = 'f_gu'
            for i in range(11):
                piece, pres = load_piece(l, "FI%d" % i)
                for j in range(2):
                    c = 2 * i + j
                    banks = ((0, 1), (2, 3), (4, 5))[pr % 3]
                    pr += 1
                    bg = proj_fm(piece, pres, 512, 256 * j, hrhs, ["hT"], 512, (banks[0],))
                    bu = proj_fm(piece, pres, 512, 256 * j + 128, hrhs, ["hT"], 512, (banks[1],))
                    sf = sgf[c % 2]
                    sfn = "sgf%d" % (c % 2)
                    act(sf[:, :], bank(bg), AF.Silu, ["ps%d" % bg], [sfn])
                    tt(actT[:, c, :], bank(bu), sf[:, :], ALU.mult, ["ps%d" % bu, sfn], ["actT"])
            p.phase = 'f_y2'
            for st in range(4):
                g = 4 * t4 + st
                pb_ = (0, 2, 4)[st % 3]
                for hf in range(2):
                    for c in range(22):
                        mm(bank(pb_ + hf), actT[:, c, 128 * st:128 * st + 128], wfo[:, c, 512 * hf:512 * hf + 512],
                           c == 0, c == 21, ["actT", "wfo"], ["ps%d" % (pb_ + hf)], c == 21)
                post_norm(psm[:, 512 * pb_:512 * pb_ + 1024], ["ps%d" % pb_, "ps%d" % (pb_ + 1)], g)

    p.dma("cst", [lambda e: e.dma_start(out=cb[:, :], in_=cb_d)], reads=[], writes=["cb"])
    p.op("dve", lambda e: e.memset(stats[:, :], 0.0), [], ["stats0"])
    p.op("dve", lambda e: e.memset(eps_c, EPS), ["stats0"], ["eps"])
    prologue()
    p.barrier()
    for sq in range(NS):
        p.epoch = sq
        p.dma("xld", [lambda e, sq=sq, q4=q4: e.dma_start(out=x_sb[:, 4 * q4:4 * q4 + 4, :],
                                                      in_=x_d[sq, 512 * q4:512 * q4 + 512, :].rearrange("(g p) d -> p g d", p=128)) for q4 in range(4)],
              reads=[], writes=["x%d" % g for g in range(16)])
        for l in range(depth):
            layer(sq, l)
            ffn(sq, l)
        p.dma("xst", [lambda e, sq=sq, q4=q4: e.dma_start(out=y_d[sq, 512 * q4:512 * q4 + 512, :].rearrange("(g p) d -> p g d", p=128),
                                                      in_=x_sb[:, 4 * q4:4 * q4 + 4, :]) for q4 in range(4)],
              reads=["x%d" % g for g in range(16)], writes=[])
    p.barrier()

    sems = {}
    for s in p.count:
        sems[s] = nc.alloc_semaphore(s)

    def replay(eng, e):
        for waits, fn, inc in p.q[eng]:
            for s, v in waits:
                e.wait_ge(sems[s], v)
            if fn is None:
                continue
            ins = fn(e)
            if inc is not None:
                ins.then_inc(sems[inc[0]], inc[1])

    with nc.Block() as block:
        @block.tensor
        def _(e):
            replay("pe", e)

        @block.scalar
        def _(e):
            replay("act", e)

        @block.vector
        def _(e):
            replay("dve", e)

        @block.gpsimd
        def _(e):
            replay("pool", e)

        @block.sync
        def _(e):
            replay("sp", e)
    return nc, p


_CACHE = {}


def _run(x_all, mem_all, weights, depth=DEPTH):
    NS = x_all.shape[0] // NCORES
    key = (NS, depth)
    if key not in _CACHE:
        _CACHE[key] = build(NS, depth)[0]
    nc = _CACHE[key]
    cbc, ropec = _consts()
    in_maps = []
    for c in range(NCORES):
        m = {"x": np.ascontiguousarray(x_all[NS * c:NS * (c + 1)]),
             "mem": np.ascontiguousarray(mem_all[NS * c:NS * (c + 1)]),
             "c_bf": cbc, "c_rope": ropec}
        m.update(weights)
        in_maps.append(m)
    res = run_bass_kernel_spmd(nc, in_maps, core_ids=list(range(NCORES)))
    return np.concatenate([np.asarray(r["y"]) for r in res.results], axis=0)


def kernel(x_prompt, x_sample, mem_prompt, mem_sample, norm_mix_pre, norm_mix_post, norm_mem,
           w_in, sink_a, w_mem_kv, w_o_a, w_o_b, w_o_c, w_out, norm_ffn_pre, norm_ffn_post,
           w_ffn_in, w_ffn_out):
    f = lambda a: np.ascontiguousarray(np.asarray(a, dtype=np.float32))
    x_all = np.concatenate([f(x_prompt), f(x_sample)], axis=0)
    mem_all = np.concatenate([f(mem_prompt), f(mem_sample)], axis=0)
    weights = dict(norm_mix_pre=f(norm_mix_pre), norm_mix_post=f(norm_mix_post), norm_mem=f(norm_mem),
                   w_in=f(w_in), sink_a=f(sink_a), w_mem_kv=f(w_mem_kv), w_o_a=f(w_o_a), w_o_b=f(w_o_b),
                   w_o_c=f(w_o_c), w_out=f(w_out), norm_ffn_pre=f(norm_ffn_pre), norm_ffn_post=f(norm_ffn_post),
                   w_ffn_in=f(w_ffn_in), w_ffn_out=f(w_ffn_out))
    y = _run(x_all, mem_all, weights)
    nb = x_prompt.shape[0]
    return (np.ascontiguousarray(y[:nb]), np.ascontiguousarray(y[nb:]))
```

```python
import numpy as np
import ml_dtypes
import concourse.bass as bass
import concourse.mybir as mybir
from concourse.bass_utils import run_bass_kernel_spmd

F32 = mybir.dt.float32
BF16 = mybir.dt.bfloat16
AF = mybir.ActivationFunctionType
ALU = mybir.AluOpType

D = 1024
S = 2048
NMEM = 256
DEPTH = 4
NCORES = 8
INW = 6656
FH = 2816
EPS = 1e-6
NEG = -30000.0
ENGS = ("pe", "act", "dve", "pool", "sp")


class Prog:
    def __init__(self):
        self.q = {e: [] for e in ENGS}
        self.count = {}
        self.seen = {e: {} for e in ENGS}
        self.res = {}
        self.epoch = 0
        self.phase = ''
        self.pe_phase = []

    def esem(self, eng):
        return "%s_%d" % (eng, self.epoch)

    def _deps(self, eng, reads, writes):
        deps = {}

        def add(s, v):
            if deps.get(s, 0) < v:
                deps[s] = v

        for r in reads:
            st = self.res.get(r)
            if st and st[0]:
                add(*st[0])
        for w in writes:
            st = self.res.get(w)
            if st:
                if st[0]:
                    add(*st[0])
                for s, v in st[1].items():
                    add(s, v)
        waits = []
        for s, v in deps.items():
            if eng == "pe" and s.startswith("pe_"):
                continue
            if self.seen[eng].get(s, 0) >= v:
                continue
            self.seen[eng][s] = v
            waits.append((s, v))
        return waits

    def _commit(self, ev, reads, writes):
        for r in reads:
            st = self.res.setdefault(r, [None, {}])
            if st[1].get(ev[0], 0) < ev[1]:
                st[1][ev[0]] = ev[1]
        for w in writes:
            self.res[w] = [ev, {}]

    def op(self, eng, fn, reads=(), writes=(), signal=True):
        waits = self._deps(eng, reads, writes)
        s = self.esem(eng)
        ev = (s, self.count.get(s, 0) + 1)
        if signal:
            self.count[s] = ev[1]
        self._commit(ev, reads, writes)
        if eng == 'pe':
            self.pe_phase.append(self.phase)
        self.q[eng].append((waits, fn, (s, 1) if signal else None))

    def dma(self, sem, fns, reads=(), writes=(), eng="sp"):
        waits = self._deps(eng, reads, writes)
        tgt = self.count.get(sem, 0) + 16 * len(fns)
        self.count[sem] = tgt
        self._commit((sem, tgt), reads, writes)
        for i, fn in enumerate(fns):
            self.q[eng].append((waits if i == 0 else [], fn, (sem, 16)))

    def barrier(self):
        for e in ENGS:
            waits = []
            for s, v in self.count.items():
                if v > 0 and self.seen[e].get(s, 0) < v:
                    if e == "pe" and s.startswith("pe_"):
                        continue
                    self.seen[e][s] = v
                    waits.append((s, v))
            if waits:
                self.q[e].append((waits, None, None))


def _consts():
    bf = ml_dtypes.bfloat16
    ident = np.eye(128, dtype=np.float32)
    perm = np.zeros((128, 128), np.float32)
    for k in range(128):
        d = k % 64
        if d < 8:
            perm[k, k + 8] = 1.0
        elif d < 16:
            perm[k, k - 8] = 1.0
    ones = np.ones((128, 128), np.float32)
    ki = np.arange(128)[:, None]
    qa = np.arange(384)[None, :]
    maskA = np.where((qa >= ki) & (qa <= ki + 256), 0.0, NEG).astype(np.float32)
    qb = np.arange(256)[None, :]
    maskB = np.where((qb >= ki) & (qb <= ki + 128), 0.0, NEG).astype(np.float32)
    cb = np.concatenate([ident, perm, ones, maskA, maskB], axis=1).astype(bf)
    inv_freq = (np.float32(500000.0) ** (-np.arange(0, 16, 2, dtype=np.float32) / np.float32(16))).astype(np.float32)
    ang = (np.arange(S, dtype=np.float32)[:, None] * inv_freq[None, :]).astype(np.float32)
    cos = np.cos(ang).astype(np.float32).T
    sin = np.sin(ang).astype(np.float32).T
    C = np.ones((128, S), np.float32)
    S2 = np.zeros((128, S), np.float32)
    for k in range(128):
        d = k % 64
        if d < 8:
            C[k] = cos[d]
            S2[k] = sin[d]
        elif d < 16:
            C[k] = cos[d - 8]
            S2[k] = -sin[d - 8]
    rope = np.stack([C, S2], axis=1)
    return cb, np.ascontiguousarray(rope)


def _win_pieces():
    P = {}
    P["KV0"] = [(512, 128), (1536, 384)]
    P["KV1"] = [(1920, 384)]
    P["V0"] = [(640, 128), (2304, 256)]
    P["V1"] = [(2560, 512)]
    segs = []
    for j in range(4):
        segs.append((64 * j, 64))
        segs.append((64 * (4 + j), 64))
    P["QA"] = segs
    P["QB0"] = [(768, 512)]
    P["QB1"] = [(1280, 256)]
    P["QC"] = [(3072, 512)]
    for i, b in enumerate("ABC"):
        P["G%s0" % b] = [(3584 + 1024 * i, 512)]
        P["G%s1" % b] = [(3584 + 1024 * i + 512, 512)]
    return P


WIN_P = _win_pieces()
WIN_ORDER = list(WIN_P.keys())


def build(NS, depth):
    nc = bass.Bass("TRN2", target_bir_lowering=False, dynamic_dma_scratch_size=256)
    p = Prog()

    def din(name, shape, dt=F32):
        return nc.dram_tensor(name, list(shape), dt, kind="ExternalInput").ap()

    x_d = din("x", [NS, S, D])
    mem_d = din("mem", [NS, NMEM, D])
    g_d = {n: din(n, [DEPTH, D]) for n in ("norm_mix_pre", "norm_mix_post", "norm_mem", "norm_ffn_pre", "norm_ffn_post")}
    sink_d = din("sink_a", [DEPTH, 8])
    w_in_d = din("w_in", [DEPTH, D, INW])
    w_mkv_d = din("w_mem_kv", [DEPTH, D, 1024])
    w_oa_d = din("w_o_a", [DEPTH, 512, D])
    w_ob_d = din("w_o_b", [DEPTH, 256, D])
    w_oc_d = din("w_o_c", [DEPTH, 512, D])
    w_out_d = din("w_out", [DEPTH, D, D])
    w_fi_d = din("w_ffn_in", [DEPTH, D, 2 * FH])
    w_fo_d = din("w_ffn_out", [DEPTH, FH, D])
    cb_d = din("c_bf", [128, 1024], BF16)
    rope_d = din("c_rope", [128, 2, S])
    y_d = nc.dram_tensor("y", [NS, S, D], F32, kind="ExternalOutput").ap()

    NPIECE = len(WIN_ORDER) + 2 + 2 + 1 + 1 + 2 + 11 + 6
    scr = nc.dram_tensor("wscr", [depth, NPIECE, 128, 4096], BF16, kind="Internal").ap()
    pidx = {}
    for n in WIN_ORDER + ["MK", "MV", "OA0", "OA1", "OB", "OC", "WO0", "WO1"] + ["FI%d" % i for i in range(11)] + ["FO%d" % i for i in range(6)]:
        pidx[n] = len(pidx)
    assert len(pidx) == NPIECE

    off = [512]

    def sb(name, shape, dt):
        nbytes = int(np.prod(shape[1:])) * (2 if dt == BF16 else 4)
        t = nc.alloc_sbuf_tensor_at(name, list(shape), dt, offset=off[0])
        off[0] += (nbytes + 31) // 32 * 32
        return t

    x_sb = sb("x_sb", [128, 16, D], F32)
    cb = sb("cb", [128, 1024], BF16)
    gA = sb("gA", [128, D], F32)
    gB = sb("gB", [128, D], F32)
    stats = sb("stats", [128, 64], F32)
    hT = sb("hT", [128, 8, 512], BF16)
    ring = [sb("ring%d" % i, [128, 4096], BF16) for i in range(3)]
    htok = sb("htok", [128, D], BF16)
    ytmp = sb("ytmp", [128, D], F32)
    junk = sb("junk", [128, D], BF16)
    phase0 = off[0]
    kT = sb("kT", [128, 7, S], BF16)
    v0 = sb("v0", [128, 16, 384], BF16)
    vb1 = sb("vb1", [128, 16, 256], BF16)
    vb2 = sb("vb2", [128, 16, 256], BF16)
    mkT = sb("mkT", [128, 4, 256], BF16)
    mv = sb("mv", [128, 2, 512], BF16)
    rope = sb("rope", [128, 2, 512], F32)
    qreg = off[0]
    qT = sb("qT", [128, 8, 512], BF16)
    OT = sb("OT", [128, 8, 512], BF16)
    merged = sb("merged", [128, 8, 512], F32)
    sg = [sb("sg%d" % i, [128, 512], F32) for i in range(2)]
    pT = [sb("pT%d" % i, [128, 512], BF16) for i in range(4)]
    rqc = sb("rqc", [128, 512], F32)
    rqs = sb("rqs", [128, 512], BF16)
    rec = sb("rec", [128, 512], F32)
    mix_end = off[0]
    hTf = nc.alloc_sbuf_tensor_at("hTf", [128, 8, S], BF16, offset=qreg)
    mbf = nc.alloc_sbuf_tensor_at("mbf", [128, 8, 512], BF16, offset=qreg)
    memT = nc.alloc_sbuf_tensor_at("memT", [128, 8, 256], BF16, offset=qreg + 32768)
    off[0] = phase0
    wfo = sb("wfo", [128, 22, D], BF16)
    actT = sb("actT", [128, 22, 512], BF16)
    sgf = [sb("sgf%d" % i, [128, 512], F32) for i in range(2)]
    hTb = sb("hTb", [128, 8, 512], BF16)
    off[0] = phase0
    stg = [sb("stg%d" % i, [128, 4096], F32) for i in range(4)]
    stb = [sb("stb%d" % i, [128, 4096], BF16) for i in range(4)]
    assert mix_end < nc.sbuf_top, (mix_end, nc.sbuf_top)

    psm = nc.alloc_psum_tensor("psm", [128, 4096], F32)
    pst = psm[:, 3584:4096].bitcast(BF16)
    MMB = (0, 1, 2, 3)

    def bank(b, n=512):
        return psm[:, 512 * b:512 * b + n]

    ident = cb[:, 0:128]
    perm = cb[:, 128:256]
    ones = cb[:, 256:384]
    maskA = cb[:, 384:768]
    maskB = cb[:, 768:1024]
    eps_c = stats[:, 63:64]

    E = {"pe": "tensor", "act": "scalar", "dve": "vector", "pool": "gpsimd", "sp": "sync"}

    def mm(out, lhsT, rhs, start, stop, reads, writes, signal):
        p.op("pe", lambda e: e.matmul(out, lhsT, rhs, start=start, stop=stop), reads, writes, signal)

    def act(out, in_, func, reads, writes, **kw):
        p.op("act", lambda e: e.activation(out, in_, func, **kw), reads, writes)

    def tt(out, in0, in1, op, reads, writes, eng="dve"):
        p.op(eng, lambda e: e.tensor_tensor(out, in0, in1, op), reads, writes)

    rr = [0]

    def load_piece(l, name, parts=128):
        s = rr[0] % 3
        rr[0] += 1
        src = scr[l, pidx[name]]
        dst = ring[s]
        p.dma("ring%d" % s, [lambda e: e.dma_start(out=dst[0:parts, :], in_=src[0:parts, :])],
              reads=["scr"], writes=["ring%d" % s])
        return dst, "ring%d" % s

    mmrot = [0]

    def mmbank(banks):
        b = banks[mmrot[0] % len(banks)]
        mmrot[0] += 1
        return b

    pl = [0]
    cast_engs = ("act", "dve")

    def convert(l, name, loads, parts=128, width=4096):
        i = pl[0] % 4
        eng = cast_engs[pl[0] % 2]
        pl[0] += 1
        st_, sbt = stg[i], stb[i]
        fns = []
        for dv, src in loads:
            d_ = dv(st_)
            fns.append(lambda e, d_=d_, src=src: e.dma_start(out=d_, in_=src))
        p.dma("plin%d" % i, fns, reads=[], writes=["stg%d" % i])
        if eng == "act":
            p.op("act", lambda e: e.activation(sbt[0:parts, 0:width], st_[0:parts, 0:width], AF.Copy),
                 ["stg%d" % i], ["stb%d" % i])
        else:
            p.op(eng, lambda e: e.tensor_copy(sbt[0:parts, 0:width], st_[0:parts, 0:width]),
                 ["stg%d" % i], ["stb%d" % i])
        dst = scr[l, pidx[name]]
        p.dma("plout%d" % i, [lambda e: e.dma_start(out=dst[0:parts, 0:width], in_=sbt[0:parts, 0:width])],
              reads=["stb%d" % i], writes=["scr"])

    def prologue():
        for l in range(depth):
            for name in WIN_ORDER:
                segs = WIN_P[name]
                tot = sum(n for _, n in segs)
                loads = []
                o = 0
                for (c0, n) in segs:
                    src = w_in_d[l, :, c0:c0 + n].rearrange("(kc p) c -> p kc c", p=128)
                    loads.append((lambda t, o=o, n=n, tot=tot: t[:, 0:8 * tot].rearrange("p (kc c) -> p kc c", kc=8)[:, :, o:o + n], src))
                    o += n
                convert(l, name, loads, width=8 * tot)
            for name, c0 in (("MK", 0), ("MV", 512)):
                src = w_mkv_d[l, :, c0:c0 + 512].rearrange("(kc p) c -> p kc c", p=128)
                convert(l, name, [(lambda t: t[:, :].rearrange("p (kc c) -> p kc c", kc=8), src)])
            for hf in range(2):
                src = w_oa_d[l, :, 512 * hf:512 * hf + 512].rearrange("(h d) c -> d h c", d=64)
                convert(l, "OA%d" % hf, [(lambda t: t[0:64, :].rearrange("p (h c) -> p h c", h=8), src)], parts=64)
            src = w_ob_d[l].rearrange("(h d) c -> d h c", d=64)
            convert(l, "OB", [(lambda t: t[0:64, :].rearrange("p (h c) -> p h c", h=4), src)], parts=64)
            src = w_oc_d[l].rearrange("(h d) c -> d h c", d=128)
            convert(l, "OC", [(lambda t: t[:, :].rearrange("p (h c) -> p h c", h=4), src)])
            for hf in range(2):
                src = w_out_d[l, :, 512 * hf:512 * hf + 512].rearrange("(kc p) c -> p kc c", p=128)
                convert(l, "WO%d" % hf, [(lambda t: t[:, :].rearrange("p (kc c) -> p kc c", kc=8), src)])
            for i in range(11):
                loads = []
                for j in range(2):
                    c = 2 * i + j
                    for k, base in enumerate((0, FH)):
                        src = w_fi_d[l, :, base + 128 * c:base + 128 * c + 128].rearrange("(kc p) c -> p kc c", p=128)
                        o = 256 * j + 128 * k
                        loads.append((lambda t, o=o: t[:, :].rearrange("p (kc c) -> p kc c", kc=8)[:, :, o:o + 128], src))
                convert(l, "FI%d" % i, loads)
            for i in range(6):
                nch = 4 if i < 5 else 2
                src = w_fo_d[l, 512 * i:512 * i + 128 * nch, :].rearrange("(c p) n -> p c n", p=128)
                convert(l, "FO%d" % i, [(lambda t, nch=nch: t[:, 0:1024 * nch].rearrange("p (c n) -> p c n", c=nch), src)],
                        width=1024 * nch)

    def load_g(buf, bname, which, l):
        src = g_d[which][l].partition_broadcast(128)
        p.dma("gld_" + bname, [lambda e: e.dma_start(out=buf[:, :], in_=src)], reads=[], writes=[bname])

    def rstd_from(ss_ap, n, reads_w):
        act(ss_ap, ss_ap, AF.Sqrt, [reads_w], [reads_w], bias=eps_c, scale=1.0 / D)
        p.op("dve", lambda e: e.reciprocal(ss_ap, ss_ap), [reads_w], [reads_w])

    def norm_tok(src_fn, nst, gbuf, gname, dst_fn, dst_res, src_res_fn):
        for st in range(nst):
            xin = src_fn(st)
            act(junk[:, :], xin, AF.Square, [src_res_fn(st)], ["junk", "ss%d" % st], accum_out=stats[:, st:st + 1])
        for st in range(nst):
            rstd_from(stats[:, st:st + 1], 1, "ss%d" % st)
        for st in range(nst):
            xin = src_fn(st)
            p.op("dve", lambda e, xin=xin, st=st: e.scalar_tensor_tensor(htok[:, :], xin, stats[:, st:st + 1], gbuf[:, :], ALU.mult, ALU.mult),
                 [src_res_fn(st), "ss%d" % st, gname], ["htok"])
            for c in range(8):
                p.op("pe", lambda e, c=c: e.transpose(pst[:, 128 * c:128 * c + 128], htok[:, 128 * c:128 * c + 128], ident),
                     ["htok", "cb"], ["ps7"], signal=(c == 7))
            d_ = dst_fn(st)
            act(d_, pst[:, :].rearrange("p (c t) -> p c t", c=8), AF.Copy, ["ps7"], [dst_res(st)])

    def proj_fm(piece, pres, ncols_tot, c0, rhs_fn, rhs_res, n, banks):
        b = mmbank(banks)
        pv = piece[:, 0:8 * ncols_tot].rearrange("p (kc c) -> p kc c", kc=8)
        for kc in range(8):
            mm(bank(b, n), pv[:, kc, c0:c0 + 128], rhs_fn(kc), kc == 0, kc == 7, [pres] + rhs_res, ["ps%d" % b], kc == 7)
        return b

    def rotary_evac(b, dst, dst_res, n=512):
        tt(rqc[:, 0:n], bank(b, n), rope[:, 0, 0:n], ALU.mult, ["ps%d" % b, "rope"], ["rqc"])
        tt(rqs[:, 0:n], bank(b, n), rope[:, 1, 0:n], ALU.mult, ["ps%d" % b, "rope"], ["rqs"])

        def stage2():
            b2 = mmbank(MMB)
            mm(bank(b2, n), perm, rqs[:, 0:n], True, True, ["rqs", "cb"], ["ps%d" % b2], True)
            tt(dst, bank(b2, n), rqc[:, 0:n], ALU.add, ["ps%d" % b2, "rqc"], [dst_res])
        return stage2

    def rot_chunks(items):
        pend = None
        for pj, dst, dres in items:
            b = pj()
            if pend:
                pend()
            pend = rotary_evac(b, dst, dres)
        if pend:
            pend()

    ptrot = [0]
    odrot = [0]

    def od_pair():
        pr = ((4, 5), (6, 7))[odrot[0] % 2]
        odrot[0] += 1
        return pr

    def emit_stream(tiles, GB=2):
        nt = len(tiles)
        nb = (nt + GB - 1) // GB
        slots = {}
        for b in range(nb + 1):
            if b < nb:
                cur = list(range(GB * b, min(GB * b + GB, nt)))
                outs = {}
                for i in cur:
                    t = tiles[i]
                    sbk = mmbank(MMB)
                    pi = ptrot[0] % 4
                    ptrot[0] += 1
                    slots[i] = pi
                    so = bank(sbk, t["n"])
                    if t.get("h4"):
                        so = so.rearrange("p (h q) -> p h q", h=4)
                    outs[i] = (sbk, so)
                    mm(so, t["k"], t["q"], True, t["mask"] is None, ["kv", "qT"], ["ps%d" % sbk], t["mask"] is None)
                for i in cur:
                    t = tiles[i]
                    if t["mask"] is not None:
                        sbk, so = outs[i]
                        mm(so, ident, t["mask"], False, True, ["cb"], ["ps%d" % sbk], True)
                for i in cur:
                    t = tiles[i]
                    sbk, so = outs[i]
                    act(pT[slots[i]][:, 0:t["n"]], bank(sbk, t["n"]), AF.Exp, ["ps%d" % sbk], ["pT%d" % slots[i]], scale=t["scale"])
            if b >= 1:
                prev = list(range(GB * (b - 1), min(GB * b, nt)))
                for i in prev:
                    t = tiles[i]
                    pi = slots[i]
                    M = t["v"].shape[-1]
                    mm(psm[0:M, 512 * t["ob"]:512 * t["ob"] + 512][:, t["cols"]], t["v"], pT[pi][:, 0:t["n"]], t["first"], t["last"],
                       ["pT%d" % pi, "kv"], ["ps%d" % t["ob"]], False)
                for k, i in enumerate(prev):
                    t = tiles[i]
                    pi = slots[i]
                    M = t["v"].shape[-1]
                    mm(psm[0:M, 512 * t["db"]:512 * t["db"] + 512][:, t["cols"]], ones[:, 0:M], pT[pi][:, 0:t["n"]], t["first"], t["last"],
                       ["pT%d" % pi, "cb"], ["ps%d" % t["db"]], t["last"] or k == len(prev) - 1)
                for i in prev:
                    t = tiles[i]
                    if t["last"] and t.get("post"):
                        t["post"]()

    def band_tiles(tt_i, R, dil, L, mask, kT_fn, qT_fn, v_fn):
        t0 = 512 * tt_i
        out = []
        for c in range(dil):
            mlo, mhi = t0 // dil, (t0 + 512) // dil
            for kt in range(L // 128):
                lo = max(128 * kt - R, 0, mlo)
                hi = min(128 * kt + 128 + R, L, mhi)
                if hi <= lo:
                    continue
                n = hi - lo
                qi0 = lo - (128 * kt - R)
                loc = lo * dil + c - t0
                if dil == 1:
                    cols = slice(loc, loc + n)
                else:
                    cols = slice(loc, loc + (n - 1) * dil + 1, dil)
                out.append(dict(k=kT_fn(c, kt), q=qT_fn(cols), n=n, mask=mask[:, qi0:qi0 + n], scale=0.125, v=v_fn(c, kt), cols=cols))
        return out

    def layer(sq, l):
        p.phase = 'kv_norm'
        p.barrier()
        load_g(gA, "gA", "norm_mix_pre", l)
        load_g(gB, "gB", "norm_mem", l)
        src = sink_d[l].partition_broadcast(128)
        p.dma("gld_sink", [lambda e: e.dma_start(out=stats[:, 48:56], in_=src)], reads=[], writes=["sink"])
        act(stats[:, 48:56], stats[:, 48:56], AF.Exp, ["sink"], ["sink"])
        for t4 in range(4):
            norm_tok(lambda st: x_sb[:, 4 * t4 + st, :], 4, gA, "gA",
                     lambda st: hTf[:, :, 512 * t4 + 128 * st:512 * t4 + 128 * st + 128],
                     lambda st: "hTf", lambda st: "x%d" % (4 * t4 + st))
        p.phase = 'kv_K'
        for pn, nchunk, cbase in (("KV0", 4, 0), ("KV1", 3, 4)):
            piece, pres = load_piece(l, pn)
            for t4 in range(4):
                p.dma("ropeld", [lambda e, t4=t4: e.dma_start(out=rope[:, :, :], in_=rope_d[:, :, 512 * t4:512 * t4 + 512])],
                      reads=[], writes=["rope"])
                rot_chunks([((lambda ch=ch, t4=t4: proj_fm(piece, pres, 128 * nchunk, 128 * ch,
                                                          lambda kc: hTf[:, kc, 512 * t4:512 * t4 + 512], ["hTf"], 512, MMB)),
                             kT[:, cbase + ch, 512 * t4:512 * t4 + 512], "kv") for ch in range(nchunk)])
        p.phase = 'kv_V'
        piece, pres = load_piece(l, "V0")
        pv = piece[:, 0:8 * 384].rearrange("p (kc c) -> p kc c", kc=8)
        for g in range(16):
            b = mmbank(MMB)
            for kc in range(8):
                mm(bank(b, 384), hTf[:, kc, 128 * g:128 * g + 128], pv[:, kc, :], kc == 0, kc == 7, [pres, "hTf"], ["ps%d" % b], kc == 7)
            act(v0[:, g, :], bank(b, 384), AF.Copy, ["ps%d" % b], ["kv"])
        piece, pres = load_piece(l, "V1")
        pv = piece[:, :].rearrange("p (kc c) -> p kc c", kc=8)
        for c in range(4):
            for kt in range(4):
                b = mmbank(MMB)
                for kc in range(8):
                    mm(bank(b, 256), hTf[:, kc, 512 * kt + c:512 * kt + 512:4], pv[:, kc, 0:256], kc == 0, kc == 7, [pres, "hTf"], ["ps%d" % b], kc == 7)
                p.op("dve", lambda e, b=b, c=c, kt=kt: e.tensor_copy(vb1[:, 4 * c + kt, :], bank(b, 256)), ["ps%d" % b], ["kv"])
        for c in range(16):
            b = mmbank(MMB)
            for kc in range(8):
                mm(bank(b, 256), hTf[:, kc, c:S:16], pv[:, kc, 256:512], kc == 0, kc == 7, [pres, "hTf"], ["ps%d" % b], kc == 7)
            p.op("dve", lambda e, b=b, c=c: e.tensor_copy(vb2[:, c, :], bank(b, 256)), ["ps%d" % b], ["kv"])
        p.phase = 'kv_mem'
        for mt in range(2):
            p.dma("memld", [lambda e, mt=mt: e.dma_start(out=ytmp[:, :], in_=mem_d[sq, 128 * mt:128 * mt + 128, :])], reads=[], writes=["ytmp"])
            norm_tok(lambda st: ytmp[:, :], 1, gB, "gB", lambda st, mt=mt: memT[:, :, 128 * mt:128 * mt + 128], lambda st: "memT", lambda st: "ytmp")
        piece, pres = load_piece(l, "MK")
        for h in range(4):
            b = proj_fm(piece, pres, 512, 128 * h, lambda kc: memT[:, kc, :], ["memT"], 256, MMB)
            act(mkT[:, h, :], bank(b, 256), AF.Copy, ["ps%d" % b], ["kv"])
        piece, pres = load_piece(l, "MV")
        pv = piece[:, :].rearrange("p (kc c) -> p kc c", kc=8)
        for mt in range(2):
            b = mmbank(MMB)
            for kc in range(8):
                mm(bank(b), memT[:, kc, 128 * mt:128 * mt + 128], pv[:, kc, :], kc == 0, kc == 7, [pres, "memT"], ["ps%d" % b], kc == 7)
            act(mv[:, mt, :], bank(b), AF.Copy, ["ps%d" % b], ["kv"])
        p.barrier()
        load_g(gB, "gB", "norm_mix_post", l)

        def q_norm(t4):
            p.phase = 'q_norm'
            norm_tok(lambda st: x_sb[:, 4 * t4 + st, :], 4, gA, "gA",
                     lambda st: hT[:, :, 128 * st:128 * st + 128], lambda st: "hT", lambda st: "x%d" % (4 * t4 + st))
        q_norm(0)
        for t4 in range(4):
            p.dma("ropeld", [lambda e, t4=t4: e.dma_start(out=rope[:, :, :], in_=rope_d[:, :, 512 * t4:512 * t4 + 512])],
                  reads=[], writes=["rope"])
            hrhs = lambda kc: hT[:, kc, :]

            def outproj(branch, nh, kdim, wfn, first_b, last_b):
                for hf in range(2):
                    gp, gres = load_piece(l, "G%s%d" % (branch, hf))
                    wpc, wres, wv = wfn(hf)
                    for o4 in range(4):
                        oc = 4 * hf + o4
                        bg = proj_fm(gp, gres, 512, 128 * o4, hrhs, ["hT"], 512, MMB)
                        sgt = sg[oc % 2]
                        sgn = "sg%d" % (oc % 2)
                        act(sgt[:, :], bank(bg), AF.Sigmoid, ["ps%d" % bg], [sgn])
                        bo = mmbank(MMB)
                        for h in range(nh):
                            mm(bank(bo), wv(h, o4), OT[0:kdim, h, :], h == 0, h == nh - 1, [wres, "OT"], ["ps%d" % bo], h == nh - 1)
                        if first_b:
                            tt(merged[:, oc, :], bank(bo), sgt[:, :], ALU.mult, ["ps%d" % bo, sgn], ["merged"])
                        else:
                            tt(rec[:, :], bank(bo), sgt[:, :], ALU.mult, ["ps%d" % bo, sgn], ["rec"])
                            if last_b:
                                tt(mbf[:, oc, :], merged[:, oc, :], rec[:, :], ALU.add, ["merged", "rec"], ["qT"])
                            else:
                                tt(merged[:, oc, :], merged[:, oc, :], rec[:, :], ALU.add, ["merged", "rec"], ["merged"])

            p.phase = 'qA_proj'
            piece, pres = load_piece(l, "QA")
            rot_chunks([((lambda ch=ch: proj_fm(piece, pres, 512, 128 * ch, hrhs, ["hT"], 512, MMB)), qT[:, ch, :], "qT")
                        for ch in range(4)])
            p.phase = 'qA_attn'
            tiles = []
            for g2 in range(2):
                pb = 64 * g2
                for qbl in range(4):
                    qb = 4 * t4 + qbl
                    ob, db = od_pair()
                    kts = [kt for kt in (qb - 1, qb, qb + 1) if 0 <= kt < 16]

                    def post(g2=g2, qbl=qbl, ob=ob, db=db):
                        for jj in range(4):
                            h = 4 * g2 + jj
                            act(rec[0:64, 128 * jj:128 * jj + 128], bank(db)[0:64, 128 * jj:128 * jj + 128], AF.Ln,
                                ["ps%d" % db, "sink"], ["rec"], bias=stats[0:64, 48 + h:49 + h])
                        act(rec[0:64, :], rec[0:64, :], AF.Exp, ["rec"], ["rec"], scale=-1.0)
                        tt(OT[0:64, 4 * g2:4 * g2 + 4, 128 * qbl:128 * qbl + 128],
                           bank(ob)[0:64, :].rearrange("p (h q) -> p h q", h=4),
                           rec[0:64, :].rearrange("p (h q) -> p h q", h=4), ALU.mult, ["ps%d" % ob, "rec"], ["OT"])
                    for i, kt in enumerate(kts):
                        if kt == qb:
                            mk = None
                        else:
                            a = 256 if kt < qb else 0
                            mk = maskA[:, a:a + 128].unsqueeze(1).broadcast_to([128, 4, 128])
                        tiles.append(dict(k=kT[pb:pb + 64, 0, 128 * kt:128 * kt + 128],
                                          q=qT[pb:pb + 64, 0:4, 128 * qbl:128 * qbl + 128], n=512, mask=mk, scale=0.125,
                                          v=v0[:, kt, 64 * g2:64 * g2 + 64], cols=slice(0, 512), ob=ob, db=db,
                                          first=(i == 0), last=(i == len(kts) - 1), post=post, h4=True))
            emit_stream(tiles)

            def wfa(hf):
                wpc, wres = load_piece(l, "OA%d" % hf, parts=64)
                v = wpc[0:64, :].rearrange("p (h c) -> p h c", h=8)
                return wpc, wres, (lambda h, o4: v[:, h, 128 * o4:128 * o4 + 128])
            p.phase = 'qA_out'
            outproj("A", 8, 64, wfa, True, False)

            p.phase = 'qB_proj'
            for pn, nchunk, cbase in (("QB0", 4, 0), ("QB1", 2, 4)):
                piece, pres = load_piece(l, pn)
                rot_chunks([((lambda ch=ch, piece=piece, pres=pres, nchunk=nchunk: proj_fm(piece, pres, 128 * nchunk, 128 * ch, hrhs, ["hT"], 512, MMB)),
                             qT[:, cbase + ch, :], "qT") for ch in range(nchunk)])
            p.phase = 'qB_attn'
            stream = []
            for j in range(4):
                tiles = []
                for gi, (dil, L) in enumerate(((1, 2048), (4, 512), (16, 128))):
                    hb = 4 * gi + j
                    chq, pb = hb // 2, 64 * (hb % 2)
                    if gi == 0:
                        vfn = lambda c, kt, j=j: v0[:, kt, 128 + 64 * j:128 + 64 * j + 64]
                    elif gi == 1:
                        vfn = lambda c, kt, j=j: vb1[:, 4 * c + kt, 64 * j:64 * j + 64]
                    else:
                        vfn = lambda c, kt, j=j: vb2[:, c, 64 * j:64 * j + 64]

                    def kfn(c, kt, chq=chq, pb=pb, dil=dil):
                        s0 = 128 * kt * dil + c
                        if dil == 1:
                            return kT[pb:pb + 64, 1 + chq, s0:s0 + 128]
                        return kT[pb:pb + 64, 1 + chq, s0:s0 + 127 * dil + 1:dil]
                    tiles += band_tiles(t4, 64, dil, L, maskB, kfn,
                                        lambda cols, chq=chq, pb=pb: qT[pb:pb + 64, chq, cols], vfn)
                ob, db = od_pair()

                def post(j=j, ob=ob, db=db):
                    act(rec[0:64, :], bank(db)[0:64, :], AF.Ln, ["ps%d" % db], ["rec"])
                    act(rec[0:64, :], rec[0:64, :], AF.Exp, ["rec"], ["rec"], scale=-1.0)
                    tt(OT[0:64, j, :], bank(ob)[0:64, :], rec[0:64, :], ALU.mult, ["ps%d" % ob, "rec"], ["OT"])
                for i, t in enumerate(tiles):
                    t.update(ob=ob, db=db, first=(i == 0), last=(i == len(tiles) - 1), post=post)
                stream += tiles
            emit_stream(stream)

            def wfb(hf):
                if hf == 0:
                    wfb.pc = load_piece(l, "OB", parts=64)
                wpc, wres = wfb.pc
                v = wpc[0:64, :].rearrange("p (h c) -> p h c", h=4)
                return wpc, wres, (lambda h, o4: v[:, h, 512 * hf + 128 * o4:512 * hf + 128 * o4 + 128])
            p.phase = 'qB_out'
            outproj("B", 4, 64, wfb, False, False)

            p.phase = 'qC_proj'
            piece, pres = load_piece(l, "QC")
            for ch in range(4):
                b = proj_fm(piece, pres, 512, 128 * ch, hrhs, ["hT"], 512, MMB)
                act(qT[:, ch, :], bank(b), AF.Copy, ["ps%d" % b], ["qT"])
            p.phase = 'qC_attn'
            stream = []
            for h in range(4):
                ob, db = od_pair()

                def post(h=h, ob=ob, db=db):
                    act(rec[:, :], bank(db), AF.Ln, ["ps%d" % db], ["rec"])
                    act(rec[:, :], rec[:, :], AF.Exp, ["rec"], ["rec"], scale=-1.0)
                    tt(OT[:, h, :], bank(ob), rec[:, :], ALU.mult, ["ps%d" % ob, "rec"], ["OT"])
                for mt in range(2):
                    stream.append(dict(k=mkT[:, h, 128 * mt:128 * mt + 128], q=qT[:, h, :], n=512, mask=None,
                                       scale=float(128 ** -0.5), v=mv[:, mt, 128 * h:128 * h + 128], cols=slice(0, 512),
                                       ob=ob, db=db, first=(mt == 0), last=(mt == 1), post=post))
            emit_stream(stream)

            def wfc(hf):
                if hf == 0:
                    wfc.pc = load_piece(l, "OC")
                wpc, wres = wfc.pc
                v = wpc[:, :].rearrange("p (h c) -> p h c", h=4)
                return wpc, wres, (lambda h, o4: v[:, h, 512 * hf + 128 * o4:512 * hf + 128 * o4 + 128])
            p.phase = 'qC_out'
            outproj("C", 4, 128, wfc, False, True)

            if t4 < 3:
                q_norm(t4 + 1)
            p.phase = 'q_wout'
            wo = [load_piece(l, "WO%d" % hf) for hf in range(2)]
            for st in range(4):
                g = 4 * t4 + st
                yb = (0, 2)[st % 2]
                for hf in range(2):
                    wv = wo[hf][0][:, :].rearrange("p (kc c) -> p kc c", kc=8)
                    for kc in range(8):
                        mm(bank(yb + hf), mbf[:, kc, 128 * st:128 * st + 128], wv[:, kc, :], kc == 0, kc == 7, [wo[hf][1], "qT"], ["ps%d" % (yb + hf)], kc == 7)
                post_norm(psm[:, 512 * yb:512 * yb + 1024], ["ps%d" % yb, "ps%d" % (yb + 1)], g)

    def post_norm(yps, yres, g):
        act(junk[:, :], yps, AF.Square, yres, ["junk", "ssp"], accum_out=stats[:, 32:33])
        rstd_from(stats[:, 32:33], 1, "ssp")
        p.op("dve", lambda e: e.scalar_tensor_tensor(ytmp[:, :], yps, stats[:, 32:33], gB[:, :], ALU.mult, ALU.mult),
             yres + ["ssp", "gB"], ["ytmp"])
        tt(x_sb[:, g, :], x_sb[:, g, :], ytmp[:, :], ALU.add, ["x%d" % g, "ytmp"], ["x%d" % g], eng="pool")

    def ffn(sq, l):
        p.barrier()
        load_g(gA, "gA", "norm_ffn_pre", l)
        load_g(gB, "gB", "norm_ffn_post", l)
        for i in range(6):
            nch = 4 if i < 5 else 2
            src = scr[l, pidx["FO%d" % i]]
            p.dma("wfold", [lambda e, i=i, nch=nch, src=src: e.dma_start(
                out=wfo[:, 4 * i:4 * i + nch, :], in_=src[:, 0:1024 * nch].rearrange("p (c n) -> p c n", c=nch))],
                reads=["scr"], writes=["wfo"])
        hbufs = ((hT, "hT"), (hTb, "hTb"))

        def f_norm(t4):
            p.phase = 'f_norm'
            hb, hn = hbufs[t4 % 2]
            norm_tok(lambda st: x_sb[:, 4 * t4 + st, :], 4, gA, "gA",
                     lambda st: hb[:, :, 128 * st:128 * st + 128], lambda st: hn, lambda st: "x%d" % (4 * t4 + st))
        f_norm(0)
        for t4 in range(4):
            hb, hn = hbufs[t4 % 2]
            hrhs = lambda kc: hb[:, kc, :]
            pr = 0
            p.phase = 'f_gu'
            for i in range(11):
                piece, pres = load_piece(l, "FI%d" % i)
                for j in range(2):
                    c = 2 * i + j
                    banks = ((0, 1), (2, 3), (4, 5))[pr % 3]
                    pr += 1
                    bg = proj_fm(piece, pres, 512, 256 * j, hrhs, [hn], 512, (banks[0],))
                    bu = proj_fm(piece, pres, 512, 256 * j + 128, hrhs, [hn], 512, (banks[1],))
                    sf = sgf[c % 2]
                    sfn = "sgf%d" % (c % 2)
                    act(sf[:, :], bank(bg), AF.Silu, ["ps%d" % bg], [sfn])
                    tt(actT[:, c, :], bank(bu), sf[:, :], ALU.mult, ["ps%d" % bu, sfn], ["actT"])
            if t4 < 3:
                f_norm(t4 + 1)
            p.phase = 'f_y2'
            for st in range(4):
                g = 4 * t4 + st
                pb_ = (0, 2, 4)[st % 3]
                for hf in range(2):
                    for c in range(22):
                        mm(bank(pb_ + hf), actT[:, c, 128 * st:128 * st + 128], wfo[:, c, 512 * hf:512 * hf + 512],
                           c == 0, c == 21, ["actT", "wfo"], ["ps%d" % (pb_ + hf)], c == 21)
                post_norm(psm[:, 512 * pb_:512 * pb_ + 1024], ["ps%d" % pb_, "ps%d" % (pb_ + 1)], g)

    p.dma("cst", [lambda e: e.dma_start(out=cb[:, :], in_=cb_d)], reads=[], writes=["cb"])
    p.op("dve", lambda e: e.memset(stats[:, :], 0.0), [], ["stats0"])
    p.op("dve", lambda e: e.memset(eps_c, EPS), ["stats0"], ["eps"])
    prologue()
    p.barrier()
    for sq in range(NS):
        p.epoch = sq
        p.dma("xld", [lambda e, sq=sq, q4=q4: e.dma_start(out=x_sb[:, 4 * q4:4 * q4 + 4, :],
                                                      in_=x_d[sq, 512 * q4:512 * q4 + 512, :].rearrange("(g p) d -> p g d", p=128)) for q4 in range(4)],
              reads=[], writes=["x%d" % g for g in range(16)])
        for l in range(depth):
            layer(sq, l)
            ffn(sq, l)
        p.dma("xst", [lambda e, sq=sq, q4=q4: e.dma_start(out=y_d[sq, 512 * q4:512 * q4 + 512, :].rearrange("(g p) d -> p g d", p=128),
                                                      in_=x_sb[:, 4 * q4:4 * q4 + 4, :]) for q4 in range(4)],
              reads=["x%d" % g for g in range(16)], writes=[])
    p.barrier()

    sems = {}
    for s in p.count:
        sems[s] = nc.alloc_semaphore(s)

    def replay(eng, e):
        for waits, fn, inc in p.q[eng]:
            for s, v in waits:
                e.wait_ge(sems[s], v)
            if fn is None:
                continue
            ins = fn(e)
            if inc is not None:
                ins.then_inc(sems[inc[0]], inc[1])

    with nc.Block() as block:
        @block.tensor
        def _(e):
            replay("pe", e)

        @block.scalar
        def _(e):
            replay("act", e)

        @block.vector
        def _(e):
            replay("dve", e)

        @block.gpsimd
        def _(e):
            replay("pool", e)

        @block.sync
        def _(e):
            replay("sp", e)
    return nc, p


_CACHE = {}


def _run(x_all, mem_all, weights, depth=DEPTH):
    NS = x_all.shape[0] // NCORES
    key = (NS, depth)
    if key not in _CACHE:
        _CACHE[key] = build(NS, depth)[0]
    nc = _CACHE[key]
    cbc, ropec = _consts()
    in_maps = []
    for c in range(NCORES):
        m = {"x": np.ascontiguousarray(x_all[NS * c:NS * (c + 1)]),
             "mem": np.ascontiguousarray(mem_all[NS * c:NS * (c + 1)]),
             "c_bf": cbc, "c_rope": ropec}
        m.update(weights)
        in_maps.append(m)
    res = run_bass_kernel_spmd(nc, in_maps, core_ids=list(range(NCORES)))
    return np.concatenate([np.asarray(r["y"]) for r in res.results], axis=0)


def kernel(x_prompt, x_sample, mem_prompt, mem_sample, norm_mix_pre, norm_mix_post, norm_mem,
           w_in, sink_a, w_mem_kv, w_o_a, w_o_b, w_o_c, w_out, norm_ffn_pre, norm_ffn_post,
           w_ffn_in, w_ffn_out):
    f = lambda a: np.ascontiguousarray(np.asarray(a, dtype=np.float32))
    x_all = np.concatenate([f(x_prompt), f(x_sample)], axis=0)
    mem_all = np.concatenate([f(mem_prompt), f(mem_sample)], axis=0)
    weights = dict(norm_mix_pre=f(norm_mix_pre), norm_mix_post=f(norm_mix_post), norm_mem=f(norm_mem),
                   w_in=f(w_in), sink_a=f(sink_a), w_mem_kv=f(w_mem_kv), w_o_a=f(w_o_a), w_o_b=f(w_o_b),
                   w_o_c=f(w_o_c), w_out=f(w_out), norm_ffn_pre=f(norm_ffn_pre), norm_ffn_post=f(norm_ffn_post),
                   w_ffn_in=f(w_ffn_in), w_ffn_out=f(w_ffn_out))
    y = _run(x_all, mem_all, weights)
    nb = x_prompt.shape[0]
    return (np.ascontiguousarray(y[:nb]), np.ascontiguousarray(y[nb:]))
```

```python
import numpy as np
import ml_dtypes
import concourse.bass as bass
import concourse.mybir as mybir
from concourse.bass_utils import run_bass_kernel_spmd

F32 = mybir.dt.float32
BF16 = mybir.dt.bfloat16
AF = mybir.ActivationFunctionType
ALU = mybir.AluOpType

D = 1024
S = 2048
NMEM = 256
DEPTH = 4
NCORES = 8
INW = 6656
FH = 2816
EPS = 1e-6
NEG = -30000.0
ENGS = ("pe", "act", "dve", "pool", "sp")


class Prog:
    def __init__(self):
        self.q = {e: [] for e in ENGS}
        self.count = {}
        self.seen = {e: {} for e in ENGS}
        self.res = {}
        self.epoch = 0
        self.phase = ''
        self.pe_phase = []

    def esem(self, eng):
        return "%s_%d" % (eng, self.epoch)

    def _deps(self, eng, reads, writes):
        deps = {}

        def add(s, v):
            if deps.get(s, 0) < v:
                deps[s] = v

        for r in reads:
            st = self.res.get(r)
            if st and st[0]:
                add(*st[0])
        for w in writes:
            st = self.res.get(w)
            if st:
                if st[0]:
                    add(*st[0])
                for s, v in st[1].items():
                    add(s, v)
        waits = []
        for s, v in deps.items():
            if eng == "pe" and s.startswith("pe_"):
                continue
            if self.seen[eng].get(s, 0) >= v:
                continue
            self.seen[eng][s] = v
            waits.append((s, v))
        return waits

    def _commit(self, ev, reads, writes):
        for r in reads:
            st = self.res.setdefault(r, [None, {}])
            if st[1].get(ev[0], 0) < ev[1]:
                st[1][ev[0]] = ev[1]
        for w in writes:
            self.res[w] = [ev, {}]

    def op(self, eng, fn, reads=(), writes=(), signal=True):
        waits = self._deps(eng, reads, writes)
        s = self.esem(eng)
        ev = (s, self.count.get(s, 0) + 1)
        if signal:
            self.count[s] = ev[1]
        self._commit(ev, reads, writes)
        if eng == 'pe':
            self.pe_phase.append(self.phase)
        self.q[eng].append((waits, fn, (s, 1) if signal else None))

    def dma(self, sem, fns, reads=(), writes=(), eng="sp"):
        waits = self._deps(eng, reads, writes)
        tgt = self.count.get(sem, 0) + 16 * len(fns)
        self.count[sem] = tgt
        self._commit((sem, tgt), reads, writes)
        for i, fn in enumerate(fns):
            self.q[eng].append((waits if i == 0 else [], fn, (sem, 16)))

    def barrier(self):
        for e in ENGS:
            waits = []
            for s, v in self.count.items():
                if v > 0 and self.seen[e].get(s, 0) < v:
                    if e == "pe" and s.startswith("pe_"):
                        continue
                    self.seen[e][s] = v
                    waits.append((s, v))
            if waits:
                self.q[e].append((waits, None, None))


def _consts():
    bf = ml_dtypes.bfloat16
    ident = np.eye(128, dtype=np.float32)
    perm = np.zeros((128, 128), np.float32)
    for k in range(128):
        d = k % 64
        if d < 8:
            perm[k, k + 8] = 1.0
        elif d < 16:
            perm[k, k - 8] = 1.0
    ones = np.ones((128, 128), np.float32)
    ki = np.arange(128)[:, None]
    qa = np.arange(384)[None, :]
    maskA = np.where((qa >= ki) & (qa <= ki + 256), 0.0, NEG).astype(np.float32)
    qb = np.arange(256)[None, :]
    maskB = np.where((qb >= ki) & (qb <= ki + 128), 0.0, NEG).astype(np.float32)
    cb = np.concatenate([ident, perm, ones, maskA, maskB], axis=1).astype(bf)
    inv_freq = (np.float32(500000.0) ** (-np.arange(0, 16, 2, dtype=np.float32) / np.float32(16))).astype(np.float32)
    ang = (np.arange(S, dtype=np.float32)[:, None] * inv_freq[None, :]).astype(np.float32)
    cos = np.cos(ang).astype(np.float32).T
    sin = np.sin(ang).astype(np.float32).T
    C = np.ones((128, S), np.float32)
    S2 = np.zeros((128, S), np.float32)
    for k in range(128):
        d = k % 64
        if d < 8:
            C[k] = cos[d]
            S2[k] = sin[d]
        elif d < 16:
            C[k] = cos[d - 8]
            S2[k] = -sin[d - 8]
    rope = np.stack([C, S2], axis=1)
    return cb, np.ascontiguousarray(rope)


def _win_pieces():
    P = {}
    P["KV0"] = [(512, 128), (1536, 384)]
    P["KV1"] = [(1920, 384)]
    P["V0"] = [(640, 128), (2304, 256)]
    P["V1"] = [(2560, 512)]
    segs = []
    for j in range(4):
        segs.append((64 * j, 64))
        segs.append((64 * (4 + j), 64))
    P["QA"] = segs
    P["QB0"] = [(768, 512)]
    P["QB1"] = [(1280, 256)]
    P["QC"] = [(3072, 512)]
    for i, b in enumerate("ABC"):
        P["G%s0" % b] = [(3584 + 1024 * i, 512)]
        P["G%s1" % b] = [(3584 + 1024 * i + 512, 512)]
    return P


WIN_P = _win_pieces()
WIN_ORDER = list(WIN_P.keys())


def build(NS, depth):
    nc = bass.Bass("TRN2", target_bir_lowering=False, dynamic_dma_scratch_size=256)
    p = Prog()

    def din(name, shape, dt=F32):
        return nc.dram_tensor(name, list(shape), dt, kind="ExternalInput").ap()

    x_d = din("x", [NS, S, D])
    mem_d = din("mem", [NS, NMEM, D])
    g_d = {n: din(n, [DEPTH, D]) for n in ("norm_mix_pre", "norm_mix_post", "norm_mem", "norm_ffn_pre", "norm_ffn_post")}
    sink_d = din("sink_a", [DEPTH, 8])
    w_in_d = din("w_in", [DEPTH, D, INW])
    w_mkv_d = din("w_mem_kv", [DEPTH, D, 1024])
    w_oa_d = din("w_o_a", [DEPTH, 512, D])
    w_ob_d = din("w_o_b", [DEPTH, 256, D])
    w_oc_d = din("w_o_c", [DEPTH, 512, D])
    w_out_d = din("w_out", [DEPTH, D, D])
    w_fi_d = din("w_ffn_in", [DEPTH, D, 2 * FH])
    w_fo_d = din("w_ffn_out", [DEPTH, FH, D])
    cb_d = din("c_bf", [128, 1024], BF16)
    rope_d = din("c_rope", [128, 2, S])
    y_d = nc.dram_tensor("y", [NS, S, D], F32, kind="ExternalOutput").ap()

    NPIECE = len(WIN_ORDER) + 2 + 2 + 1 + 1 + 2 + 11 + 6
    scr = nc.dram_tensor("wscr", [depth, NPIECE, 128, 4096], BF16, kind="Internal").ap()
    pidx = {}
    for n in WIN_ORDER + ["MK", "MV", "OA0", "OA1", "OB", "OC", "WO0", "WO1"] + ["FI%d" % i for i in range(11)] + ["FO%d" % i for i in range(6)]:
        pidx[n] = len(pidx)
    assert len(pidx) == NPIECE

    off = [512]

    def sb(name, shape, dt):
        nbytes = int(np.prod(shape[1:])) * (2 if dt == BF16 else 4)
        t = nc.alloc_sbuf_tensor_at(name, list(shape), dt, offset=off[0])
        off[0] += (nbytes + 31) // 32 * 32
        return t

    x_sb = sb("x_sb", [128, 16, D], F32)
    cb = sb("cb", [128, 1024], BF16)
    gA = sb("gA", [128, D], F32)
    gB = sb("gB", [128, D], F32)
    stats = sb("stats", [128, 64], F32)
    hT = sb("hT", [128, 8, 512], BF16)
    ring = [sb("ring%d" % i, [128, 4096], BF16) for i in range(3)]
    htok = sb("htok", [128, D], BF16)
    ytmp = sb("ytmp", [128, D], F32)
    junk = sb("junk", [128, D], BF16)
    phase0 = off[0]
    kT = sb("kT", [128, 7, S], BF16)
    v0 = sb("v0", [128, 16, 384], BF16)
    vb1 = sb("vb1", [128, 16, 256], BF16)
    vb2 = sb("vb2", [128, 16, 256], BF16)
    mkT = sb("mkT", [128, 4, 256], BF16)
    mv = sb("mv", [128, 2, 512], BF16)
    rope = sb("rope", [128, 2, 512], F32)
    qreg = off[0]
    qT = sb("qT", [128, 8, 512], BF16)
    OT = sb("OT", [128, 8, 512], BF16)
    merged = sb("merged", [128, 8, 512], F32)
    sg = [sb("sg%d" % i, [128, 512], F32) for i in range(2)]
    pT = [sb("pT%d" % i, [128, 512], BF16) for i in range(4)]
    rqc = sb("rqc", [128, 512], F32)
    rqs = sb("rqs", [128, 512], BF16)
    rec = sb("rec", [128, 512], F32)
    mix_end = off[0]
    hTf = nc.alloc_sbuf_tensor_at("hTf", [128, 8, S], BF16, offset=qreg)
    mbf = nc.alloc_sbuf_tensor_at("mbf", [128, 8, 512], BF16, offset=qreg)
    memT = nc.alloc_sbuf_tensor_at("memT", [128, 8, 256], BF16, offset=qreg + 32768)
    off[0] = phase0
    wfo = sb("wfo", [128, 22, D], BF16)
    actT = sb("actT", [128, 22, 512], BF16)
    sgf = [sb("sgf%d" % i, [128, 512], F32) for i in range(2)]
    hTb = sb("hTb", [128, 8, 512], BF16)
    off[0] = phase0
    stg = [sb("stg%d" % i, [128, 4096], F32) for i in range(4)]
    stb = [sb("stb%d" % i, [128, 4096], BF16) for i in range(4)]
    assert mix_end < nc.sbuf_top, (mix_end, nc.sbuf_top)

    psm = nc.alloc_psum_tensor("psm", [128, 4096], F32)
    pst = psm[:, 3584:4096].bitcast(BF16)
    MMB = (0, 1, 2, 3)

    def bank(b, n=512):
        return psm[:, 512 * b:512 * b + n]

    ident = cb[:, 0:128]
    perm = cb[:, 128:256]
    ones = cb[:, 256:384]
    maskA = cb[:, 384:768]
    maskB = cb[:, 768:1024]
    eps_c = stats[:, 63:64]

    E = {"pe": "tensor", "act": "scalar", "dve": "vector", "pool": "gpsimd", "sp": "sync"}

    def mm(out, lhsT, rhs, start, stop, reads, writes, signal):
        p.op("pe", lambda e: e.matmul(out, lhsT, rhs, start=start, stop=stop), reads, writes, signal)

    def act(out, in_, func, reads, writes, **kw):
        p.op("act", lambda e: e.activation(out, in_, func, **kw), reads, writes)

    def tt(out, in0, in1, op, reads, writes, eng="dve"):
        p.op(eng, lambda e: e.tensor_tensor(out, in0, in1, op), reads, writes)

    rr = [0]

    def load_piece(l, name, parts=128):
        s = rr[0] % 3
        rr[0] += 1
        src = scr[l, pidx[name]]
        dst = ring[s]
        p.dma("ring%d" % s, [lambda e: e.dma_start(out=dst[0:parts, :], in_=src[0:parts, :])],
              reads=["scr"], writes=["ring%d" % s])
        return dst, "ring%d" % s

    mmrot = [0]

    def mmbank(banks):
        b = banks[mmrot[0] % len(banks)]
        mmrot[0] += 1
        return b

    pl = [0]
    cast_engs = ("act", "dve")

    def convert(l, name, loads, parts=128, width=4096):
        i = pl[0] % 4
        eng = cast_engs[pl[0] % 2]
        pl[0] += 1
        st_, sbt = stg[i], stb[i]
        fns = []
        for dv, src in loads:
            d_ = dv(st_)
            fns.append(lambda e, d_=d_, src=src: e.dma_start(out=d_, in_=src))
        p.dma("plin%d" % i, fns, reads=[], writes=["stg%d" % i])
        if eng == "act":
            p.op("act", lambda e: e.activation(sbt[0:parts, 0:width], st_[0:parts, 0:width], AF.Copy),
                 ["stg%d" % i], ["stb%d" % i])
        else:
            p.op(eng, lambda e: e.tensor_copy(sbt[0:parts, 0:width], st_[0:parts, 0:width]),
                 ["stg%d" % i], ["stb%d" % i])
        dst = scr[l, pidx[name]]
        p.dma("plout%d" % i, [lambda e: e.dma_start(out=dst[0:parts, 0:width], in_=sbt[0:parts, 0:width])],
              reads=["stb%d" % i], writes=["scr"])

    def prologue():
        for l in range(depth):
            for name in WIN_ORDER:
                segs = WIN_P[name]
                tot = sum(n for _, n in segs)
                loads = []
                o = 0
                for (c0, n) in segs:
                    src = w_in_d[l, :, c0:c0 + n].rearrange("(kc p) c -> p kc c", p=128)
                    loads.append((lambda t, o=o, n=n, tot=tot: t[:, 0:8 * tot].rearrange("p (kc c) -> p kc c", kc=8)[:, :, o:o + n], src))
                    o += n
                convert(l, name, loads, width=8 * tot)
            for name, c0 in (("MK", 0), ("MV", 512)):
                src = w_mkv_d[l, :, c0:c0 + 512].rearrange("(kc p) c -> p kc c", p=128)
                convert(l, name, [(lambda t: t[:, :].rearrange("p (kc c) -> p kc c", kc=8), src)])
            for hf in range(2):
                src = w_oa_d[l, :, 512 * hf:512 * hf + 512].rearrange("(h d) c -> d h c", d=64)
                convert(l, "OA%d" % hf, [(lambda t: t[0:64, :].rearrange("p (h c) -> p h c", h=8), src)], parts=64)
            src = w_ob_d[l].rearrange("(h d) c -> d h c", d=64)
            convert(l, "OB", [(lambda t: t[0:64, :].rearrange("p (h c) -> p h c", h=4), src)], parts=64)
            src = w_oc_d[l].rearrange("(h d) c -> d h c", d=128)
            convert(l, "OC", [(lambda t: t[:, :].rearrange("p (h c) -> p h c", h=4), src)])
            for hf in range(2):
                src = w_out_d[l, :, 512 * hf:512 * hf + 512].rearrange("(kc p) c -> p kc c", p=128)
                convert(l, "WO%d" % hf, [(lambda t: t[:, :].rearrange("p (kc c) -> p kc c", kc=8), src)])
            for i in range(11):
                loads = []
                for j in range(2):
                    c = 2 * i + j
                    for k, base in enumerate((0, FH)):
                        src = w_fi_d[l, :, base + 128 * c:base + 128 * c + 128].rearrange("(kc p) c -> p kc c", p=128)
                        o = 256 * j + 128 * k
                        loads.append((lambda t, o=o: t[:, :].rearrange("p (kc c) -> p kc c", kc=8)[:, :, o:o + 128], src))
                convert(l, "FI%d" % i, loads)
            for i in range(6):
                nch = 4 if i < 5 else 2
                src = w_fo_d[l, 512 * i:512 * i + 128 * nch, :].rearrange("(c p) n -> p c n", p=128)
                convert(l, "FO%d" % i, [(lambda t, nch=nch: t[:, 0:1024 * nch].rearrange("p (c n) -> p c n", c=nch), src)],
                        width=1024 * nch)

    def load_g(buf, bname, which, l):
        src = g_d[which][l].partition_broadcast(128)
        p.dma("gld_" + bname, [lambda e: e.dma_start(out=buf[:, :], in_=src)], reads=[], writes=[bname])

    def rstd_from(ss_ap, n, reads_w):
        act(ss_ap, ss_ap, AF.Sqrt, [reads_w], [reads_w], bias=eps_c, scale=1.0 / D)
        p.op("dve", lambda e: e.reciprocal(ss_ap, ss_ap), [reads_w], [reads_w])

    def norm_tok(src_fn, nst, gbuf, gname, dst_fn, dst_res, src_res_fn):
        for st in range(nst):
            xin = src_fn(st)
            act(junk[:, :], xin, AF.Square, [src_res_fn(st)], ["junk", "ss%d" % st], accum_out=stats[:, st:st + 1])
        for st in range(nst):
            rstd_from(stats[:, st:st + 1], 1, "ss%d" % st)
        for st in range(nst):
            xin = src_fn(st)
            p.op("dve", lambda e, xin=xin, st=st: e.scalar_tensor_tensor(htok[:, :], xin, stats[:, st:st + 1], gbuf[:, :], ALU.mult, ALU.mult),
                 [src_res_fn(st), "ss%d" % st, gname], ["htok"])
            for c in range(8):
                p.op("pe", lambda e, c=c: e.transpose(pst[:, 128 * c:128 * c + 128], htok[:, 128 * c:128 * c + 128], ident),
                     ["htok", "cb"], ["ps7"], signal=(c == 7))
            d_ = dst_fn(st)
            act(d_, pst[:, :].rearrange("p (c t) -> p c t", c=8), AF.Copy, ["ps7"], [dst_res(st)])

    def proj_fm(piece, pres, ncols_tot, c0, rhs_fn, rhs_res, n, banks):
        b = mmbank(banks)
        pv = piece[:, 0:8 * ncols_tot].rearrange("p (kc c) -> p kc c", kc=8)
        for kc in range(8):
            mm(bank(b, n), pv[:, kc, c0:c0 + 128], rhs_fn(kc), kc == 0, kc == 7, [pres] + rhs_res, ["ps%d" % b], kc == 7)
        return b

    def rotary_evac(b, dst, dst_res, n=512):
        tt(rqc[:, 0:n], bank(b, n), rope[:, 0, 0:n], ALU.mult, ["ps%d" % b, "rope"], ["rqc"])
        tt(rqs[:, 0:n], bank(b, n), rope[:, 1, 0:n], ALU.mult, ["ps%d" % b, "rope"], ["rqs"])

        def stage2():
            b2 = mmbank(MMB)
            mm(bank(b2, n), perm, rqs[:, 0:n], True, True, ["rqs", "cb"], ["ps%d" % b2], True)
            tt(dst, bank(b2, n), rqc[:, 0:n], ALU.add, ["ps%d" % b2, "rqc"], [dst_res])
        return stage2

    def rot_chunks(items):
        pend = None
        for pj, dst, dres in items:
            b = pj()
            if pend:
                pend()
            pend = rotary_evac(b, dst, dres)
        if pend:
            pend()

    ptrot = [0]
    odrot = [0]

    def od_pair():
        pr = ((4, 5), (6, 7))[odrot[0] % 2]
        odrot[0] += 1
        return pr

    def emit_stream(tiles, GB=2):
        nt = len(tiles)
        nb = (nt + GB - 1) // GB
        slots = {}
        for b in range(nb + 1):
            if b < nb:
                cur = list(range(GB * b, min(GB * b + GB, nt)))
                outs = {}
                for i in cur:
                    t = tiles[i]
                    sbk = mmbank(MMB)
                    pi = ptrot[0] % 4
                    ptrot[0] += 1
                    slots[i] = pi
                    so = bank(sbk, t["n"])
                    if t.get("h4"):
                        so = so.rearrange("p (h q) -> p h q", h=4)
                    outs[i] = (sbk, so)
                    if t.get("multi"):
                        mm(so.rearrange("p (c i) -> p c i", c=t["mc"]), ident, t["mask"], True, False, ["cb"], ["ps%d" % sbk], False)
                        ns = len(t["scores"])
                        for si, (k_ap, q_ap, c0, n_) in enumerate(t["scores"]):
                            mm(so[:, c0:c0 + n_], k_ap, q_ap, False, si == ns - 1, ["kv", "qT"], ["ps%d" % sbk], si == ns - 1)
                        continue
                    mm(so, t["k"], t["q"], True, t["mask"] is None, ["kv", "qT"], ["ps%d" % sbk], t["mask"] is None)
                for i in cur:
                    t = tiles[i]
                    if t.get("multi"):
                        continue
                    if t["mask"] is not None:
                        sbk, so = outs[i]
                        mm(so, ident, t["mask"], False, True, ["cb"], ["ps%d" % sbk], True)
                for i in cur:
                    t = tiles[i]
                    sbk, so = outs[i]
                    act(pT[slots[i]][:, 0:t["n"]], bank(sbk, t["n"]), AF.Exp, ["ps%d" % sbk], ["pT%d" % slots[i]], scale=t["scale"])
            if b >= 1:
                prev = list(range(GB * (b - 1), min(GB * b, nt)))
                for i in prev:
                    t = tiles[i]
                    pi = slots[i]
                    if t.get("multi"):
                        npv = len(t["pvs"])
                        for vi, (v_ap, c0, n_, ocols) in enumerate(t["pvs"]):
                            mm(psm[0:64, 512 * t["ob"]:512 * t["ob"] + 512][:, ocols], v_ap, pT[pi][:, c0:c0 + n_],
                               t["first"] and vi == 0, t["last"] and vi == npv - 1, ["pT%d" % pi, "kv"], ["ps%d" % t["ob"]], False)
                        continue
                    M = t["v"].shape[-1]
                    mm(psm[0:M, 512 * t["ob"]:512 * t["ob"] + 512][:, t["cols"]], t["v"], pT[pi][:, 0:t["n"]], t["first"], t["last"],
                       ["pT%d" % pi, "kv"], ["ps%d" % t["ob"]], False)
                for k, i in enumerate(prev):
                    t = tiles[i]
                    pi = slots[i]
                    if t.get("multi"):
                        mc = t["mc"]
                        mm(psm[0:64, 512 * t["db"]:512 * t["db"] + 512].rearrange("p (i c) -> p c i", c=mc), ones[:, 0:64],
                           pT[pi][:, 0:512].rearrange("p (c i) -> p c i", c=mc), t["first"], t["last"],
                           ["pT%d" % pi, "cb"], ["ps%d" % t["db"]], t["last"] or k == len(prev) - 1)
                        continue
                    M = t["v"].shape[-1]
                    mm(psm[0:M, 512 * t["db"]:512 * t["db"] + 512][:, t["cols"]], ones[:, 0:M], pT[pi][:, 0:t["n"]], t["first"], t["last"],
                       ["pT%d" % pi, "cb"], ["ps%d" % t["db"]], t["last"] or k == len(prev) - 1)
                for i in prev:
                    t = tiles[i]
                    if t["last"] and t.get("post"):
                        t["post"]()

    def band_tiles(tt_i, R, dil, L, mask, kT_fn, qT_fn, v_fn):
        t0 = 512 * tt_i
        out = []
        for c in range(dil):
            mlo, mhi = t0 // dil, (t0 + 512) // dil
            for kt in range(L // 128):
                lo = max(128 * kt - R, 0, mlo)
                hi = min(128 * kt + 128 + R, L, mhi)
                if hi <= lo:
                    continue
                n = hi - lo
                qi0 = lo - (128 * kt - R)
                loc = lo * dil + c - t0
                if dil == 1:
                    cols = slice(loc, loc + n)
                else:
                    cols = slice(loc, loc + (n - 1) * dil + 1, dil)
                out.append(dict(k=kT_fn(c, kt), q=qT_fn(cols), n=n, mask=mask[:, qi0:qi0 + n], scale=0.125, v=v_fn(c, kt), cols=cols))
        return out

    def layer(sq, l):
        p.phase = 'kv_norm'
        p.barrier()
        load_g(gA, "gA", "norm_mix_pre", l)
        load_g(gB, "gB", "norm_mem", l)
        src = sink_d[l].partition_broadcast(128)
        p.dma("gld_sink", [lambda e: e.dma_start(out=stats[:, 48:56], in_=src)], reads=[], writes=["sink"])
        act(stats[:, 48:56], stats[:, 48:56], AF.Exp, ["sink"], ["sink"])
        for t4 in range(4):
            norm_tok(lambda st: x_sb[:, 4 * t4 + st, :], 4, gA, "gA",
                     lambda st: hTf[:, :, 512 * t4 + 128 * st:512 * t4 + 128 * st + 128],
                     lambda st: "hTf", lambda st: "x%d" % (4 * t4 + st))
        p.phase = 'kv_K'
        for pn, nchunk, cbase in (("KV0", 4, 0), ("KV1", 3, 4)):
            piece, pres = load_piece(l, pn)
            for t4 in range(4):
                p.dma("ropeld", [lambda e, t4=t4: e.dma_start(out=rope[:, :, :], in_=rope_d[:, :, 512 * t4:512 * t4 + 512])],
                      reads=[], writes=["rope"])
                rot_chunks([((lambda ch=ch, t4=t4: proj_fm(piece, pres, 128 * nchunk, 128 * ch,
                                                          lambda kc: hTf[:, kc, 512 * t4:512 * t4 + 512], ["hTf"], 512, MMB)),
                             kT[:, cbase + ch, 512 * t4:512 * t4 + 512], "kv") for ch in range(nchunk)])
        p.phase = 'kv_V'
        piece, pres = load_piece(l, "V0")
        pv = piece[:, 0:8 * 384].rearrange("p (kc c) -> p kc c", kc=8)
        for g in range(16):
            b = mmbank(MMB)
            for kc in range(8):
                mm(bank(b, 384), hTf[:, kc, 128 * g:128 * g + 128], pv[:, kc, :], kc == 0, kc == 7, [pres, "hTf"], ["ps%d" % b], kc == 7)
            act(v0[:, g, :], bank(b, 384), AF.Copy, ["ps%d" % b], ["kv"])
        piece, pres = load_piece(l, "V1")
        pv = piece[:, :].rearrange("p (kc c) -> p kc c", kc=8)
        for c in range(4):
            for kt in range(4):
                b = mmbank(MMB)
                for kc in range(8):
                    mm(bank(b, 256), hTf[:, kc, 512 * kt + c:512 * kt + 512:4], pv[:, kc, 0:256], kc == 0, kc == 7, [pres, "hTf"], ["ps%d" % b], kc == 7)
                p.op("dve", lambda e, b=b, c=c, kt=kt: e.tensor_copy(vb1[:, 4 * c + kt, :], bank(b, 256)), ["ps%d" % b], ["kv"])
        for c in range(16):
            b = mmbank(MMB)
            for kc in range(8):
                mm(bank(b, 256), hTf[:, kc, c:S:16], pv[:, kc, 256:512], kc == 0, kc == 7, [pres, "hTf"], ["ps%d" % b], kc == 7)
            p.op("dve", lambda e, b=b, c=c: e.tensor_copy(vb2[:, c, :], bank(b, 256)), ["ps%d" % b], ["kv"])
        p.phase = 'kv_mem'
        for mt in range(2):
            p.dma("memld", [lambda e, mt=mt: e.dma_start(out=ytmp[:, :], in_=mem_d[sq, 128 * mt:128 * mt + 128, :])], reads=[], writes=["ytmp"])
            norm_tok(lambda st: ytmp[:, :], 1, gB, "gB", lambda st, mt=mt: memT[:, :, 128 * mt:128 * mt + 128], lambda st: "memT", lambda st: "ytmp")
        piece, pres = load_piece(l, "MK")
        for h in range(4):
            b = proj_fm(piece, pres, 512, 128 * h, lambda kc: memT[:, kc, :], ["memT"], 256, MMB)
            act(mkT[:, h, :], bank(b, 256), AF.Copy, ["ps%d" % b], ["kv"])
        piece, pres = load_piece(l, "MV")
        pv = piece[:, :].rearrange("p (kc c) -> p kc c", kc=8)
        for mt in range(2):
            b = mmbank(MMB)
            for kc in range(8):
                mm(bank(b), memT[:, kc, 128 * mt:128 * mt + 128], pv[:, kc, :], kc == 0, kc == 7, [pres, "memT"], ["ps%d" % b], kc == 7)
            act(mv[:, mt, :], bank(b), AF.Copy, ["ps%d" % b], ["kv"])
        p.barrier()
        load_g(gB, "gB", "norm_mix_post", l)

        def q_norm(t4):
            p.phase = 'q_norm'
            norm_tok(lambda st: x_sb[:, 4 * t4 + st, :], 4, gA, "gA",
                     lambda st: hT[:, :, 128 * st:128 * st + 128], lambda st: "hT", lambda st: "x%d" % (4 * t4 + st))
        q_norm(0)
        for t4 in range(4):
            p.dma("ropeld", [lambda e, t4=t4: e.dma_start(out=rope[:, :, :], in_=rope_d[:, :, 512 * t4:512 * t4 + 512])],
                  reads=[], writes=["rope"])
            hrhs = lambda kc: hT[:, kc, :]

            def outproj(branch, nh, kdim, wfn, first_b, last_b):
                for hf in range(2):
                    gp, gres = load_piece(l, "G%s%d" % (branch, hf))
                    wpc, wres, wv = wfn(hf)
                    for o4 in range(4):
                        oc = 4 * hf + o4
                        bg = proj_fm(gp, gres, 512, 128 * o4, hrhs, ["hT"], 512, MMB)
                        sgt = sg[oc % 2]
                        sgn = "sg%d" % (oc % 2)
                        act(sgt[:, :], bank(bg), AF.Sigmoid, ["ps%d" % bg], [sgn])
                        bo = mmbank(MMB)
                        for h in range(nh):
                            mm(bank(bo), wv(h, o4), OT[0:kdim, h, :], h == 0, h == nh - 1, [wres, "OT"], ["ps%d" % bo], h == nh - 1)
                        if first_b:
                            tt(merged[:, oc, :], bank(bo), sgt[:, :], ALU.mult, ["ps%d" % bo, sgn], ["merged"])
                        else:
                            tt(rec[:, :], bank(bo), sgt[:, :], ALU.mult, ["ps%d" % bo, sgn], ["rec"])
                            if last_b:
                                tt(mbf[:, oc, :], merged[:, oc, :], rec[:, :], ALU.add, ["merged", "rec"], ["qT"])
                            else:
                                tt(merged[:, oc, :], merged[:, oc, :], rec[:, :], ALU.add, ["merged", "rec"], ["merged"])

            p.phase = 'qA_proj'
            piece, pres = load_piece(l, "QA")
            rot_chunks([((lambda ch=ch: proj_fm(piece, pres, 512, 128 * ch, hrhs, ["hT"], 512, MMB)), qT[:, ch, :], "qT")
                        for ch in range(4)])
            p.phase = 'qA_attn'
            tiles = []
            for g2 in range(2):
                pb = 64 * g2
                for qbl in range(4):
                    qb = 4 * t4 + qbl
                    ob, db = od_pair()
                    kts = [kt for kt in (qb - 1, qb, qb + 1) if 0 <= kt < 16]

                    def post(g2=g2, qbl=qbl, ob=ob, db=db):
                        for jj in range(4):
                            h = 4 * g2 + jj
                            act(rec[0:64, 128 * jj:128 * jj + 128], bank(db)[0:64, 128 * jj:128 * jj + 128], AF.Ln,
                                ["ps%d" % db, "sink"], ["rec"], bias=stats[0:64, 48 + h:49 + h])
                        act(rec[0:64, :], rec[0:64, :], AF.Exp, ["rec"], ["rec"], scale=-1.0)
                        tt(OT[0:64, 4 * g2:4 * g2 + 4, 128 * qbl:128 * qbl + 128],
                           bank(ob)[0:64, :].rearrange("p (h q) -> p h q", h=4),
                           rec[0:64, :].rearrange("p (h q) -> p h q", h=4), ALU.mult, ["ps%d" % ob, "rec"], ["OT"])
                    for i, kt in enumerate(kts):
                        if kt == qb:
                            mk = None
                        else:
                            a = 256 if kt < qb else 0
                            mk = maskA[:, a:a + 128].unsqueeze(1).broadcast_to([128, 4, 128])
                        tiles.append(dict(k=kT[pb:pb + 64, 0, 128 * kt:128 * kt + 128],
                                          q=qT[pb:pb + 64, 0:4, 128 * qbl:128 * qbl + 128], n=512, mask=mk, scale=0.125,
                                          v=v0[:, kt, 64 * g2:64 * g2 + 64], cols=slice(0, 512), ob=ob, db=db,
                                          first=(i == 0), last=(i == len(kts) - 1), post=post, h4=True))
            emit_stream(tiles)

            def wfa(hf):
                wpc, wres = load_piece(l, "OA%d" % hf, parts=64)
                v = wpc[0:64, :].rearrange("p (h c) -> p h c", h=8)
                return wpc, wres, (lambda h, o4: v[:, h, 128 * o4:128 * o4 + 128])
            p.phase = 'qA_out'
            outproj("A", 8, 64, wfa, True, False)

            p.phase = 'qB_proj'
            for pn, nchunk, cbase in (("QB0", 4, 0), ("QB1", 2, 4)):
                piece, pres = load_piece(l, pn)
                rot_chunks([((lambda ch=ch, piece=piece, pres=pres, nchunk=nchunk: proj_fm(piece, pres, 128 * nchunk, 128 * ch, hrhs, ["hT"], 512, MMB)),
                             qT[:, cbase + ch, :], "qT") for ch in range(nchunk)])
            p.phase = 'qB_attn'
            stream = []
            for j in range(4):
                tiles = []
                for gi, (dil, L) in enumerate(((1, 2048), (4, 512), (16, 128))):
                    hb = 4 * gi + j
                    chq, pb = hb // 2, 64 * (hb % 2)
                    if gi == 0:
                        vfn = lambda c, kt, j=j: v0[:, kt, 128 + 64 * j:128 + 64 * j + 64]
                    elif gi == 1:
                        vfn = lambda c, kt, j=j: vb1[:, 4 * c + kt, 64 * j:64 * j + 64]
                    else:
                        vfn = lambda c, kt, j=j: vb2[:, c, 64 * j:64 * j + 64]

                    def kfn(c, kt, chq=chq, pb=pb, dil=dil):
                        s0 = 128 * kt * dil + c
                        if dil == 1:
                            return kT[pb:pb + 64, 1 + chq, s0:s0 + 128]
                        return kT[pb:pb + 64, 1 + chq, s0:s0 + 127 * dil + 1:dil]
                    if gi == 2:
                        qi0 = 32 * t4 + 64
                        tiles.append(dict(multi=True, mc=16, n=512, scale=0.125,
                                          mask=maskB[:, qi0:qi0 + 32].unsqueeze(1).broadcast_to([128, 16, 32]),
                                          scores=[(kfn(c, 0), qT[pb:pb + 64, chq, c:512:16], 32 * c, 32) for c in range(16)],
                                          pvs=[(vfn(c, 0), 32 * c, 32, slice(c, 512, 16)) for c in range(16)]))
                        continue
                    tiles += band_tiles(t4, 64, dil, L, maskB, kfn,
                                        lambda cols, chq=chq, pb=pb: qT[pb:pb + 64, chq, cols], vfn)
                ob, db = od_pair()

                def post(j=j, ob=ob, db=db):
                    act(rec[0:64, :], bank(db)[0:64, :], AF.Ln, ["ps%d" % db], ["rec"])
                    act(rec[0:64, :], rec[0:64, :], AF.Exp, ["rec"], ["rec"], scale=-1.0)
                    tt(OT[0:64, j, :], bank(ob)[0:64, :], rec[0:64, :], ALU.mult, ["ps%d" % ob, "rec"], ["OT"])
                for i, t in enumerate(tiles):
                    t.update(ob=ob, db=db, first=(i == 0), last=(i == len(tiles) - 1), post=post)
                stream += tiles
            emit_stream(stream)

            def wfb(hf):
                if hf == 0:
                    wfb.pc = load_piece(l, "OB", parts=64)
                wpc, wres = wfb.pc
                v = wpc[0:64, :].rearrange("p (h c) -> p h c", h=4)
                return wpc, wres, (lambda h, o4: v[:, h, 512 * hf + 128 * o4:512 * hf + 128 * o4 + 128])
            p.phase = 'qB_out'
            outproj("B", 4, 64, wfb, False, False)

            p.phase = 'qC_proj'
            piece, pres = load_piece(l, "QC")
            for ch in range(4):
                b = proj_fm(piece, pres, 512, 128 * ch, hrhs, ["hT"], 512, MMB)
                act(qT[:, ch, :], bank(b), AF.Copy, ["ps%d" % b], ["qT"])
            p.phase = 'qC_attn'
            stream = []
            for h in range(4):
                ob, db = od_pair()

                def post(h=h, ob=ob, db=db):
                    act(rec[:, :], bank(db), AF.Ln, ["ps%d" % db], ["rec"])
                    act(rec[:, :], rec[:, :], AF.Exp, ["rec"], ["rec"], scale=-1.0)
                    tt(OT[:, h, :], bank(ob), rec[:, :], ALU.mult, ["ps%d" % ob, "rec"], ["OT"])
                for mt in range(2):
                    stream.append(dict(k=mkT[:, h, 128 * mt:128 * mt + 128], q=qT[:, h, :], n=512, mask=None,
                                       scale=float(128 ** -0.5), v=mv[:, mt, 128 * h:128 * h + 128], cols=slice(0, 512),
                                       ob=ob, db=db, first=(mt == 0), last=(mt == 1), post=post))
            emit_stream(stream)

            def wfc(hf):
                if hf == 0:
                    wfc.pc = load_piece(l, "OC")
                wpc, wres = wfc.pc
                v = wpc[:, :].rearrange("p (h c) -> p h c", h=4)
                return wpc, wres, (lambda h, o4: v[:, h, 512 * hf + 128 * o4:512 * hf + 128 * o4 + 128])
            p.phase = 'qC_out'
            outproj("C", 4, 128, wfc, False, True)

            if t4 < 3:
                q_norm(t4 + 1)
            p.phase = 'q_wout'
            wo = [load_piece(l, "WO%d" % hf) for hf in range(2)]
            for st in range(4):
                g = 4 * t4 + st
                yb = (0, 2)[st % 2]
                for hf in range(2):
                    wv = wo[hf][0][:, :].rearrange("p (kc c) -> p kc c", kc=8)
                    for kc in range(8):
                        mm(bank(yb + hf), mbf[:, kc, 128 * st:128 * st + 128], wv[:, kc, :], kc == 0, kc == 7, [wo[hf][1], "qT"], ["ps%d" % (yb + hf)], kc == 7)
                post_norm(psm[:, 512 * yb:512 * yb + 1024], ["ps%d" % yb, "ps%d" % (yb + 1)], g)

    def post_norm(yps, yres, g):
        act(junk[:, :], yps, AF.Square, yres, ["junk", "ssp"], accum_out=stats[:, 32:33])
        rstd_from(stats[:, 32:33], 1, "ssp")
        p.op("dve", lambda e: e.scalar_tensor_tensor(ytmp[:, :], yps, stats[:, 32:33], gB[:, :], ALU.mult, ALU.mult),
             yres + ["ssp", "gB"], ["ytmp"])
        tt(x_sb[:, g, :], x_sb[:, g, :], ytmp[:, :], ALU.add, ["x%d" % g, "ytmp"], ["x%d" % g], eng="pool")

    def ffn(sq, l):
        p.barrier()
        load_g(gA, "gA", "norm_ffn_pre", l)
        load_g(gB, "gB", "norm_ffn_post", l)
        for i in range(6):
            nch = 4 if i < 5 else 2
            src = scr[l, pidx["FO%d" % i]]
            p.dma("wfold", [lambda e, i=i, nch=nch, src=src: e.dma_start(
                out=wfo[:, 4 * i:4 * i + nch, :], in_=src[:, 0:1024 * nch].rearrange("p (c n) -> p c n", c=nch))],
                reads=["scr"], writes=["wfo"])
        hbufs = ((hT, "hT"), (hTb, "hTb"))

        def f_norm(t4):
            p.phase = 'f_norm'
            hb, hn = hbufs[t4 % 2]
            norm_tok(lambda st: x_sb[:, 4 * t4 + st, :], 4, gA, "gA",
                     lambda st: hb[:, :, 128 * st:128 * st + 128], lambda st: hn, lambda st: "x%d" % (4 * t4 + st))
        f_norm(0)
        for t4 in range(4):
            hb, hn = hbufs[t4 % 2]
            hrhs = lambda kc: hb[:, kc, :]
            pr = 0
            p.phase = 'f_gu'
            for i in range(11):
                piece, pres = load_piece(l, "FI%d" % i)
                for j in range(2):
                    c = 2 * i + j
                    banks = ((0, 1), (2, 3), (4, 5))[pr % 3]
                    pr += 1
                    bg = proj_fm(piece, pres, 512, 256 * j, hrhs, [hn], 512, (banks[0],))
                    bu = proj_fm(piece, pres, 512, 256 * j + 128, hrhs, [hn], 512, (banks[1],))
                    sf = sgf[c % 2]
                    sfn = "sgf%d" % (c % 2)
                    act(sf[:, :], bank(bg), AF.Silu, ["ps%d" % bg], [sfn])
                    tt(actT[:, c, :], bank(bu), sf[:, :], ALU.mult, ["ps%d" % bu, sfn], ["actT"])
            if t4 < 3:
                f_norm(t4 + 1)
            p.phase = 'f_y2'
            for st in range(4):
                g = 4 * t4 + st
                pb_ = (0, 2, 4)[st % 3]
                for hf in range(2):
                    for c in range(22):
                        mm(bank(pb_ + hf), actT[:, c, 128 * st:128 * st + 128], wfo[:, c, 512 * hf:512 * hf + 512],
                           c == 0, c == 21, ["actT", "wfo"], ["ps%d" % (pb_ + hf)], c == 21)
                post_norm(psm[:, 512 * pb_:512 * pb_ + 1024], ["ps%d" % pb_, "ps%d" % (pb_ + 1)], g)

    p.dma("cst", [lambda e: e.dma_start(out=cb[:, :], in_=cb_d)], reads=[], writes=["cb"])
    p.op("dve", lambda e: e.memset(stats[:, :], 0.0), [], ["stats0"])
    p.op("dve", lambda e: e.memset(eps_c, EPS), ["stats0"], ["eps"])
    prologue()
    p.barrier()
    for sq in range(NS):
        p.epoch = sq
        p.dma("xld", [lambda e, sq=sq, q4=q4: e.dma_start(out=x_sb[:, 4 * q4:4 * q4 + 4, :],
                                                      in_=x_d[sq, 512 * q4:512 * q4 + 512, :].rearrange("(g p) d -> p g d", p=128)) for q4 in range(4)],
              reads=[], writes=["x%d" % g for g in range(16)])
        for l in range(depth):
            layer(sq, l)
            ffn(sq, l)
        p.dma("xst", [lambda e, sq=sq, q4=q4: e.dma_start(out=y_d[sq, 512 * q4:512 * q4 + 512, :].rearrange("(g p) d -> p g d", p=128),
                                                      in_=x_sb[:, 4 * q4:4 * q4 + 4, :]) for q4 in range(4)],
              reads=["x%d" % g for g in range(16)], writes=[])
    p.barrier()

    sems = {}
    for s in p.count:
        sems[s] = nc.alloc_semaphore(s)

    def replay(eng, e):
        for waits, fn, inc in p.q[eng]:
            for s, v in waits:
                e.wait_ge(sems[s], v)
            if fn is None:
                continue
            ins = fn(e)
            if inc is not None:
                ins.then_inc(sems[inc[0]], inc[1])

    with nc.Block() as block:
        @block.tensor
        def _(e):
            replay("pe", e)

        @block.scalar
        def _(e):
            replay("act", e)

        @block.vector
        def _(e):
            replay("dve", e)

        @block.gpsimd
        def _(e):
            replay("pool", e)

        @block.sync
        def _(e):
            replay("sp", e)
    return nc, p


_CACHE = {}


def _run(x_all, mem_all, weights, depth=DEPTH):
    NS = x_all.shape[0] // NCORES
    key = (NS, depth)
    if key not in _CACHE:
        _CACHE[key] = build(NS, depth)[0]
    nc = _CACHE[key]
    cbc, ropec = _consts()
    in_maps = []
    for c in range(NCORES):
        m = {"x": np.ascontiguousarray(x_all[NS * c:NS * (c + 1)]),
             "mem": np.ascontiguousarray(mem_all[NS * c:NS * (c + 1)]),
             "c_bf": cbc, "c_rope": ropec}
        m.update(weights)
        in_maps.append(m)
    res = run_bass_kernel_spmd(nc, in_maps, core_ids=list(range(NCORES)))
    return np.concatenate([np.asarray(r["y"]) for r in res.results], axis=0)


def kernel(x_prompt, x_sample, mem_prompt, mem_sample, norm_mix_pre, norm_mix_post, norm_mem,
           w_in, sink_a, w_mem_kv, w_o_a, w_o_b, w_o_c, w_out, norm_ffn_pre, norm_ffn_post,
           w_ffn_in, w_ffn_out):
    f = lambda a: np.ascontiguousarray(np.asarray(a, dtype=np.float32))
    x_all = np.concatenate([f(x_prompt), f(x_sample)], axis=0)
    mem_all = np.concatenate([f(mem_prompt), f(mem_sample)], axis=0)
    weights = dict(norm_mix_pre=f(norm_mix_pre), norm_mix_post=f(norm_mix_post), norm_mem=f(norm_mem),
                   w_in=f(w_in), sink_a=f(sink_a), w_mem_kv=f(w_mem_kv), w_o_a=f(w_o_a), w_o_b=f(w_o_b),
                   w_o_c=f(w_o_c), w_out=f(w_out), norm_ffn_pre=f(norm_ffn_pre), norm_ffn_post=f(norm_ffn_post),
                   w_ffn_in=f(w_ffn_in), w_ffn_out=f(w_ffn_out))
    y = _run(x_all, mem_all, weights)
    nb = x_prompt.shape[0]
    return (np.ascontiguousarray(y[:nb]), np.ascontiguousarray(y[nb:]))
```
